# Optimizing a Trainium2 kernel written in Bass

```python
import jax, jax.numpy as jnp
from jax import lax
import numpy as np

D_MODEL = 2048
BATCH = 4
SEQ = 2048
DEPTH = 1
DEC_BATCH = 8
DEC_SEQ = 16
PAST_LEN = 1024

CHUNK = 64
N_PREV_CHUNKS = 8
BAND_CHUNKS = N_PREV_CHUNKS + 1
ATT_REACH = N_PREV_CHUNKS * CHUNK
HEAD_DIM = 64
N_ATT_HEADS = 16
N_RWKV_HEADS = 16
D_ATT = N_ATT_HEADS * HEAD_DIM
D_RWKV = N_RWKV_HEADS * HEAD_DIM
D_MIX = D_ATT + D_RWKV
REL_CLIP = 128
N_REL = 2 * REL_CLIP + 1
RANK_W = 64
RANK_A = 64
RANK_G = 128
D_SHIFT = 3 * D_RWKV + RANK_W + RANK_A + RANK_G
D_IN = 3 * D_ATT + D_SHIFT
RWKV_SPLITS = (D_RWKV, 2 * D_RWKV, 3 * D_RWKV, 3 * D_RWKV + RANK_W, 3 * D_RWKV + RANK_W + RANK_A)
D_FF = 5632
CONV_W = 3
RMS_EPS = 1e-6
GN_EPS = 64e-5
ATT_SCALE = HEAD_DIM ** -0.5
NEG_INF = -1e30

kernel_name = 'chunk_band_attn_rwkv7_hybrid_step'


def rms_norm(x, g):
    xf = x.astype(jnp.float32)
    y = xf * lax.rsqrt(jnp.mean(xf * xf, axis=-1, keepdims=True) + RMS_EPS)
    return (y * g.astype(jnp.float32)).astype(x.dtype)


def rel_bias(table, n_q, n_k, q_offset):
    rel = q_offset + jnp.arange(n_q)[:, None] - jnp.arange(n_k)[None, :]
    idx = jnp.clip(rel, -REL_CLIP, REL_CLIP) + REL_CLIP
    return table.astype(jnp.float32)[:, idx]


def band_attend(qb, kb, vb, bias, mask=None):
    s = jnp.einsum('bnqhd,bnkhd->bnhqk', qb, kb).astype(jnp.float32) * ATT_SCALE + bias
    if mask is not None:
        s = jnp.where(mask, s, NEG_INF)
    p = jax.nn.softmax(s, axis=-1).astype(vb.dtype)
    return jnp.einsum('bnhqk,bnkhd->bnqhd', p, vb)


def chunk_band_attention(q, k, v, table):
    B, S, H, dh = q.shape
    NC = S // CHUNK
    pad = jnp.zeros((B, ATT_REACH, H, dh), k.dtype)
    kc = jnp.concatenate([pad, k], axis=1).reshape(B, NC + N_PREV_CHUNKS, CHUNK, H, dh)
    vc = jnp.concatenate([pad, v], axis=1).reshape(B, NC + N_PREV_CHUNKS, CHUNK, H, dh)
    band = jnp.arange(NC)[:, None] + jnp.arange(BAND_CHUNKS)[None, :]
    kb = kc[:, band].reshape(B, NC, BAND_CHUNKS * CHUNK, H, dh)
    vb = vc[:, band].reshape(B, NC, BAND_CHUNKS * CHUNK, H, dh)
    key_pos = (jnp.arange(NC)[:, None] - N_PREV_CHUNKS) * CHUNK + jnp.arange(BAND_CHUNKS * CHUNK)[None, :]
    mask = (key_pos >= 0)[None, :, None, None, :]
    bias = rel_bias(table, CHUNK, BAND_CHUNKS * CHUNK, ATT_REACH)
    out = band_attend(q.reshape(B, NC, CHUNK, H, dh), kb, vb, bias, mask)
    return out.reshape(B, S, H, dh)


def cached_chunk_attention(q, k, v, k_past, v_past, table):
    R = k_past.shape[1]
    T = q.shape[1]
    kb = jnp.concatenate([k_past.astype(k.dtype), k], axis=1)[:, None]
    vb = jnp.concatenate([v_past.astype(v.dtype), v], axis=1)[:, None]
    bias = rel_bias(table, T, R + T, R)
    return band_attend(q[:, None], kb, vb, bias)[:, 0]


def rwkv7_scan(S0, r, decay, k, v, kk, a):
    def step(S, inp):
        r_t, w_t, k_t, v_t, kk_t, a_t = inp
        sa = jnp.einsum('bhvk,bhk->bhv', S, -kk_t)
        S = (S * w_t[:, :, None, :] + sa[..., None] * (kk_t * a_t)[:, :, None, :]
             + v_t[..., None] * k_t[:, :, None, :])
        return S, jnp.einsum('bhvk,bhk->bhv', S, r_t)
    xs = tuple(jnp.swapaxes(t, 0, 1) for t in (r, decay, k, v, kk, a))
    S, ys = lax.scan(step, S0, xs)
    return jnp.swapaxes(ys, 0, 1), S


def rwkv7_time_mix(z, shift_prev, S0, p):
    B, T, _ = z.shape
    f32 = jnp.float32
    z_prev = jnp.concatenate([shift_prev.astype(z.dtype)[:, None], z[:, :-1]], axis=1)
    zs = z + (z_prev - z) * p['mu_shift']
    r, k, v, wd, ad, gd = jnp.split(zs, RWKV_SPLITS, axis=-1)
    w_log = -jax.nn.softplus(-(p['w0'] + jnp.tanh(wd) @ p['w2']).astype(f32)) - 0.5
    decay = jnp.exp(-jnp.exp(w_log))
    a = jax.nn.sigmoid((p['a0'] + ad @ p['a2']).astype(f32))
    g = jax.nn.sigmoid(gd) @ p['g2']
    hd = lambda t: t.astype(f32).reshape(B, T, N_RWKV_HEADS, HEAD_DIM)
    ph = lambda t: t.astype(f32).reshape(N_RWKV_HEADS, HEAD_DIM)
    r, k, v, decay, a = hd(r), hd(k), hd(v), hd(decay), hd(a)
    kk = k * ph(p['k_k'])
    kk = kk / jnp.maximum(jnp.sqrt(jnp.sum(kk * kk, axis=-1, keepdims=True)), 1e-12)
    k = k * (1.0 + (a - 1.0) * ph(p['k_a']))
    y, S = rwkv7_scan(S0, r, decay, k, v, kk, a)
    mu = jnp.mean(y, axis=-1, keepdims=True)
    var = jnp.mean(jnp.square(y - mu), axis=-1, keepdims=True)
    yn = ((y - mu) * lax.rsqrt(var + GN_EPS)).reshape(B, T, D_RWKV)
    yn = yn * p['ln_x_w'].astype(f32) + p['ln_x_b'].astype(f32)
    bonus = jnp.sum(r * k * p['r_k'].astype(f32), axis=-1, keepdims=True) * v
    out = (yn + bonus.reshape(B, T, D_RWKV)).astype(z.dtype) * g
    return out, S, z[:, -1]


def causal_dwconv(h, h_past, w, b):
    T = h.shape[1]
    hp = jnp.concatenate([h_past.astype(h.dtype), h], axis=1)
    out = b + hp[:, 0:T] * w[0]
    for j in range(1, CONV_W):
        out = out + hp[:, j:j + T] * w[j]
    return out, hp[:, -(CONV_W - 1):]


def layer(x, c, k_past, v_past, S0, shift_prev, conv_prev, p):
    B, T, _ = x.shape
    mod = jax.nn.silu(c) @ p['w_ada'] + p['b_ada']
    sh_a, sc_a, g_a, sh_f, sc_f, g_f = jnp.split(mod[:, None, :], 6, axis=-1)
    h = rms_norm(x, p['norm_att_g']) * (1 + sc_a) + sh_a
    z = h @ p['w_in']
    q = rms_norm(z[..., :D_ATT].reshape(B, T, N_ATT_HEADS, HEAD_DIM), p['q_norm_g'])
    k = rms_norm(z[..., D_ATT:2 * D_ATT].reshape(B, T, N_ATT_HEADS, HEAD_DIM), p['k_norm_g'])
    v = z[..., 2 * D_ATT:3 * D_ATT].reshape(B, T, N_ATT_HEADS, HEAD_DIM)
    if k_past is None:
        att = chunk_band_attention(q, k, v, p['rel_bias'])
    else:
        att = cached_chunk_attention(q, k, v, k_past, v_past, p['rel_bias'])
    rw, S, shift_last = rwkv7_time_mix(z[..., 3 * D_ATT:], shift_prev, S0, p)
    mix = jnp.concatenate([att.reshape(B, T, D_ATT), rw], axis=-1) @ p['w_out']
    x = x + g_a * mix
    h = rms_norm(x, p['norm_ffn_g']) * (1 + sc_f) + sh_f
    gate_pre, val = jnp.split(h @ p['w_up'], 2, axis=-1)
    gate_c, conv_last = causal_dwconv(gate_pre, conv_prev, p['dw_conv'], p['dw_bias'])
    x = x + g_f * ((jax.nn.gelu(gate_c, approximate=False) * val) @ p['w_down'])
    return x, k, v, S, shift_last, conv_last


def setup_inputs(seed: int = 0) -> dict:
    key = jax.random.key(seed)
    ks = iter(jax.random.split(key, 40))
    f32 = jnp.float32
    L = DEPTH
    R = min(ATT_REACH, PAST_LEN)

    def nrm(shape, scale=1.0):
        return scale * jax.random.normal(next(ks), shape, f32)

    return {
        'x_prompt': nrm((BATCH, SEQ, D_MODEL)),
        'x_sample': nrm((DEC_BATCH, DEC_SEQ, D_MODEL)),
        'c_prompt': nrm((BATCH, D_MODEL)),
        'c_sample': nrm((DEC_BATCH, D_MODEL)),
        'cache_att_k': nrm((L, DEC_BATCH, R, N_ATT_HEADS, HEAD_DIM)),
        'cache_att_v': nrm((L, DEC_BATCH, R, N_ATT_HEADS, HEAD_DIM)),
        'state_rwkv': nrm((L, DEC_BATCH, N_RWKV_HEADS, HEAD_DIM, HEAD_DIM)),
        'state_shift': nrm((L, DEC_BATCH, D_SHIFT)),
        'state_ffn_conv': nrm((L, DEC_BATCH, CONV_W - 1, D_FF)),
        'norm_att_g': 1.0 + nrm((L, D_MODEL), 0.1),
        'norm_ffn_g': 1.0 + nrm((L, D_MODEL), 0.1),
        'w_ada': nrm((L, D_MODEL, 6 * D_MODEL), 0.5 * D_MODEL ** -0.5),
        'b_ada': nrm((L, 6 * D_MODEL), 0.01),
        'w_in': nrm((L, D_MODEL, D_IN), D_MODEL ** -0.5),
        'q_norm_g': 1.0 + nrm((L, HEAD_DIM), 0.1),
        'k_norm_g': 1.0 + nrm((L, HEAD_DIM), 0.1),
        'rel_bias': nrm((L, N_ATT_HEADS, N_REL), 0.5),
        'mu_shift': jax.random.uniform(next(ks), (L, D_SHIFT), f32),
        'w0': jax.random.uniform(next(ks), (L, D_RWKV), f32, -5.0, 1.0),
        'w2': nrm((L, RANK_W, D_RWKV), 0.5 * RANK_W ** -0.5),
        'a0': nrm((L, D_RWKV), 0.5),
        'a2': nrm((L, RANK_A, D_RWKV), 0.5 * RANK_A ** -0.5),
        'g2': nrm((L, RANK_G, D_RWKV), RANK_G ** -0.5),
        'k_k': 1.0 + nrm((L, D_RWKV), 0.1),
        'k_a': 1.0 + nrm((L, D_RWKV), 0.1),
        'r_k': nrm((L, N_RWKV_HEADS, HEAD_DIM), 0.1),
        'ln_x_w': 1.0 + nrm((L, D_RWKV), 0.1),
        'ln_x_b': nrm((L, D_RWKV), 0.01),
        'w_out': nrm((L, D_MIX, D_MODEL), D_MIX ** -0.5),
        'w_up': nrm((L, D_MODEL, 2 * D_FF), D_MODEL ** -0.5),
        'dw_conv': nrm((L, CONV_W, D_FF), CONV_W ** -0.5),
        'dw_bias': nrm((L, D_FF), 0.01),
        'w_down': nrm((L, D_FF, D_MODEL), D_FF ** -0.5),
    }


def reference(x_prompt, x_sample, c_prompt, c_sample, cache_att_k, cache_att_v, state_rwkv,
              state_shift, state_ffn_conv, norm_att_g, norm_ffn_g, w_ada, b_ada, w_in,
              q_norm_g, k_norm_g, rel_bias, mu_shift, w0, w2, a0, a2, g2, k_k, k_a, r_k,
              ln_x_w, ln_x_b, w_out, w_up, dw_conv, dw_bias, w_down):
    hp, hs = x_prompt, x_sample
    Bp, Tp = hp.shape[0], hp.shape[1]
    keep = min(ATT_REACH, Tp)
    kp_l, vp_l, Sp_l, shp_l, cvp_l = [], [], [], [], []
    ks_l, vs_l, Ss_l, shs_l, cvs_l = [], [], [], [], []
    for l in range(DEPTH):
        p = dict(norm_att_g=norm_att_g[l], norm_ffn_g=norm_ffn_g[l], w_ada=w_ada[l], b_ada=b_ada[l],
                 w_in=w_in[l], q_norm_g=q_norm_g[l], k_norm_g=k_norm_g[l], rel_bias=rel_bias[l],
                 mu_shift=mu_shift[l], w0=w0[l], w2=w2[l], a0=a0[l], a2=a2[l], g2=g2[l],
                 k_k=k_k[l], k_a=k_a[l], r_k=r_k[l], ln_x_w=ln_x_w[l], ln_x_b=ln_x_b[l],
                 w_out=w_out[l], w_up=w_up[l], dw_conv=dw_conv[l], dw_bias=dw_bias[l], w_down=w_down[l])
        hp, kp, vp, Sp, shp, cvp = layer(
            hp, c_prompt, None, None,
            jnp.zeros((Bp, N_RWKV_HEADS, HEAD_DIM, HEAD_DIM), jnp.float32),
            jnp.zeros((Bp, D_SHIFT), hp.dtype),
            jnp.zeros((Bp, CONV_W - 1, D_FF), hp.dtype), p)
        kp_l.append(kp[:, Tp - keep:]); vp_l.append(vp[:, Tp - keep:])
        Sp_l.append(Sp.astype(hp.dtype)); shp_l.append(shp); cvp_l.append(cvp)
        hs, kn, vn, Sn, shn, cvn = layer(
            hs, c_sample, cache_att_k[l], cache_att_v[l],
            state_rwkv[l].astype(jnp.float32), state_shift[l], state_ffn_conv[l], p)
        ks_l.append(kn); vs_l.append(vn)
        Ss_l.append(Sn.astype(hs.dtype)); shs_l.append(shn); cvs_l.append(cvn)
    return (hp, hs,
            jnp.stack(kp_l), jnp.stack(vp_l), jnp.stack(Sp_l), jnp.stack(shp_l), jnp.stack(cvp_l),
            jnp.stack(ks_l), jnp.stack(vs_l), jnp.stack(Ss_l), jnp.stack(shs_l), jnp.stack(cvs_l))
```

```python
import types
import numpy as np
from contextlib import ExitStack
import concourse.bass as bass
import concourse.mybir as mybir
from concourse.bass_utils import run_bass_kernel_spmd

F32 = mybir.dt.float32
BF16 = mybir.dt.bfloat16
AF = mybir.ActivationFunctionType
ALU = mybir.AluOpType

D = 2048
TW = 2048
TS = 16
NT = TW + TS
DFF = 5632
NJ = DFF // 128
OWN0 = 1024
HALO0 = 960
KV0 = 448
RMS_EPS = 1e-6
GN_EPS = 64e-5
EPOCH = 16000
NDMA_SEM = 8

PC = {}
_o = 0
for _n, _w in [("nag", 16), ("nfg", 16), ("mu", 26), ("omu", 26), ("w0", 8), ("a0", 8), ("kk", 8), ("ka", 8), ("omka", 8),
               ("rk", 8), ("lnw", 8), ("lnb", 8), ("qg", 1), ("kg", 1), ("dw0", 44), ("dw1", 44), ("dw2", 44),
               ("dwb", 44), ("flag", 1), ("negm", 1), ("shs", 26), ("cvs", 88)]:
    PC[_n] = _o
    _o += _w
NP = _o


class Prog:
    ENG = ["pe", "act", "dve", "pool", "sp"]

    def __init__(self):
        self.ops = {e: [] for e in self.ENG}
        self.cnt = {e: 0 for e in self.ENG}
        self.ndma = {e: 0 for e in self.ENG}
        self.waited = {e: {} for e in self.ENG}
        self.lastw = {}
        self.readers = {}
        self.same_sync = {"pe": False, "act": True, "dve": True, "pool": True, "sp": True}
        self.out_events = []
        self.last_ps = {}

    def _wait(self, eng, ev, waits):
        sk, val = ev
        if sk[0] == "eng":
            e2, ep = sk[1], sk[2]
            if e2 == eng and not self.same_sync[eng]:
                return
            cur = self.waited[eng].get(("engmax", e2), (-1, 0))
            if (ep, val) <= cur:
                return
            self.waited[eng][("engmax", e2)] = (ep, val)
            waits.append((sk, val))
        else:
            if self.waited[eng].get(sk, 0) >= val:
                return
            self.waited[eng][sk] = val
            waits.append((sk, val))

    @staticmethod
    def _freeze(fn):
        if fn.__closure__ is None:
            return fn
        cells = []
        for c in fn.__closure__:
            try:
                cells.append(types.CellType(c.cell_contents))
            except ValueError:
                cells.append(c)
        g = types.FunctionType(fn.__code__, fn.__globals__, fn.__name__, fn.__defaults__, tuple(cells))
        g.__kwdefaults__ = fn.__kwdefaults__
        return g

    dead = False

    def op(self, eng, fn, r=(), w=(), dma=False, out=False):
        if self.dead:
            return None
        fn = self._freeze(fn)
        deps = []
        for b in r:
            if b in self.lastw:
                deps.append(self.lastw[b])
        for b in w:
            if b in self.lastw:
                deps.append(self.lastw[b])
            deps.extend(self.readers.get(b, ()))
        is_ps = eng in ("act", "dve") and any(k.startswith("pb") for k in list(r) + list(w))
        if is_ps:
            other = "dve" if eng == "act" else "act"
            if other in self.last_ps:
                deps.append(self.last_ps[other])
        best = {}
        for (sk, val) in deps:
            if sk[0] == "eng":
                key = ("eng", sk[1])
                cand = (sk[2], val)
                if key not in best or cand > best[key]:
                    best[key] = cand
            else:
                if sk not in best or val > best[sk]:
                    best[sk] = val
        waits = []
        for key, v in best.items():
            if key[0] == "eng":
                self._wait(eng, (("eng", key[1], v[0]), v[1]), waits)
            else:
                self._wait(eng, (key, v), waits)
        if dma:
            n = self.ndma[eng]
            self.ndma[eng] += 1
            i = n % NDMA_SEM
            sk = ("dma", eng, i)
            val = 16 * (n // NDMA_SEM + 1)
            if n >= NDMA_SEM:
                self._wait(eng, (sk, val - 16), waits)
            ev = (sk, val)
        else:
            c = self.cnt[eng]
            self.cnt[eng] += 1
            ev = (("eng", eng, c // EPOCH), c % EPOCH + 1)
        self.ops[eng].append((waits, fn, ev, dma))
        if is_ps:
            self.last_ps[eng] = ev
        for b in w:
            self.lastw[b] = ev
            self.readers[b] = []
        for b in r:
            if b not in w:
                self.readers.setdefault(b, []).append(ev)
        if out:
            self.out_events.append(ev)
        return ev

    def barrier(self):
        if self.dead:
            return
        evs = []
        for e in self.ENG:
            if self.cnt[e] > 0:
                c = self.cnt[e] - 1
                evs.append((("eng", e, c // EPOCH), c % EPOCH + 1))
            n = self.ndma[e]
            for i in range(NDMA_SEM):
                if n > i:
                    last = ((n - 1 - i) // NDMA_SEM) * NDMA_SEM + i
                    evs.append((("dma", e, i), 16 * (last // NDMA_SEM + 1)))
        for e in self.ENG:
            waits = []
            for ev in evs:
                self._wait(e, ev, waits)
            if waits:
                self.ops[e].append((waits, None, None, False))
        self.lastw = {}
        self.readers = {}

    def finish(self):
        waits = []
        for ev in self.out_events:
            self._wait("sp", ev, waits)
        self.ops["sp"].append((waits, None, None, False))

    def sem_keys(self):
        ks = set()
        for e in self.ENG:
            for waits, fn, ev, dma in self.ops[e]:
                if ev is not None:
                    ks.add(ev[0])
        return sorted(ks)

    def replay(self, nc, sems):
        engmap = {"pe": "tensor", "act": "scalar", "dve": "vector", "pool": "gpsimd", "sp": "sync"}
        with nc.Block() as block:
            for ename in self.ENG:
                ops = self.ops[ename]

                def body(e, ops=ops):
                    for waits, fn, ev, dma in ops:
                        for sk, val in waits:
                            e.wait_ge(sems[sk], val)
                        if fn is not None:
                            ins = fn(e)
                            ins.then_inc(sems[ev[0]], 16 if dma else 1)
                getattr(block, engmap[ename])(body)


class StopBuild(Exception):
    pass


def build_program(stop=None):
    nc = bass.Bass("TRN2", target_bir_lowering=False)
    P = Prog()
    dt = nc.dram_tensor
    P.stop = stop
    P.dbg_names = []

    def checkpoint(name, dumps):
        if stop != name:
            return
        P.barrier()
        for (label, ap, ncols, dtype) in dumps:
            d = dt("dbg_" + label, [ap.shape[0], ncols], dtype, kind="ExternalOutput").ap()
            P.dbg_names.append("dbg_" + label)
            step = 4096
            for c0 in range(0, ncols, step):
                c1 = min(ncols, c0 + step)
                P.op("sp", lambda e, d=d, ap=ap, c0=c0, c1=c1: e.dma_start(out=d[:, c0:c1], in_=ap[:, c0:c1]), dma=True, out=True)
        P.dead = True
    P.checkpoint = checkpoint

    def din(name, shape, dtype=F32):
        return dt(name, list(shape), dtype, kind="ExternalInput").ap()

    def dout(name, shape):
        return dt(name, list(shape), F32, kind="ExternalOutput").ap()

    xw = din("xw", [NT, D])
    cT = din("cT", [128, 32])
    params = din("params", [128, NP])
    badat = din("badat", [128, 96])
    w_ada_t = din("w_ada_t", [96, 128, 2048])
    w_in_t = din("w_in_t", [50, 128, 2048])
    w_out_t = din("w_out_t", [16, 128, 2048])
    w_up_t = din("w_up_t", [88, 128, 2048])
    w_down_t = din("w_down_t", [16, 128, NJ * 128])
    w2a2 = din("w2a2", [128, 1024])
    g2d = din("g2", [128, 1024])
    biasT = din("biasT", [8, 128, 576])
    biasS = din("biasS", [16, 128, 80])
    consts = din("consts", [128, 128 * 2 + 3 * 512])
    cache_k = din("cache_k", [512, 1024])
    cache_v = din("cache_v", [512, 1024])
    state0 = din("state0", [64, 1024])

    y_own = dout("y_own", [1024, D])
    y_s = dout("y_s", [TS, D])
    k_out = dout("k_out", [512, 1024])
    v_out = dout("v_out", [512, 1024])
    ks_out = dout("ks_out", [TS, 1024])
    vs_out = dout("vs_out", [TS, 1024])
    S_out = dout("S_out", [64, 1024])
    Ss_out = dout("Ss_out", [64, 1024])
    sh_out = dout("sh_out", [128, 52])
    cv_out = dout("cv_out", [128, 176])

    es = ExitStack()
    with es:
        def sb(name, shape, dtype=F32):
            return es.enter_context(nc.sbuf_tensor(name, list(shape), dtype))

        pb = [es.enter_context(nc.psum_tensor(f"pb{i}", [128, 512], F32)) for i in range(8)]

        try:
            prm = sb("prm", [128, NP])
            cst = sb("cst", [128, 256])
            mskb = sb("mskb", [128, 3 * 512], BF16)
            identf = cst[:, 0:128]
            bdones = cst[:, 128:256]
            m_su = mskb[:, 0:512]
            m_ui = mskb[:, 512:1024]
            m_sl = mskb[:, 1024:1536]
            identb = sb("identb", [128, 128], BF16)
            onesb = sb("onesb", [128, 128], BF16)
            modT = sb("modT", [128, 192])
            A1 = sb("A1", [128, 32])
            F1 = sb("F1", [128, 32])
            shcol = sb("shcol", [128, 52])
            cvcol = sb("cvcol", [128, 176])

            def pcol(name, i=0):
                c = PC[name] + i
                return prm[:, c:c + 1]

            def mod(g, kc, v):
                c = (g * 16 + kc) * 2 + v
                return modT[:, c:c + 1]

            P.op("sp", lambda e: e.dma_start(out=prm[:], in_=params[:, :]), w=["prm"], dma=True)
            P.op("sp", lambda e: e.dma_start(out=cst[:], in_=consts[:, 0:256]), w=["cst"], dma=True)
            P.op("pool", lambda e: e.dma_start(out=mskb[:], in_=consts[:, 256:1792]), w=["cst"], dma=True)
            P.op("dve", lambda e: e.tensor_copy(out=identb[:], in_=identf), r=["cst"], w=["identb"])
            P.op("dve", lambda e: e.memset(onesb[:], 1.0), w=["onesb"])

            with ExitStack() as ph:
                def sbp(name, shape, dtype=F32):
                    return ph.enter_context(nc.sbuf_tensor(name, list(shape), dtype))
                cTs = sbp("cTs", [128, 32])
                cs = sbp("cs", [128, 32], BF16)
                bad = sbp("bad", [128, 96])
                wab = [sbp(f"wab{i}", [128, 2048], BF16) for i in range(4)]
                P.op("sp", lambda e: e.dma_start(out=cTs[:], in_=cT[:, :]), w=["cTs"], dma=True)
                P.op("sp", lambda e: e.dma_start(out=bad[:], in_=badat[:, :]), w=["bad"], dma=True)
                P.op("act", lambda e: e.activation(out=cs[:], in_=cTs[:], func=AF.Silu), r=["cTs"], w=["cs"])
                import os
                for j in range(96 if not os.environ.get("KSKIPA") else 0):
                    wb = wab[j % 4]
                    P.op("pool", lambda e, wb=wb, j=j: e.dma_start(out=wb[:], in_=w_ada_t[j]), w=[f"wab{j%4}"], dma=True)
                    for kc in range(16):
                        P.op("pe", lambda e, wb=wb, j=j, kc=kc: e.matmul(pb[0][:, 2 * j:2 * j + 2], lhsT=wb[:, kc * 128:(kc + 1) * 128],
                                                                     rhs=cs[:, 2 * kc:2 * kc + 2], start=(kc == 0), stop=(kc == 15)),
                             r=[f"wab{j%4}", "cs"], w=["pb0"])
                    if j % 16 == 15:
                        g = j // 16
                        P.op("dve", lambda e, g=g: e.tensor_tensor(
                            out=modT[:, g * 32:(g + 1) * 32].rearrange("p (j v) -> p j v", v=2),
                            in0=pb[0][:, g * 32:(g + 1) * 32].rearrange("p (j v) -> p j v", v=2),
                            in1=bad[:, g * 16:(g + 1) * 16].unsqueeze(2).to_broadcast([128, 16, 2]), op=ALU.add),
                            r=["pb0", "bad"], w=[f"modT{g}"])
                        if g in (1, 4):
                            dst = A1 if g == 1 else F1
                            gn = "nag" if g == 1 else "nfg"
                            P.op("dve", lambda e, g=g, dst=dst: e.tensor_scalar(out=dst[:], in0=modT[:, g * 32:(g + 1) * 32], scalar1=1.0,
                                                                              scalar2=None, op0=ALU.add),
                                 r=[f"modT{g}"], w=["A1" if g == 1 else "F1"])
                            P.op("dve", lambda e, dst=dst, gn=gn: e.tensor_tensor(
                                out=dst[:].rearrange("p (j v) -> p j v", v=2), in0=dst[:].rearrange("p (j v) -> p j v", v=2),
                                in1=prm[:, PC[gn]:PC[gn] + 16].unsqueeze(2).to_broadcast([128, 16, 2]), op=ALU.mult),
                                r=["prm"], w=["A1" if g == 1 else "F1"])
                P.barrier()
                P.checkpoint("A", [("modT", modT[:], 192, F32), ("A1", A1[:], 32, F32), ("F1", F1[:], 32, F32)])

            with ExitStack() as ph:
                def sbp(name, shape, dtype=F32):
                    return ph.enter_context(nc.sbuf_tensor(name, list(shape), dtype))
                NMX = 1088 + TS
                Rr = sbp("Rr", [128, 16 * 1040])
                mixT = sbp("mixT", [128, 16 * NMX], BF16)

                def mixcol(tok):
                    return tok - HALO0

                with ExitStack() as ph2:
                    def sb2(name, shape, dtype=F32):
                        return ph2.enter_context(nc.sbuf_tensor(name, list(shape), dtype))
                    h1T = Rr[:].bitcast(BF16)

                    with ExitStack() as ph3:
                        def sb3(name, shape, dtype=F32):
                            return ph3.enter_context(nc.sbuf_tensor(name, list(shape), dtype))
                        xblk = [sb3(f"xblk{i}", [128, D]) for i in range(2)]
                        junk = sb3("junk", [128, D], BF16)
                        ssb = sb3("ssb", [128, 17])
                        rsb = sb3("rsb", [128, 17])
                        for tb in range(17):
                            n = 128 if tb < 16 else TS
                            v = 0 if tb < 16 else 1
                            xb = xblk[tb % 2]
                            xk = f"xblk{tb%2}"
                            P.op("sp", lambda e, xb=xb, tb=tb, n=n: e.dma_start(out=xb[:n, :], in_=xw[tb * 128:tb * 128 + n, :]), w=[xk], dma=True)
                            P.op("act", lambda e, xb=xb, tb=tb, n=n: e.activation(out=junk[:n, :], in_=xb[:n, :], func=AF.Square,
                                                                               accum_out=ssb[:n, tb:tb + 1]), r=[xk], w=["junk", f"ss{tb}"])
                            P.op("act", lambda e, tb=tb, n=n: e.activation(out=rsb[:n, tb:tb + 1], in_=ssb[:n, tb:tb + 1], func=AF.Sqrt,
                                                                        scale=1.0 / D, bias=RMS_EPS), r=[f"ss{tb}"], w=[f"rs{tb}"])
                            P.op("dve", lambda e, tb=tb, n=n: e.reciprocal(out=rsb[:n, tb:tb + 1], in_=rsb[:n, tb:tb + 1]), r=[f"rs{tb}"], w=[f"rs{tb}"])
                            P.op("dve", lambda e, xb=xb, tb=tb, n=n: e.tensor_scalar(out=xb[:n, :], in0=xb[:n, :], scalar1=rsb[:n, tb:tb + 1],
                                                                                  scalar2=None, op0=ALU.mult), r=[xk, f"rs{tb}"], w=[xk])
                            if tb == 0:
                                P.checkpoint("B0", [("xb", xblk[0][:], D, F32), ("rsb", rsb[:], 17, F32), ("ssb", ssb[:], 17, F32)])
                            if tb == 1:
                                P.checkpoint("B1", [("h1T", h1T[:, 0:16 * NT], 16 * NT, BF16)])
                            for q4 in range(4):
                                gq = tb * 4 + q4
                                for i in range(4):
                                    kc = q4 * 4 + i
                                    bi = (1 + gq % 2) if i % 2 == 0 else (3 + gq % 2)
                                    co = (i // 2) * 128
                                    P.op("pe", lambda e, xb=xb, n=n, kc=kc, co=co, bi=bi: e.matmul(
                                        pb[bi][:, co:co + n], lhsT=xb[:n, kc * 128:(kc + 1) * 128], rhs=identf[:n, :n], start=True, stop=True),
                                        r=[xk, "cst"], w=[f"pb{bi}"])
                                for i in range(4):
                                    kc = q4 * 4 + i
                                    bi = (1 + gq % 2) if i % 2 == 0 else (3 + gq % 2)
                                    co = (i // 2) * 128
                                    dstap = h1T[:, kc * NT + tb * 128:kc * NT + tb * 128 + n]
                                    if i % 2 == 0:
                                        P.op("dve", lambda e, dstap=dstap, bi=bi, co=co, n=n, kc=kc, v=v: e.tensor_scalar(
                                            out=dstap, in0=pb[bi][:, co:co + n], scalar1=A1[:, kc * 2 + v:kc * 2 + v + 1], scalar2=mod(0, kc, v),
                                            op0=ALU.mult, op1=ALU.add), r=[f"pb{bi}", "A1", "modT0"], w=[f"h1T{kc}"])
                                    else:
                                        P.op("act", lambda e, dstap=dstap, bi=bi, co=co, n=n, kc=kc, v=v: e.activation(
                                            out=dstap, in_=pb[bi][:, co:co + n], func=AF.Identity, scale=A1[:, kc * 2 + v:kc * 2 + v + 1],
                                            bias=mod(0, kc, v)), r=[f"pb{bi}", "A1", "modT0"], w=[f"h1T{kc}"])
                                if tb == 0 and q4 == 0:
                                    P.checkpoint("B0a", [("xb", xblk[0][:], D, F32)])
                        P.barrier()
                        P.checkpoint("B", [("h1T", h1T[:, 0:16 * NT], 16 * NT, BF16)])

                    H1R = [f"h1T{kc}" for kc in range(16)]

                    def dense(wt, wkey, tiles, evac, banks=(0, 1), rhsT=None, rkeys=None, RN=NT, nk=16):
                        rt = h1T if rhsT is None else rhsT
                        rk = H1R if rkeys is None else rkeys
                        for ti, (c0, n) in enumerate(tiles):
                            bi = banks[dense.ctr % len(banks)]
                            dense.ctr += 1
                            for kc in range(nk):
                                P.op("pe", lambda e, kc=kc, c0=c0, n=n, bi=bi: e.matmul(
                                    pb[bi][:, 0:n], lhsT=wt[:, kc * 128:(kc + 1) * 128], rhs=rt[:, kc * RN + c0:kc * RN + c0 + n],
                                    start=(kc == 0), stop=(kc == nk - 1)), r=[wkey] + rk, w=[f"pb{bi}"])
                            evac(pb[bi], f"pb{bi}", c0, n)
                    dense.ctr = 0

                    with ExitStack() as ph3:
                        def sb3(name, shape, dtype=F32):
                            return ph3.enter_context(nc.sbuf_tensor(name, list(shape), dtype))
                        NQ = 1088 + TS
                        NK = 1600 + TS
                        wq = [sb3(f"wq{i}", [128, 2048], BF16) for i in range(2)]
                        wk = [sb3(f"wk{i}", [128, 2048], BF16) for i in range(2)]
                        wv = [sb3(f"wv{i}", [128, 2048], BF16) for i in range(2)]
                        zq = sb3("zq", [128, NQ])
                        zk = sb3("zk", [128, NK])
                        zv = sb3("zv", [128, NK])
                        sq = sb3("sq", [128, 512])
                        rs = sb3("rs", [128, 512])
                        sq_b = sb3("sq_b", [128, 512])
                        rs_b = sb3("rs_b", [128, 512])
                        qT = sb3("qT", [128, NQ], BF16)
                        kT = sb3("kT", [128, NK], BF16)
                        kTs = sb3("kTs", [128, 512], BF16)
                        vtok = sb3("vtok", [128, 25 * 64], BF16)
                        vstok = sb3("vstok", [128, 128], BF16)
                        ckf = [sb3(f"ckf{i}", [128, 512]) for i in range(2)]
                        cvb = [sb3(f"cvb{i}", [128, 512], BF16) for i in range(2)]
                        bT = [sb3(f"bT{i}", [128, 576]) for i in range(2)]
                        bS = [sb3(f"bS{i}", [128, 160]) for i in range(2)]
                        sbuf_s = [sb3(f"sbs{i}", [128, 576]) for i in range(2)]
                        pT = [sb3(f"pT{i}", [128, 576], BF16) for i in range(2)]
                        rden = sb3("rden", [128, 128])
                        bT2 = sb3("bT2", [128, 1280])
                        sb2 = sb3("sb2", [128, 1280])
                        pT2 = [sb3(f"pT2{i}", [128, 1280], BF16) for i in range(2)]
                        otok = sb3("otok", [128, 512])
                        otok2 = sb3("otok2", [128, 512])
                        kvs = sb3("kvs", [16, 256])


                        qtiles = [(960, 512), (1472, 512), (1984, 64), (2048, TS)]
                        ktiles = [(448, 512), (960, 512), (1472, 512), (1984, 64), (2048, TS)]

                        def qcol(c0):
                            return c0 - 960 if c0 < 2048 else 1088

                        def kcol(c0):
                            return c0 - 448 if c0 < 2048 else 1600

                        for p in range(8):
                            pi = p % 2
                            for (wt, nm, ch) in ((wq[pi], "wq", p), (wk[pi], "wk", 8 + p), (wv[pi], "wv", 16 + p)):
                                P.op("pool", lambda e, wt=wt, ch=ch: e.dma_start(out=wt[:], in_=w_in_t[ch]), w=[f"{nm}{pi}"], dma=True)
                            for blk in range(4):
                                P.op("sp", lambda e, blk=blk, p=p, pi=pi: e.dma_start(out=ckf[pi][:, blk * 128:(blk + 1) * 128], in_=cache_k[blk * 128:(blk + 1) * 128, p * 128:(p + 1) * 128]),
                                     w=[f"ckf{pi}"], dma=True)
                                P.op("pool", lambda e, blk=blk, p=p, pi=pi: e.dma_start(out=cvb[pi][:, blk * 128:(blk + 1) * 128], in_=cache_v[blk * 128:(blk + 1) * 128, p * 128:(p + 1) * 128]),
                                     w=[f"cvb{pi}"], dma=True)
                            P.op("sp", lambda e, p=p, pi=pi: e.dma_start(out=bT[pi][:], in_=biasT[p]), w=[f"bT{pi}"], dma=True)
                            P.op("sp", lambda e, p=p, pi=pi: e.dma_start(out=bS[pi][:, 0:80], in_=biasS[2 * p]), w=[f"bS{pi}"], dma=True)
                            P.op("sp", lambda e, p=p, pi=pi: e.dma_start(out=bS[pi][:, 80:160], in_=biasS[2 * p + 1]), w=[f"bS{pi}"], dma=True)

                            def ev_q(bank, bk, c0, n):
                                P.op("act", lambda e: e.activation(out=zq[:, qcol(c0):qcol(c0) + n], in_=bank[:, 0:n], func=AF.Copy), r=[bk], w=["zq"])
                            def ev_k(bank, bk, c0, n):
                                P.op("act", lambda e: e.activation(out=zk[:, kcol(c0):kcol(c0) + n], in_=bank[:, 0:n], func=AF.Copy), r=[bk], w=["zk"])
                            def ev_v(bank, bk, c0, n):
                                P.op("dve", lambda e: e.tensor_copy(out=zv[:, kcol(c0):kcol(c0) + n], in_=bank[:, 0:n]), r=[bk], w=["zv"])
                            dense(wq[pi], f"wq{pi}", qtiles, ev_q)
                            dense(wk[pi], f"wk{pi}", ktiles, ev_k)
                            dense(wv[pi], f"wv{pi}", ktiles, ev_v)

                            qk_tiles = []
                            for (z, zkey, ncols, gname, outb, okey, outf) in ((zq, "zq", NQ, "qg", qT, "qT", False), (zk, "zk", NK, "kg", kT, "kT", True)):
                                c0 = 0
                                while c0 < ncols:
                                    n = min(512, ncols - c0)
                                    qk_tiles.append((z, zkey, c0, n, gname, outb, okey, outf))
                                    c0 += n
                            sqs = [sq, sq_b]
                            rss = [rs, rs_b]

                            def qk_stageA(i):
                                z, zkey, c0, n, gname, outb, okey, outf = qk_tiles[i]
                                b = i % 2
                                P.op("act", lambda e: e.activation(out=sqs[b][:, 0:n], in_=z[:, c0:c0 + n], func=AF.Square), r=[zkey], w=[f"sq{b}"])
                                P.op("pe", lambda e: e.matmul(pb[6 + b][:, 0:n], lhsT=bdones, rhs=sqs[b][:, 0:n], start=True, stop=True), r=[f"sq{b}", "cst"], w=[f"pb{6+b}"])

                            def qk_stageB(i):
                                z, zkey, c0, n, gname, outb, okey, outf = qk_tiles[i]
                                b = i % 2
                                P.op("act", lambda e: e.activation(out=rss[b][:, 0:n], in_=pb[6 + b][:, 0:n], func=AF.Ln, scale=1.0 / 64, bias=RMS_EPS), r=[f"pb{6+b}"], w=[f"rs{b}"])
                                P.op("act", lambda e: e.activation(out=rss[b][:, 0:n], in_=rss[b][:, 0:n], func=AF.Exp, scale=-0.5), r=[f"rs{b}"], w=[f"rs{b}"])
                                if outf:
                                    P.op("dve", lambda e: e.scalar_tensor_tensor(out=z[:, c0:c0 + n], in0=z[:, c0:c0 + n], scalar=pcol(gname), in1=rss[b][:, 0:n], op0=ALU.mult, op1=ALU.mult),
                                         r=[zkey, f"rs{b}", "prm"], w=[zkey])
                                    P.op("act", lambda e: e.activation(out=outb[:, c0:c0 + n], in_=z[:, c0:c0 + n], func=AF.Copy), r=[zkey], w=[okey])
                                else:
                                    P.op("dve", lambda e: e.scalar_tensor_tensor(out=outb[:, c0:c0 + n], in0=z[:, c0:c0 + n], scalar=pcol(gname), in1=rss[b][:, 0:n], op0=ALU.mult, op1=ALU.mult),
                                         r=[zkey, f"rs{b}", "prm"], w=[okey])
                            qk_stageA(0)
                            for i in range(len(qk_tiles)):
                                if i + 1 < len(qk_tiles):
                                    qk_stageA(i + 1)
                                qk_stageB(i)

                            for (z, zkey, od, ods, stg, skey, so) in ((zk, "zk", k_out, ks_out, otok, "otok", 0), (zv, "zv", v_out, vs_out, otok2, "otok2", 128)):
                                for blk in range(4):
                                    c0 = kcol(1536 + blk * 128)
                                    P.op("pe", lambda e, z=z, c0=c0, blk=blk: e.matmul(pb[7][:, blk * 128:(blk + 1) * 128], lhsT=z[:, c0:c0 + 128], rhs=identf,
                                                                                   start=True, stop=True), r=[zkey, "cst"], w=["pb7"])
                                P.op("dve", lambda e, stg=stg: e.tensor_copy(out=stg[:, 0:512], in_=pb[7][:, 0:512]), r=["pb7"], w=[skey])
                                for blk in range(4):
                                    P.op("sp", lambda e, od=od, stg=stg, blk=blk, p=p: e.dma_start(out=od[blk * 128:(blk + 1) * 128, p * 128:(p + 1) * 128],
                                                                                           in_=stg[:, blk * 128:(blk + 1) * 128]), r=[skey], dma=True, out=True)
                                P.op("pe", lambda e, z=z: e.matmul(pb[7][0:TS, 0:128], lhsT=z[:, 1600:1600 + TS], rhs=identf, start=True, stop=True),
                                     r=[zkey, "cst"], w=["pb7"])
                                P.op("dve", lambda e, so=so: e.tensor_copy(out=kvs[:, so:so + 128], in_=pb[7][0:TS, 0:128]), r=["pb7"], w=["kvs"])
                                P.op("sp", lambda e, ods=ods, so=so, p=p: e.dma_start(out=ods[:, p * 128:(p + 1) * 128], in_=kvs[:, so:so + 128]), r=["kvs"], dma=True, out=True)
                            P.op("act", lambda e: e.activation(out=vstok[0:TS, :], in_=kvs[:, 128:256], func=AF.Copy), r=["kvs"], w=["vstok"])

                            for hh in range(2):
                                pbs = hh * 64
                                bi = 6 + hh
                                for c0 in range(0, 25, 8):
                                    nch = min(8, 25 - c0)
                                    for ci in range(nch):
                                        col = (c0 + ci) * 64
                                        P.op("pe", lambda e, pbs=pbs, bi=bi, ci=ci, col=col: e.matmul(
                                            pb[bi][pbs:pbs + 64, ci * 64:(ci + 1) * 64], lhsT=zv[pbs:pbs + 64, col:col + 64],
                                            rhs=identf[pbs:pbs + 64, pbs:pbs + 64], start=True, stop=True), r=["zv", "cst"], w=[f"pb{bi}"])
                                    P.op("act" if hh == 0 else "dve",
                                         (lambda e, pbs=pbs, bi=bi, c0=c0, nch=nch: e.activation(out=vtok[pbs:pbs + 64, c0 * 64:(c0 + nch) * 64],
                                                                                              in_=pb[bi][pbs:pbs + 64, 0:nch * 64], func=AF.Copy)) if hh == 0 else
                                         (lambda e, pbs=pbs, bi=bi, c0=c0, nch=nch: e.tensor_copy(out=vtok[pbs:pbs + 64, c0 * 64:(c0 + nch) * 64],
                                                                                               in_=pb[bi][pbs:pbs + 64, 0:nch * 64])),
                                         r=[f"pb{bi}"], w=["vtok"])
                            for blk in range(4):
                                P.op("pe", lambda e, blk=blk, pi=pi: e.matmul(pb[6][:, blk * 128:(blk + 1) * 128],
                                                                          lhsT=ckf[pi][:, blk * 128:(blk + 1) * 128], rhs=identf, start=True, stop=True),
                                     r=[f"ckf{pi}", "cst"], w=["pb6"])
                            P.op("act", lambda e: e.activation(out=kTs[:], in_=pb[6][:, 0:512], func=AF.Copy), r=["pb6"], w=["kTs"])

                            def att_scores(c, par):
                                b0, b1 = (2, 3) if par == 0 else (4, 5)
                                qc = qcol(c * 64)
                                for jj in range(9):
                                    kc_ = kcol((c - 8 + jj) * 64)
                                    bi_ = b0 if jj < 8 else b1
                                    co = (jj % 8) * 64
                                    for hh in range(2):
                                        sl = slice(hh * 64, hh * 64 + 64)
                                        P.op("pe", lambda e: e.matmul(pb[bi_][sl, co:co + 64], lhsT=kT[sl, kc_:kc_ + 64], rhs=qT[sl, qc:qc + 64], start=True, stop=True),
                                             r=["kT", "qT"], w=[f"pb{bi_}"])
                                sbk = f"sbs{par}"
                                P.op("dve", lambda e: e.scalar_tensor_tensor(out=sbuf_s[par][:, 0:512], in0=pb[b0][:, 0:512], scalar=0.125, in1=bT[pi][:, 0:512], op0=ALU.mult, op1=ALU.add),
                                     r=[f"pb{b0}", f"bT{pi}"], w=[sbk])
                                P.op("dve", lambda e: e.scalar_tensor_tensor(out=sbuf_s[par][:, 512:576], in0=pb[b1][:, 0:64], scalar=0.125, in1=bT[pi][:, 512:576], op0=ALU.mult, op1=ALU.add),
                                     r=[f"pb{b1}", f"bT{pi}"], w=[sbk])
                                nm = min(8, max(0, 24 - c))
                                if nm > 0:
                                    P.op("act", lambda e: e.activation(out=pT[par][:, 0:nm * 64], in_=sbuf_s[par][:, 0:nm * 64], func=AF.Exp, bias=prm[:, PC["negm"]:PC["negm"] + 1]),
                                         r=[sbk, "prm"], w=[f"pT{par}"])
                                P.op("act", lambda e: e.activation(out=pT[par][:, nm * 64:576], in_=sbuf_s[par][:, nm * 64:576], func=AF.Exp), r=[sbk], w=[f"pT{par}"])

                            def att_pv(c, par):
                                bo = 6 + par
                                for jj in range(9):
                                    vc = (c - 8 + jj - 7) * 64
                                    for hh in range(2):
                                        sl = slice(hh * 64, hh * 64 + 64)
                                        P.op("pe", lambda e: e.matmul(pb[bo][sl, 0:64], lhsT=vtok[sl, vc:vc + 64], rhs=pT[par][sl, jj * 64:(jj + 1) * 64], start=(jj == 0), stop=(jj == 8)),
                                             r=["vtok", f"pT{par}"], w=[f"pb{bo}"])
                                for jj in range(9):
                                    for hh in range(2):
                                        sl = slice(hh * 64, hh * 64 + 64)
                                        P.op("pe", lambda e: e.matmul(pb[bo][sl, 64:128], lhsT=onesb[sl, 0:64], rhs=pT[par][sl, jj * 64:(jj + 1) * 64], start=(jj == 0), stop=(jj == 8)),
                                             r=["onesb", f"pT{par}"], w=[f"pb{bo}"])
                                P.op("dve", lambda e: e.reciprocal(out=rden[:, 0:64], in_=pb[bo][:, 64:128]), r=[f"pb{bo}"], w=["rden"])
                                mc = p * NMX + mixcol(c * 64)
                                P.op("dve", lambda e: e.tensor_tensor(out=mixT[:, mc:mc + 64], in0=pb[bo][:, 0:64], in1=rden[:, 0:64], op=ALU.mult),
                                     r=[f"pb{bo}", "rden"], w=[f"mixT{p}"])

                            def v3_(ap, w):
                                return ap.rearrange("p (u w) -> p u w", w=w)
                            P.op("pool", lambda e: e.memset(bT2[:, :], -30000.0), w=["bT2"])
                            P.op("pool", lambda e: e.tensor_copy(out=v3_(bT2[:, :], 128)[:, 0:9, 0:64], in_=v3_(bT[pi][:, 0:576], 64)), r=[f"bT{pi}"], w=["bT2"])
                            P.op("pool", lambda e: e.tensor_copy(out=v3_(bT2[:, :], 128)[:, 1:10, 64:128], in_=v3_(bT[pi][:, 0:576], 64)), r=[f"bT{pi}"], w=["bT2"])

                            def att2_scores(c, par):
                                banks = (2, 3, 4) if par == 0 else (5, 6, 7)
                                qc = qcol(c * 64)
                                for u in range(10):
                                    kc_ = kcol((c - 8 + u) * 64)
                                    bi_ = banks[u // 4]
                                    co = (u % 4) * 128
                                    for hh in range(2):
                                        sl = slice(hh * 64, hh * 64 + 64)
                                        P.op("pe", lambda e: e.matmul(pb[bi_][sl, co:co + 128], lhsT=kT[sl, kc_:kc_ + 64], rhs=qT[sl, qc:qc + 128], start=True, stop=True),
                                             r=["kT", "qT"], w=[f"pb{bi_}"])
                                for bi3, (c0_, n_) in enumerate(((0, 512), (512, 512), (1024, 256))):
                                    bb = banks[bi3]
                                    P.op("dve", lambda e: e.scalar_tensor_tensor(out=sb2[:, c0_:c0_ + n_], in0=pb[bb][:, 0:n_], scalar=0.125, in1=bT2[:, c0_:c0_ + n_], op0=ALU.mult, op1=ALU.add),
                                         r=[f"pb{bb}", "bT2"], w=["sb2"])
                                nmA = min(8, max(0, 24 - c))
                                nmB = min(8, max(0, 23 - c)) + 1 if c + 1 < 24 else 0
                                negm = prm[:, PC["negm"]:PC["negm"] + 1]
                                s3 = v3_(sb2[:, :], 128)
                                p3 = v3_(pT2[par][:, :], 128)
                                for (u0, u1, h0, masked) in ((0, nmA, 0, True), (nmA, 10, 0, False), (0, nmB, 64, True), (nmB, 10, 64, False)):
                                    if u1 <= u0:
                                        continue
                                    if masked:
                                        P.op("act", lambda e: e.activation(out=p3[:, u0:u1, h0:h0 + 64], in_=s3[:, u0:u1, h0:h0 + 64], func=AF.Exp, bias=negm), r=["sb2", "prm"], w=[f"pT2{par}"])
                                    else:
                                        P.op("act", lambda e: e.activation(out=p3[:, u0:u1, h0:h0 + 64], in_=s3[:, u0:u1, h0:h0 + 64], func=AF.Exp), r=["sb2"], w=[f"pT2{par}"])

                            def att2_pv(c, par):
                                bo = par
                                for u in range(10):
                                    vc = (c - 8 + u - 7) * 64
                                    for hh in range(2):
                                        sl = slice(hh * 64, hh * 64 + 64)
                                        P.op("pe", lambda e: e.matmul(pb[bo][sl, 0:128], lhsT=vtok[sl, vc:vc + 64], rhs=pT2[par][sl, u * 128:(u + 1) * 128], start=(u == 0), stop=(u == 9)),
                                             r=["vtok", f"pT2{par}"], w=[f"pb{bo}"])
                                for u in range(10):
                                    for hh in range(2):
                                        sl = slice(hh * 64, hh * 64 + 64)
                                        P.op("pe", lambda e: e.matmul(pb[bo][sl, 128:256], lhsT=onesb[sl, 0:64], rhs=pT2[par][sl, u * 128:(u + 1) * 128], start=(u == 0), stop=(u == 9)),
                                             r=["onesb", f"pT2{par}"], w=[f"pb{bo}"])
                                P.op("dve", lambda e: e.reciprocal(out=rden[:, 0:128], in_=pb[bo][:, 128:256]), r=[f"pb{bo}"], w=["rden"])
                                mc = p * NMX + mixcol(c * 64)
                                P.op("dve", lambda e: e.tensor_tensor(out=mixT[:, mc:mc + 128], in0=pb[bo][:, 0:128], in1=rden[:, 0:128], op=ALU.mult),
                                     r=[f"pb{bo}", "rden"], w=[f"mixT{p}"])

                            dlist = list(range(15, 31, 2))
                            for ci, c in enumerate(dlist):
                                att2_scores(c, ci % 2)
                                if ci > 0:
                                    att2_pv(dlist[ci - 1], (ci - 1) % 2)
                            att_scores(31, 0)
                            att2_pv(dlist[-1], (len(dlist) - 1) % 2)
                            att_pv(31, 0)

                            for hh in range(2):
                                pbs = hh * 64
                                sl = slice(pbs, pbs + 64)
                                b0, bo = 2 + 2 * hh, 6 + hh
                                qc = 1088
                                for blk in range(4):
                                    P.op("pe", lambda e, sl=sl, b0=b0, blk=blk: e.matmul(pb[b0][:, blk * 16:(blk + 1) * 16], lhsT=kTs[sl, blk * 128:(blk + 1) * 128],
                                                                                      rhs=qT[sl, qc:qc + TS], start=True, stop=True), r=["kTs", "qT"], w=[f"pb{b0}"])
                                P.op("pe", lambda e, sl=sl, b0=b0: e.matmul(pb[b0][0:TS, 64:80], lhsT=kT[sl, 1600:1600 + TS], rhs=qT[sl, qc:qc + TS], start=True, stop=True),
                                     r=["kT", "qT"], w=[f"pb{b0}"])
                                sbk = f"sbs{hh}"
                                P.op("dve", lambda e, b0=b0, hh=hh, pi=pi: e.scalar_tensor_tensor(
                                    out=sbuf_s[hh][:, 0:80], in0=pb[b0][:, 0:80], scalar=0.125, in1=bS[pi][:, hh * 80:(hh + 1) * 80], op0=ALU.mult, op1=ALU.add),
                                    r=[f"pb{b0}", f"bS{pi}"], w=[sbk])
                                P.op("dve", lambda e, hh=hh: e.memset(pT[hh][:, 0:80], 0.0), w=[f"pT{hh}"])
                                P.op("act", lambda e, hh=hh: e.activation(out=pT[hh][:, 0:64], in_=sbuf_s[hh][:, 0:64], func=AF.Exp), r=[sbk], w=[f"pT{hh}"])
                                P.op("act", lambda e, hh=hh: e.activation(out=pT[hh][0:TS, 64:80], in_=sbuf_s[hh][0:TS, 64:80], func=AF.Exp), r=[sbk], w=[f"pT{hh}"])
                                for blk in range(5):
                                    if blk < 4:
                                        lhs = cvb[pi][:, blk * 128:(blk + 1) * 128]
                                        rhs = pT[hh][:, blk * 16:(blk + 1) * 16]
                                        lo = onesb[:, :]
                                    else:
                                        lhs = vstok[0:TS, :]
                                        rhs = pT[hh][0:TS, 64:80]
                                        lo = onesb[0:TS, :]
                                    P.op("pe", lambda e, lhs=lhs, rhs=rhs, bo=bo, blk=blk: e.matmul(pb[bo][:, 0:TS], lhsT=lhs, rhs=rhs, start=(blk == 0), stop=(blk == 4)),
                                         r=[f"cvb{pi}", "vstok", f"pT{hh}"], w=[f"pb{bo}"])
                                for blk in range(5):
                                    if blk < 4:
                                        rhs = pT[hh][:, blk * 16:(blk + 1) * 16]
                                        lo = onesb[:, :]
                                    else:
                                        rhs = pT[hh][0:TS, 64:80]
                                        lo = onesb[0:TS, :]
                                    P.op("pe", lambda e, lo=lo, rhs=rhs, bo=bo, blk=blk: e.matmul(pb[bo][:, 64:64 + TS], lhsT=lo, rhs=rhs, start=(blk == 0), stop=(blk == 4)),
                                         r=["onesb", f"pT{hh}"], w=[f"pb{bo}"])
                                P.op("dve", lambda e, sl=sl, bo=bo: e.reciprocal(out=rden[sl, 0:TS], in_=pb[bo][sl, 64:64 + TS]), r=[f"pb{bo}"], w=["rden"])
                                mc = p * NMX + 1088
                                P.op("dve", lambda e, sl=sl, bo=bo, mc=mc: e.tensor_tensor(out=mixT[sl, mc:mc + TS], in0=pb[bo][sl, 0:TS], in1=rden[sl, 0:TS], op=ALU.mult),
                                     r=[f"pb{bo}", "rden"], w=[f"mixT{p}"])
                        P.barrier()
                        P.checkpoint("C", [("mixT", mixT[:], 16 * NMX, BF16)])

                    rwkv_phase(nc, P, es, pb, dense, h1T, H1R, mixT, NMX, prm, pcol, cst, identf, bdones, m_su, m_ui, m_sl, identb,
                               w_in_t, w2a2, g2d, state0, S_out, Ss_out, shcol, sh_out)
                    P.barrier()
                    P.checkpoint("D", [("mixT", mixT[:], 16 * NMX, BF16)])

                ffn_phase(nc, P, es, pb, Rr, mixT, NMX, prm, pcol, modT, mod, F1, identf, cst, xw, w_out_t, w_up_t, w_down_t,
                          y_own, y_s, cvcol, cv_out)

        except StopBuild:
            pass
        P.finish()
        sems = {}
        for sk in P.sem_keys():
            sems[sk] = es.enter_context(nc.semaphore("s_" + "_".join(str(x) for x in sk)))
        P.replay(nc, sems)
    return nc


def rwkv_phase(nc, P, es, pb, dense, h1T, H1R, mixT, NMX, prm, pcol, cst, identf, bdones, m_su, m_ui, m_sl, identb,
               w_in_t, w2a2, g2d, state0, S_out, Ss_out, shcol, sh_out):
    with ExitStack() as ph:
        def sb(name, shape, dtype=F32):
            return ph.enter_context(nc.sbuf_tensor(name, list(shape), dtype))
        flag = pcol("flag")
        twd_ad = sb("twd_ad", [128, NT], BF16)
        sgd = sb("sgd", [128, NT], BF16)
        w2a2b = sb("w2a2b", [128, 1024], BF16)
        g2b = sb("g2b", [128, 1024], BF16)
        P.op("pool", lambda e: e.dma_start(out=w2a2b[:], in_=w2a2[:, :]), w=["w2a2b"], dma=True)
        P.op("pool", lambda e: e.dma_start(out=g2b[:], in_=g2d[:, :]), w=["g2b"], dma=True)
        wr = [sb("wr0", [128, 2048], BF16)]
        wk_ = [sb("wkk0", [128, 2048], BF16)]
        wv_ = [sb("wvv0", [128, 2048], BF16)]
        wl = [wr[0], wk_[0]]
        wlk = ["wr0", "wkk0"]
        zb = sb("zb", [128, 1 + 512])
        zm = sb("zm", [128, 512])
        rmask = sb("rmask", [128, 512], BF16)
        P.op("pool", lambda e: e.memset(rmask[:], 1.0), w=["rmask"])
        P.op("pool", lambda e: e.memset(rmask[:].rearrange("p (c l) -> p c l", l=64)[:, :, 0:1], 0.0), w=["rmask"])

        tiles = [(0, 512), (512, 512), (1024, 512), (1536, 512), (2048, TS)]

        zcar = sb("zcar", [128, 4])

        def shifted(bank, bk, ch, c0, n, dst_fn, cs=3):
            first = c0 == 0
            samp = c0 == 2048
            car = zcar[:, cs:cs + 1]
            if first:
                P.op("dve", lambda e: e.memset(zb[:, 0:1], 0.0), w=["zb"])
            elif samp:
                P.op("dve", lambda e: e.tensor_copy(out=zb[:, 0:1], in_=prm[:, PC["shs"] + ch:PC["shs"] + ch + 1]), r=["prm"], w=["zb"])
            elif c0 == OWN0:
                P.op("dve", lambda e: e.tensor_scalar(out=zb[:, 0:1], in0=car, scalar1=flag, scalar2=None, op0=ALU.mult), r=["zcar", "prm"], w=["zb"])
            else:
                P.op("dve", lambda e: e.tensor_copy(out=zb[:, 0:1], in_=car), r=["zcar"], w=["zb"])
            P.op("act", lambda e: e.activation(out=zb[:, 1:1 + n], in_=bank[:, 0:n], func=AF.Copy), r=[bk], w=["zb"])
            P.op("dve", lambda e: e.tensor_copy(out=car, in_=zb[:, n:n + 1]), r=["zb"], w=["zcar"])
            if c0 == 1536 or samp:
                col = ch if not samp else 26 + ch
                P.op("dve", lambda e: e.tensor_copy(out=shcol[:, col:col + 1], in_=zb[:, n:n + 1]), r=["zb"], w=["shcol"])
            P.op("act", lambda e: e.activation(out=zm[:, 0:n], in_=bank[:, 0:n], func=AF.Identity, scale=pcol("omu", ch)), r=[bk, "prm"], w=["zm"])
            P.op("dve", lambda e: e.scalar_tensor_tensor(out=zm[:, 0:n], in0=zb[:, 0:n], scalar=pcol("mu", ch), in1=zm[:, 0:n], op0=ALU.mult, op1=ALU.add),
                 r=["zb", "zm", "prm"], w=["zm"])
            dst_fn(n)

        for li, ch in enumerate((24, 25)):
            P.op("pool", lambda e, li=li, ch=ch: e.dma_start(out=wl[li][:], in_=w_in_t[24 + ch]), w=[wlk[li]], dma=True)

            def ev(bank, bk, c0, n, ch=ch):
                def dst(n, c0=c0, ch=ch):
                    if ch == 24:
                        P.op("act", lambda e: e.activation(out=twd_ad[0:64, c0:c0 + n], in_=zm[0:64, 0:n], func=AF.Tanh), r=["zm"], w=["twd_ad"])
                        P.op("dve", lambda e: e.tensor_copy(out=twd_ad[64:128, c0:c0 + n], in_=zm[64:128, 0:n]), r=["zm"], w=["twd_ad"])
                    else:
                        P.op("act", lambda e: e.activation(out=sgd[:, c0:c0 + n], in_=zm[:, 0:n], func=AF.Sigmoid), r=["zm"], w=["sgd"])
                shifted(bank, bk, ch, c0, n, dst)
            dense(wl[li], wlk[li], tiles, ev)

        def t32(name, n=512):
            return sb(name, [128, n])

        def t16(name, n=512):
            return sb(name, [128, n], BF16)
        RKV = [(t32("r_f0"), t32("k_f0"), t32("v_f0"), t16("g_f0")), (t32("r_f1"), t32("k_f1"), t32("v_f1"), t16("g_f1"))]
        ptmp1, ptmp2, ptmp3 = t32("ptmp1"), t32("ptmp2"), t32("ptmp3")
        a_f, ld_f, cl_f, kkn, tmp1, tmp2, tmp3 = t32("a_f"), t32("ld_f"), t32("cl_f"), t32("kkn"), t32("tmp1"), t32("tmp2"), t32("tmp3")
        at_b, rp_b, bp_b, kp_b, bpp_b, kpp_b, v_b = (t16(n) for n in ("at_b", "rp_b", "bp_b", "kp_b", "bpp_b", "kpp_b", "v_b"))
        gL = sb("gL", [128, 8])
        BppT, KppT, Vt = (t16(n) for n in ("BppT", "KppT", "Vt"))
        NabT, NakT, ArbT, ArkT, Nab = (t16(n) for n in ("NabT", "NakT", "ArbT", "ArkT", "Nab"))
        Pq = [t16("Pq0"), t16("Pq1")]
        PqT = [t16("PqT0"), t16("PqT1")]
        Wb = sb("Wb", [128, 1024], BF16)
        QeffT = sb("QeffT", [128, 512])
        YlocT = sb("YlocT", [128, 512])
        McT = sb("McT", [128, 512])
        Dc = sb("Dc", [128, 512])
        diagG = sb("diagG", [128, 512])
        ST = [sb("ST0", [128, 64]), sb("ST1", [128, 64])]
        yT = t32("yT")
        sstg = sb("sstg", [64, 128])
        s0ins = [sb("s0in0", [64, 128]), sb("s0in1", [64, 128])]

        curh = [0]
        tile_list = []

        def run2(a, b):
            gens = [g_ for g_ in (a, b) if g_ is not None]
            while gens:
                for g_ in list(gens):
                    try:
                        next(g_)
                    except StopIteration:
                        gens.remove(g_)

        def make_tile(p, pi, ti, c0, n, sx):
            r_f, k_f, v_f, g_f = RKV[sx]
            s0in = s0ins[p % 2]
            if True:
                samp = c0 == 2048
                L = TS if samp else 64
                G = 1 if samp else 8
                want_y = samp or c0 >= 512

                def skey(g):
                    return "0" if (G == 1 or g < G // 2) else "1"

                def prep_gen():
                    if ti == 0:
                        for (wt, nm, ch) in ((wr[pi], "wr", 24 + p), (wk_[pi], "wkk", 32 + p), (wv_[pi], "wvv", 40 + p)):
                            P.op("pool", lambda e, wt=wt, ch=ch: e.dma_start(out=wt[:], in_=w_in_t[ch]), w=[f"{nm}{pi}"], dma=True)
                        P.op("sp", lambda e: e.dma_start(out=s0ins[p % 2][:], in_=state0[:, p * 128:(p + 1) * 128]), w=[f"s0in{p%2}"], dma=True)
                    for csl, (wt, nm, mch, dstt, dk) in enumerate(((wr[pi], "wr", p, r_f, f"r_f{sx}"), (wk_[pi], "wkk", 8 + p, k_f, f"k_f{sx}"), (wv_[pi], "wvv", 16 + p, v_f, f"v_f{sx}"))):
                        def ev(bank, bk, c0_, n_, mch=mch, dstt=dstt, dk=dk, csl=csl):
                            def dst(n__):
                                P.op("act", lambda e: e.activation(out=dstt[:, 0:n__], in_=zm[:, 0:n__], func=AF.Copy), r=["zm"], w=[dk])
                            shifted(bank, bk, mch, c0_, n_, dst, cs=csl)
                        dense(wt, f"{nm}{pi}", [(c0, n)], ev, banks=(6,))
                    P.op("pe", lambda e, c0=c0, n=n, p=p: e.matmul(pb[6][:, 0:n], lhsT=w2a2b[0:64, p * 128:(p + 1) * 128], rhs=twd_ad[0:64, c0:c0 + n], start=True, stop=True),
                         r=["w2a2b", "twd_ad"], w=["pb6"])
                    P.op("pe", lambda e, c0=c0, n=n, p=p: e.matmul(pb[7][:, 0:n], lhsT=w2a2b[64:128, p * 128:(p + 1) * 128], rhs=twd_ad[64:128, c0:c0 + n], start=True, stop=True),
                         r=["w2a2b", "twd_ad"], w=["pb7"])
                    yield
                    P.op("act", lambda e, n=n, p=p: e.activation(out=ld_f[:, 0:n], in_=pb[6][:, 0:n], func=AF.Sigmoid, bias=pcol("w0", p)), r=["pb6", "prm"], w=["ld_f"])
                    P.op("pool", lambda e, n=n: e.tensor_scalar(out=ld_f[:, 0:n], in0=ld_f[:, 0:n], scalar1=-float(np.exp(-0.5)), scalar2=None, op0=ALU.mult), r=["ld_f"], w=["ld_f"])
                    P.op("act", lambda e, n=n, p=p: e.activation(out=a_f[:, 0:n], in_=pb[7][:, 0:n], func=AF.Sigmoid, bias=pcol("a0", p)), r=["pb7", "prm"], w=["a_f"])
                    if want_y:
                        P.op("pe", lambda e, c0=c0, n=n, p=p: e.matmul(pb[6][:, 0:n], lhsT=g2b[:, p * 128:(p + 1) * 128], rhs=sgd[:, c0:c0 + n], start=True, stop=True),
                             r=["g2b", "sgd"], w=["pb6"])
                        P.op("act", lambda e, n=n: e.activation(out=g_f[:, 0:n], in_=pb[6][:, 0:n], func=AF.Copy), r=["pb6"], w=[f"g_f{sx}"])
                    yield
                    P.op("act", lambda e, n=n, p=p: e.activation(out=kkn[:, 0:n], in_=k_f[:, 0:n], func=AF.Identity, scale=pcol("kk", p)), r=[f"k_f{sx}", "prm"], w=["kkn"])
                    P.op("act", lambda e, n=n: e.activation(out=tmp1[:, 0:n], in_=kkn[:, 0:n], func=AF.Square), r=["kkn"], w=["tmp1"])
                    P.op("pe", lambda e, n=n: e.matmul(pb[7][:, 0:n], lhsT=bdones, rhs=tmp1[:, 0:n], start=True, stop=True), r=["tmp1", "cst"], w=["pb7"])
                    yield
                    P.op("dve", lambda e, n=n: e.tensor_scalar(out=tmp1[:, 0:n], in0=pb[7][:, 0:n], scalar1=1e-18, scalar2=None, op0=ALU.max), r=["pb7"], w=["tmp1"])
                    P.op("act", lambda e, n=n: e.activation(out=tmp1[:, 0:n], in_=tmp1[:, 0:n], func=AF.Ln), r=["tmp1"], w=["tmp1"])
                    P.op("act", lambda e, n=n: e.activation(out=tmp1[:, 0:n], in_=tmp1[:, 0:n], func=AF.Exp, scale=-0.5), r=["tmp1"], w=["tmp1"])
                    yield
                    P.op("pool", lambda e, n=n: e.tensor_tensor(out=kkn[:, 0:n], in0=kkn[:, 0:n], in1=tmp1[:, 0:n], op=ALU.mult), r=["kkn", "tmp1"], w=["kkn"])
                    P.op("act", lambda e, n=n, p=p: e.activation(out=tmp1[:, 0:n], in_=a_f[:, 0:n], func=AF.Identity, scale=pcol("ka", p), bias=pcol("omka", p)),
                         r=["a_f", "prm"], w=["tmp1"])
                    P.op("pool", lambda e, n=n: e.tensor_tensor(out=k_f[:, 0:n], in0=k_f[:, 0:n], in1=tmp1[:, 0:n], op=ALU.mult), r=[f"k_f{sx}", "tmp1"], w=[f"k_f{sx}"])
                    yield
                    P.op("pool", lambda e, n=n: e.tensor_tensor(out=tmp2[:, 0:n], in0=kkn[:, 0:n], in1=a_f[:, 0:n], op=ALU.mult), r=["kkn", "a_f"], w=["tmp2"])
                    if samp:
                        P.op("dve", lambda e, n=n: e.tensor_tensor_scan(out=cl_f[:, 0:n], data0=rmask[:, 0:n], data1=ld_f[:, 0:n], initial=0.0, op0=ALU.mult, op1=ALU.add),
                             r=["rmask", "ld_f"], w=["cl_f"])
                    else:
                        P.op("dve", lambda e, n=n: e.tensor_tensor_scan(out=cl_f[:, 0:n], data0=rmask[:, 0:n], data1=ld_f[:, 0:n], initial=0.0, op0=ALU.mult, op1=ALU.add),
                             r=["rmask", "ld_f"], w=["cl_f"])
                    clv = cl_f[:, 0:n].rearrange("p (g l) -> p g l", l=L)
                    clL = clv[:, :, L - 1:L].to_broadcast([128, G, L])
                    P.op("act", lambda e, n=n: e.activation(out=tmp1[:, 0:n], in_=cl_f[:, 0:n], func=AF.Exp), r=["cl_f"], w=["tmp1"])
                    P.op("pool", lambda e, n=n: e.tensor_tensor(out=tmp3[:, 0:n], in0=cl_f[:, 0:n], in1=ld_f[:, 0:n], op=ALU.subtract), r=["cl_f", "ld_f"], w=["tmp3"])
                    yield
                    P.op("act", lambda e, n=n: e.activation(out=tmp3[:, 0:n], in_=tmp3[:, 0:n], func=AF.Exp), r=["tmp3"], w=["tmp3"])
                    P.op("act", lambda e, n=n: e.activation(out=ld_f[:, 0:n], in_=cl_f[:, 0:n], func=AF.Exp, scale=-1.0), r=["cl_f", "tmp3"], w=["ld_f"])
                    P.op("pool", lambda e, n=n, clv=clv, clL=clL, L=L: e.tensor_tensor(out=a_f[:, 0:n].rearrange("p (g l) -> p g l", l=L), in0=clL, in1=clv, op=ALU.subtract),
                         r=["cl_f", "tmp2", f"k_f{sx}"], w=["a_f"])
                    yield
                    P.op("act", lambda e, n=n: e.activation(out=a_f[:, 0:n], in_=a_f[:, 0:n], func=AF.Exp), r=["a_f"], w=["a_f"])
                    yield

                def prep2_gen():
                    clv = cl_f[:, 0:n].rearrange("p (g l) -> p g l", l=L)
                    P.op("act", lambda e, clv=clv, G=G, L=L: e.activation(out=gL[:, 0:G], in_=clv[:, :, L - 1:L].rearrange("p g o -> p (g o)"), func=AF.Exp), r=["cl_f"], w=["gL"])
                    P.op("pool", lambda e, n=n: e.tensor_tensor(out=rp_b[:, 0:n], in0=r_f[:, 0:n], in1=tmp1[:, 0:n], op=ALU.mult), r=[f"r_f{sx}", "tmp1"], w=["rp_b"])
                    P.op("dve", lambda e, n=n: e.scalar_tensor_tensor(out=at_b[:, 0:n], in0=kkn[:, 0:n], scalar=-1.0, in1=tmp3[:, 0:n], op0=ALU.mult, op1=ALU.mult),
                         r=["kkn", "tmp3"], w=["at_b"])
                    P.op("pool", lambda e, n=n: e.tensor_tensor(out=bp_b[:, 0:n], in0=tmp2[:, 0:n], in1=ld_f[:, 0:n], op=ALU.mult), r=["tmp2", "ld_f"], w=["bp_b"])
                    P.op("pool", lambda e, n=n: e.tensor_tensor(out=kp_b[:, 0:n], in0=k_f[:, 0:n], in1=ld_f[:, 0:n], op=ALU.mult), r=[f"k_f{sx}", "ld_f"], w=["kp_b"])
                    P.op("dve", lambda e, n=n: e.tensor_tensor(out=bpp_b[:, 0:n], in0=tmp2[:, 0:n], in1=a_f[:, 0:n], op=ALU.mult), r=["tmp2", "a_f"], w=["bpp_b"])
                    P.op("pool", lambda e, n=n: e.tensor_tensor(out=kpp_b[:, 0:n], in0=k_f[:, 0:n], in1=a_f[:, 0:n], op=ALU.mult), r=[f"k_f{sx}", "a_f"], w=["kpp_b"])
                    P.op("act", lambda e, n=n: e.activation(out=v_b[:, 0:n], in_=v_f[:, 0:n], func=AF.Copy), r=[f"v_f{sx}"], w=["v_b"])
                    for hh in range(2):
                        sl = slice(hh * 64, hh * 64 + 64)
                        P.op("pool", lambda e, sl=sl, G=G, hh=hh: e.tensor_tensor(
                            out=diagG[sl, 0:G * 64].rearrange("p (g l) -> p g l", l=64),
                            in0=identf[sl, hh * 64:hh * 64 + 64].unsqueeze(1).to_broadcast([64, G, 64]),
                            in1=gL[sl, 0:G].unsqueeze(2).to_broadcast([64, G, 64]), op=ALU.mult), r=["cst", "gL"], w=["diagG"])
                    yield

                def alg_run():
                    nlev = int(np.log2(L))

                    def alg(s_, g0, g1):
                        Gs = g1 - g0
                        cols = slice(g0 * 64, g1 * 64)
                        Bk = [pb[3 * s_ + (i % 3)] for i in range(4)]
                        BK = [f"pb{3*s_+(i%3)}" for i in range(4)]
                        S = str(s_)

                        def v3(ap, w=64):
                            return ap.rearrange("p (g l) -> p g l", l=w)

                        def each(fn):
                            for g in range(g0, g1):
                                for hh in range(2):
                                    pbs = hh * 64
                                    fn(g, g - g0, slice(pbs, pbs + 64), slice(pbs, pbs + L), pbs)

                        for (src, skey, dstt, dkey, bi) in ((bpp_b, "bpp_b", BppT, "BppT", 1), (kpp_b, "kpp_b", KppT, "KppT", 2), (v_b, "v_b", Vt, "Vt", 3)):
                            each(lambda g, gl, sl, slL, pbs: P.op("pe", lambda e: e.matmul(Bk[bi][slL, gl * 64:(gl + 1) * 64], lhsT=src[sl, g * L:(g + 1) * L], rhs=identb[sl, pbs:pbs + 64],
                                                                                          start=True, stop=True), r=[skey, "identb"], w=[BK[bi]]))
                            P.op("dve", lambda e: e.tensor_copy(out=dstt[:, cols], in_=Bk[bi][:, 0:Gs * 64]), r=[BK[bi]], w=[dkey + S])
                        yield
                        for (lt, ltk, rt_, rtk, dstt, dkey, msk, bi) in (
                                (bp_b, "bp_b", at_b, "at_b", NabT, "NabT", m_su, 0), (kp_b, "kp_b", at_b, "at_b", NakT, "NakT", m_su, 1),
                                (bp_b, "bp_b", rp_b, "rp_b", ArbT, "ArbT", m_ui, 2), (kp_b, "kp_b", rp_b, "rp_b", ArkT, "ArkT", m_ui, 3),
                                (at_b, "at_b", bp_b, "bp_b", Nab, "Nab", m_sl, 0)):
                            each(lambda g, gl, sl, slL, pbs: P.op("pe", lambda e: e.matmul(Bk[bi][slL, gl * 64:gl * 64 + L], lhsT=lt[sl, g * L:(g + 1) * L], rhs=rt_[sl, g * L:(g + 1) * L],
                                                                                          start=True, stop=True), r=[ltk, rtk], w=[BK[bi]]))
                            P.op("dve", lambda e: e.tensor_tensor(out=v3(dstt[:, cols])[:, :, 0:L], in0=v3(Bk[bi][:, 0:Gs * 64])[:, :, 0:L], in1=v3(msk[:, 0:Gs * 64])[:, :, 0:L], op=ALU.mult),
                                 r=[BK[bi], "cst"], w=[dkey + S])
                            if bi == 3:
                                yield
                        yield
                        each(lambda g, gl, sl, slL, pbs: P.op("pe", lambda e: e.matmul(Bk[1][slL, gl * 64:(gl + 1) * 64], lhsT=NakT[slL, g * 64:g * 64 + L], rhs=Vt[slL, g * 64:(g + 1) * 64],
                                                                                      start=True, stop=True), r=["NakT" + S, "Vt" + S], w=[BK[1]]))
                        wcols = slice(g0 * 128, g1 * 128)
                        P.op("dve", lambda e: e.tensor_copy(out=v3(Wb[:, wcols], 128)[:, :, 64:128], in_=v3(Bk[1][:, 0:Gs * 64])), r=[BK[1]], w=["Wb" + S])
                        each(lambda g, gl, sl, slL, pbs: P.op("pe", lambda e: e.matmul(Bk[0][slL, gl * 64:(gl + 1) * 64], lhsT=at_b[sl, g * L:(g + 1) * L], rhs=identb[sl, pbs:pbs + 64],
                                                                                      start=True, stop=True), r=["at_b", "identb"], w=[BK[0]]))
                        P.op("dve", lambda e: e.tensor_copy(out=v3(Wb[:, wcols], 128)[:, :, 0:64], in_=v3(Bk[0][:, 0:Gs * 64])), r=[BK[0]], w=["Wb" + S])
                        yield
                        Pc, PcT, pk, ptk = Nab, NabT, "Nab" + S, "NabT" + S
                        for lev in range(nlev):
                            each(lambda g, gl, sl, slL, pbs: P.op("pe", lambda e: e.matmul(Bk[2][slL, gl * 128:(gl + 1) * 128], lhsT=PcT[slL, g * 64:g * 64 + L], rhs=Wb[slL, g * 128:(g + 1) * 128],
                                                                                          start=True, stop=True), r=[ptk, "Wb" + S], w=[BK[2]]))
                            if lev < nlev - 1:
                                each(lambda g, gl, sl, slL, pbs: P.op("pe", lambda e: e.matmul(Bk[0][slL, gl * 64:gl * 64 + L], lhsT=PcT[slL, g * 64:g * 64 + L], rhs=Pc[slL, g * 64:g * 64 + L],
                                                                                              start=True, stop=True), r=[pk, ptk], w=[BK[0]]))
                                each(lambda g, gl, sl, slL, pbs: P.op("pe", lambda e: e.matmul(Bk[1][slL, gl * 64:gl * 64 + L], lhsT=Pc[slL, g * 64:g * 64 + L], rhs=PcT[slL, g * 64:g * 64 + L],
                                                                                              start=True, stop=True), r=[pk, ptk], w=[BK[1]]))
                            P.op("dve", lambda e: e.tensor_tensor(out=Wb[:, wcols], in0=Bk[2][:, 0:Gs * 128], in1=Wb[:, wcols], op=ALU.add), r=[BK[2], "Wb" + S], w=["Wb" + S])
                            if lev < nlev - 1:
                                nP, nPT = Pq[lev % 2], PqT[lev % 2]
                                npk, nptk = f"Pq{lev%2}" + S, f"PqT{lev%2}" + S
                                P.op("dve", lambda e: e.tensor_copy(out=nP[:, cols], in_=Bk[0][:, 0:Gs * 64]), r=[BK[0]], w=[npk])
                                P.op("dve", lambda e: e.tensor_copy(out=nPT[:, cols], in_=Bk[1][:, 0:Gs * 64]), r=[BK[1]], w=[nptk])
                                Pc, PcT, pk, ptk = nP, nPT, npk, nptk
                            yield
                        yield "final"
                        each(lambda g, gl, sl, slL, pbs: P.op("pe", lambda e: e.matmul(Bk[0][sl, gl * 64:(gl + 1) * 64], lhsT=Wb[slL, g * 128:g * 128 + 64], rhs=BppT[slL, g * 64:(g + 1) * 64],
                                                                                      start=True, stop=True), r=["Wb" + S, "BppT" + S], w=[BK[0]]))
                        P.op("dve", lambda e: e.tensor_tensor(out=McT[:, cols], in0=Bk[0][:, 0:Gs * 64], in1=diagG[:, cols], op=ALU.add), r=[BK[0], "diagG"], w=["McT" + S])

                        def dc_mm(g, gl, sl, slL, pbs):
                            P.op("pe", lambda e: e.matmul(Bk[1][sl, gl * 64:(gl + 1) * 64], lhsT=BppT[slL, g * 64:(g + 1) * 64], rhs=Wb[slL, g * 128 + 64:(g + 1) * 128], start=True, stop=False),
                                 r=["Wb" + S, "BppT" + S], w=[BK[1]])
                            P.op("pe", lambda e: e.matmul(Bk[1][sl, gl * 64:(gl + 1) * 64], lhsT=KppT[slL, g * 64:(g + 1) * 64], rhs=Vt[slL, g * 64:(g + 1) * 64], start=False, stop=True),
                                 r=["KppT" + S, "Vt" + S], w=[BK[1]])
                        each(dc_mm)
                        P.op("dve", lambda e: e.tensor_copy(out=Dc[:, cols], in_=Bk[1][:, 0:Gs * 64]), r=[BK[1]], w=["Dc" + S])
                        if want_y:
                            each(lambda g, gl, sl, slL, pbs: P.op("pe", lambda e: e.matmul(Bk[2][sl, gl * 64:gl * 64 + L], lhsT=Wb[slL, g * 128:g * 128 + 64], rhs=ArbT[slL, g * 64:g * 64 + L],
                                                                                          start=True, stop=True), r=["Wb" + S, "ArbT" + S], w=[BK[2]]))
                            P.op("dve", lambda e: e.tensor_tensor(out=v3(QeffT[:, cols])[:, :, 0:L], in0=v3(Bk[2][:, 0:Gs * 64])[:, :, 0:L],
                                                                  in1=rp_b[:, g0 * L:g1 * L].rearrange("p (g l) -> p g l", l=L), op=ALU.add), r=[BK[2], "rp_b"], w=["QeffT" + S])

                            def yl_mm(g, gl, sl, slL, pbs):
                                P.op("pe", lambda e: e.matmul(Bk[3][sl, gl * 64:gl * 64 + L], lhsT=Wb[slL, g * 128 + 64:(g + 1) * 128], rhs=ArbT[slL, g * 64:g * 64 + L], start=True, stop=False),
                                     r=["Wb" + S, "ArbT" + S], w=[BK[3]])
                                P.op("pe", lambda e: e.matmul(Bk[3][sl, gl * 64:gl * 64 + L], lhsT=Vt[slL, g * 64:(g + 1) * 64], rhs=ArkT[slL, g * 64:g * 64 + L], start=False, stop=True),
                                     r=["Vt" + S, "ArkT" + S], w=[BK[3]])
                            each(yl_mm)
                            P.op("dve", lambda e: e.tensor_copy(out=YlocT[:, cols], in_=Bk[3][:, 0:Gs * 64]), r=[BK[3]], w=["YlocT" + S])
                        yield

                    return [alg(0, 0, 1)] if G == 1 else [alg(0, 0, G // 2), alg(1, G // 2, G)]


                def tail_gen():
                    cur = 0 if ti == 0 else curh[0]
                    if samp:
                        for hh in range(2):
                            sl = slice(hh * 64, hh * 64 + 64)
                            P.op("pe", lambda e, sl=sl, hh=hh: e.matmul(pb[7][sl, 0:64], lhsT=s0in[0:64, hh * 64:(hh + 1) * 64], rhs=identf[0:64, 0:64], start=True, stop=True),
                                 r=[f"s0in{p%2}", "cst"], w=["pb7"])
                        P.op("dve", lambda e: e.tensor_copy(out=ST[cur][:, :], in_=pb[7][:, 0:64]), r=["pb7"], w=[f"ST{cur}"])
                    elif ti == 0:
                        P.op("dve", lambda e: e.memset(ST[cur][:], 0.0), w=[f"ST{cur}"])
                    for g in range(G):
                        yield
                        chunk = (c0 // 64 + g) if not samp else 99
                        y_here = samp or chunk >= 15
                        sk_ = skey(g)
                        for hh in range(2):
                            sl = slice(hh * 64, hh * 64 + 64)
                            if y_here:
                                P.op("pe", lambda e, sl=sl, g=g, cur=cur: e.matmul(pb[6][sl, 0:L], lhsT=ST[cur][sl, :], rhs=QeffT[sl, g * 64:g * 64 + L], start=True, stop=True),
                                     r=[f"ST{cur}", "QeffT" + sk_], w=["pb6"])
                            P.op("pe", lambda e, sl=sl, g=g, cur=cur: e.matmul(pb[7][sl, 0:64], lhsT=McT[sl, g * 64:(g + 1) * 64], rhs=ST[cur][sl, :], start=True, stop=True),
                                 r=[f"ST{cur}", "McT" + sk_], w=["pb7"])
                        P.op("dve", lambda e, g=g, cur=cur: e.tensor_tensor(out=ST[1 - cur][:, :], in0=pb[7][:, 0:64], in1=Dc[:, g * 64:(g + 1) * 64], op=ALU.add),
                             r=["pb7", "Dc" + sk_], w=[f"ST{1-cur}"])
                        if y_here:
                            P.op("dve", lambda e, g=g: e.tensor_tensor(out=yT[:, g * L:(g + 1) * L], in0=pb[6][:, 0:L], in1=YlocT[:, g * 64:g * 64 + L], op=ALU.add),
                                 r=["pb6", "YlocT" + sk_], w=["yT"])
                        cur = 1 - cur
                        if chunk == 15:
                            P.op("dve", lambda e, cur=cur: e.tensor_scalar(out=ST[cur][:], in0=ST[cur][:], scalar1=flag, scalar2=None, op0=ALU.mult), r=[f"ST{cur}", "prm"], w=[f"ST{cur}"])
                    if c0 == 1536 or samp:
                        od = S_out if not samp else Ss_out
                        for hh in range(2):
                            h = 2 * p + hh
                            sl = slice(hh * 64, hh * 64 + 64)
                            P.op("pe", lambda e, sl=sl, hh=hh, cur=cur: e.matmul(pb[6 + hh][0:64, 0:64], lhsT=ST[cur][sl, :], rhs=identf[sl, hh * 64:hh * 64 + 64], start=True, stop=True),
                                 r=[f"ST{cur}", "cst"], w=[f"pb{6+hh}"])
                            P.op("dve", lambda e, h=h, hh=hh: e.tensor_copy(out=sstg[0:64, hh * 64:(hh + 1) * 64], in_=pb[6 + hh][0:64, 0:64]), r=[f"pb{6+hh}"], w=["sstg"])
                        P.op("sp", lambda e, od=od, p=p: e.dma_start(out=od[:, p * 128:(p + 1) * 128], in_=sstg[0:64, 0:128]), r=["sstg"], dma=True, out=True)
                    if want_y:
                        P.op("pe", lambda e, n=n: e.matmul(pb[7][:, 0:n], lhsT=bdones, rhs=yT[:, 0:n], start=True, stop=True), r=["yT", "cst"], w=["pb7"])
                        P.op("act", lambda e, n=n: e.activation(out=ptmp1[:, 0:n], in_=yT[:, 0:n], func=AF.Square), r=["yT"], w=["ptmp1"])
                        P.op("pe", lambda e, n=n: e.matmul(pb[6][:, 0:n], lhsT=bdones, rhs=ptmp1[:, 0:n], start=True, stop=True), r=["ptmp1", "cst"], w=["pb6"])
                        P.op("act", lambda e, n=n: e.activation(out=ptmp2[:, 0:n], in_=pb[7][:, 0:n], func=AF.Copy, scale=1.0 / 64), r=["pb7"], w=["ptmp2"])
                        P.op("pool", lambda e, n=n: e.tensor_tensor(out=ptmp3[:, 0:n], in0=ptmp2[:, 0:n], in1=ptmp2[:, 0:n], op=ALU.mult), r=["ptmp2"], w=["ptmp3"])
                        P.op("dve", lambda e, n=n: e.scalar_tensor_tensor(out=ptmp3[:, 0:n], in0=pb[6][:, 0:n], scalar=1.0 / 64, in1=ptmp3[:, 0:n], op0=ALU.mult, op1=ALU.subtract),
                             r=["pb6", "ptmp3"], w=["ptmp3"])
                        P.op("act", lambda e, n=n: e.activation(out=ptmp3[:, 0:n], in_=ptmp3[:, 0:n], func=AF.Ln, bias=GN_EPS), r=["ptmp3"], w=["ptmp3"])
                        P.op("act", lambda e, n=n: e.activation(out=ptmp3[:, 0:n], in_=ptmp3[:, 0:n], func=AF.Exp, scale=-0.5), r=["ptmp3"], w=["ptmp3"])
                        P.op("pool", lambda e, n=n: e.tensor_tensor(out=yT[:, 0:n], in0=yT[:, 0:n], in1=ptmp2[:, 0:n], op=ALU.subtract), r=["yT", "ptmp2"], w=["yT"])
                        P.op("pool", lambda e, n=n: e.tensor_tensor(out=yT[:, 0:n], in0=yT[:, 0:n], in1=ptmp3[:, 0:n], op=ALU.mult), r=["yT", "ptmp3"], w=["yT"])
                        P.op("act", lambda e, n=n, p=p: e.activation(out=yT[:, 0:n], in_=yT[:, 0:n], func=AF.Identity, scale=pcol("lnw", p), bias=pcol("lnb", p)), r=["yT", "prm"], w=["yT"])
                        P.op("dve", lambda e, n=n, p=p: e.scalar_tensor_tensor(out=ptmp1[:, 0:n], in0=r_f[:, 0:n], scalar=pcol("rk", p), in1=k_f[:, 0:n], op0=ALU.mult, op1=ALU.mult),
                             r=[f"r_f{sx}", f"k_f{sx}", "prm"], w=["ptmp1"])
                        P.op("pe", lambda e, n=n: e.matmul(pb[7][:, 0:n], lhsT=bdones, rhs=ptmp1[:, 0:n], start=True, stop=True), r=["ptmp1", "cst"], w=["pb7"])
                        P.op("dve", lambda e, n=n: e.tensor_tensor(out=ptmp1[:, 0:n], in0=pb[7][:, 0:n], in1=v_f[:, 0:n], op=ALU.mult), r=["pb7", f"v_f{sx}"], w=["ptmp1"])
                        P.op("pool", lambda e, n=n: e.tensor_tensor(out=yT[:, 0:n], in0=yT[:, 0:n], in1=ptmp1[:, 0:n], op=ALU.add), r=["yT", "ptmp1"], w=["yT"])
                        if samp:
                            lo, mc = 0, (8 + p) * NMX + 1088
                        elif c0 == 512:
                            lo, mc = 448, (8 + p) * NMX + 0
                        else:
                            lo, mc = 0, (8 + p) * NMX + (c0 - HALO0)
                        nn = n - lo
                        P.op("pool", lambda e, lo=lo, mc=mc, nn=nn: e.tensor_tensor(out=mixT[:, mc:mc + nn], in0=yT[:, lo:lo + nn], in1=g_f[:, lo:lo + nn], op=ALU.mult),
                             r=["yT", f"g_f{sx}"], w=[f"mixT{8+p}"])

                    curh[0] = cur
                    yield
                return prep_gen, prep2_gen, alg_run, tail_gen

        for p in range(8):
            pi = 0
            for ti, (c0, n) in enumerate(tiles):
                tile_list.append((p, pi, ti, c0, n))
        def drain(g_):
            for _ in g_:
                pass
        made = [make_tile(p_, pi_, ti_, c0_, n_, k_ % 2) for k_, (p_, pi_, ti_, c0_, n_) in enumerate(tile_list)]
        drain(made[0][0]())
        drain(made[0][1]())
        for t_ in range(len(made)):
            gens = list(made[t_][2]())
            tail_g = made[t_ - 1][3]() if t_ > 0 else None
            prep_g = made[t_ + 1][0]() if t_ + 1 < len(made) else None
            if tail_g is not None:
                gens.append(tail_g)
            prep_added = False
            while gens or not prep_added:
                if not prep_added and (tail_g is None or tail_g not in gens):
                    if prep_g is not None:
                        gens.append(prep_g)
                    prep_added = True
                for g_ in list(gens):
                    if g_ not in gens:
                        continue
                    try:
                        if next(g_) == "final" and tail_g is not None and tail_g in gens:
                            drain(tail_g)
                            gens.remove(tail_g)
                    except StopIteration:
                        gens.remove(g_)
            if t_ + 1 < len(made):
                drain(made[t_ + 1][1]())
        drain(made[-1][3]())
        P.op("sp", lambda e: e.dma_start(out=sh_out[:, :], in_=shcol[:]), r=["shcol"], dma=True, out=True)


def ffn_phase(nc, P, es, pb, xmidT, mixT, NMX, prm, pcol, modT, mod, F1, identf, cst, xw, w_out_t, w_up_t, w_down_t, y_own, y_s, cvcol, cv_out):
    flag = pcol("flag")
    with ExitStack() as ph:
        def sb(name, shape, dtype=F32):
            return ph.enter_context(nc.sbuf_tensor(name, list(shape), dtype))
        NX = 1040
        xhalo = sb("xhalo", [128, 16 * 64])

        def xm(kc, c, n):
            if c < 64:
                assert c + n <= 64
                return xhalo[:, kc * 64 + c:kc * 64 + c + n]
            if c < 1088:
                assert c + n <= 1088
                return xmidT[:, kc * NX + c - 64:kc * NX + c - 64 + n]
            return xmidT[:, kc * NX + 1024 + c - 1088:kc * NX + 1024 + c - 1088 + n]
        MIXR = [f"mixT{i}" for i in range(16)]
        with ExitStack() as phE:
            xblk = [phE.enter_context(nc.sbuf_tensor(f"xb{i}", [128, D], F32)) for i in range(2)]
            wo = [phE.enter_context(nc.sbuf_tensor(f"wo{i}", [128, 2048], BF16)) for i in range(2)]
            blocks = [(HALO0, 64, 0)] + [(OWN0 + i * 128, 128, 64 + i * 128) for i in range(8)] + [(TW, TS, 1088)]
            for bi_, (t0, n, col) in enumerate(blocks):
                xb = xblk[bi_ % 2]
                xk = f"xb{bi_%2}"
                P.op("sp", lambda e, xb=xb, t0=t0, n=n: e.dma_start(out=xb[:n, :], in_=xw[t0:t0 + n, :]), w=[xk], dma=True)
                for q4 in range(4):
                    bk = 4 + (bi_ * 4 + q4) % 4
                    for i in range(4):
                        kc = q4 * 4 + i
                        P.op("pe", lambda e, xb=xb, n=n, kc=kc, i=i, bk=bk: e.matmul(pb[bk][:, i * 128:i * 128 + n], lhsT=xb[:n, kc * 128:(kc + 1) * 128], rhs=identf[:n, :n],
                                                                               start=True, stop=True), r=[xk, "cst"], w=[f"pb{bk}"])
                    for i in range(4):
                        kc = q4 * 4 + i
                        dst = xm(kc, col, n)
                        if i % 2 == 0:
                            P.op("act", lambda e, dst=dst, bk=bk, i=i, n=n: e.activation(out=dst, in_=pb[bk][:, i * 128:i * 128 + n], func=AF.Copy), r=[f"pb{bk}"], w=[f"xm{kc}"])
                        else:
                            P.op("dve", lambda e, dst=dst, bk=bk, i=i, n=n: e.tensor_copy(out=dst, in_=pb[bk][:, i * 128:i * 128 + n]), r=[f"pb{bk}"], w=[f"xm{kc}"])
            otiles = [(0, 64), (64, 512), (576, 512), (1088, TS)]
            for j in range(16):
                wt = wo[j % 2]
                P.op("pool", lambda e, wt=wt, j=j: e.dma_start(out=wt[:], in_=w_out_t[j]), w=[f"wo{j%2}"], dma=True)
                for ti, (c0, n) in enumerate(otiles):
                    bk = ti % 2
                    v = 1 if c0 == 1088 else 0
                    for kc in range(16):
                        P.op("pe", lambda e, wt=wt, kc=kc, c0=c0, n=n, bk=bk: e.matmul(pb[bk][:, 0:n], lhsT=wt[:, kc * 128:(kc + 1) * 128], rhs=mixT[:, kc * NMX + c0:kc * NMX + c0 + n],
                                                                                 start=(kc == 0), stop=(kc == 15)), r=[f"wo{j%2}"] + MIXR, w=[f"pb{bk}"])
                    dst = xm(j, c0, n)
                    P.op("dve", lambda e, dst=dst, bk=bk, n=n, j=j, v=v: e.scalar_tensor_tensor(out=dst, in0=pb[bk][:, 0:n], scalar=mod(2, j, v), in1=dst, op0=ALU.mult, op1=ALU.add),
                         r=[f"pb{bk}", "modT2", f"xm{j}"], w=[f"xm{j}"])
            P.barrier()
        P.checkpoint("E", [("xmidT", xmidT[:], 16 * 1040, F32), ("xhalo", xhalo[:], 16 * 64, F32)])

        onesf = sb("onesf", [128, 128])
        P.op("dve", lambda e: e.memset(onesf[:], 1.0), w=["onesf"])
        NH = 592
        h2T = mixT
        uT = sb("uT", [128, NJ * 528], BF16)
        sqb = sb("sqb", [128, 512])
        rstd = sb("rstd", [128, NH])
        tmpn = sb("tmpn", [128, 512])
        wg = [sb(f"wg{i}", [128, 2048], BF16) for i in range(2)]
        wvv = [sb(f"wu{i}", [128, 2048], BF16) for i in range(2)]
        wd = [sb(f"wd{i}", [128, 22 * 128], BF16) for i in range(2)]
        gbuf = sb("gbuf", [128, 2 + 576])
        gsb = sb("gsb", [128, 2 + TS])
        carry = sb("carry", [128, NJ * 2])
        cacc = sb("cacc", [128, 576])
        gel = sb("gel", [128, 576])
        yTt = sb("yTt", [128, 528])
        ytok = sb("ytok", [128, 512])
        XMR = [f"xm{i}" for i in range(16)]

        for half in range(2):
            if half == 0:
                segs = [(0, 64), (64, 512), (1088, TS)]
            else:
                segs = [(576, 512)]
            loc = []
            o = 0
            for (c0, n) in segs:
                loc.append((c0, n, o))
                o += n
            NHc = o
            for (c0, n, lo) in loc:
                v = 1 if c0 == 1088 else 0
                for kc in range(16):
                    P.op("act", lambda e, kc=kc, c0=c0, n=n: e.activation(out=sqb[:, 0:n], in_=xm(kc, c0, n), func=AF.Square), r=[f"xm{kc}"], w=["sqb"])
                    P.op("pe", lambda e, kc=kc, n=n: e.matmul(pb[2][:, 0:n], lhsT=onesf[:], rhs=sqb[:, 0:n], start=(kc == 0), stop=(kc == 15)), r=["sqb", "onesf"], w=["pb2"])
                P.op("act", lambda e, n=n, lo=lo: e.activation(out=rstd[:, lo:lo + n], in_=pb[2][:, 0:n], func=AF.Sqrt, scale=1.0 / D, bias=RMS_EPS), r=["pb2"], w=["rstd"])
                P.op("dve", lambda e, n=n, lo=lo: e.reciprocal(out=rstd[:, lo:lo + n], in_=rstd[:, lo:lo + n]), r=["rstd"], w=["rstd"])
                for kc in range(16):
                    P.op("dve", lambda e, kc=kc, c0=c0, n=n, lo=lo: e.tensor_tensor(out=tmpn[:, 0:n], in0=xm(kc, c0, n), in1=rstd[:, lo:lo + n], op=ALU.mult),
                         r=[f"xm{kc}", "rstd"], w=["tmpn"])
                    P.op("act", lambda e, kc=kc, n=n, lo=lo, v=v: e.activation(out=h2T[:, kc * NH + lo:kc * NH + lo + n], in_=tmpn[:, 0:n], func=AF.Identity,
                                                                            scale=F1[:, kc * 2 + v:kc * 2 + v + 1], bias=mod(3, kc, v)), r=["tmpn", "F1", "modT3"], w=[f"h2T{kc}"])
            H2R = [f"h2T{kc}" for kc in range(16)]
            for j in range(NJ):
                wgt, wvt = wg[j % 2], wvv[j % 2]
                gb, vb = 2 * (j % 2), 1 + 2 * (j % 2)
                P.op("pool", lambda e, wgt=wgt, j=j: e.dma_start(out=wgt[:], in_=w_up_t[j]), w=[f"wg{j%2}"], dma=True)
                P.op("pool", lambda e, wvt=wvt, j=j: e.dma_start(out=wvt[:], in_=w_up_t[NJ + j]), w=[f"wu{j%2}"], dma=True)
                if half == 0:
                    gt = [(0, 512, 0), (512, 64, 512)]
                else:
                    gt = [(0, 512, 0)]
                for (lc, n, gcol) in gt:
                    bk = 0
                    for kc in range(16):
                        P.op("pe", lambda e, wgt=wgt, kc=kc, lc=lc, n=n: e.matmul(pb[gb][:, 0:n], lhsT=wgt[:, kc * 128:(kc + 1) * 128], rhs=h2T[:, kc * NH + lc:kc * NH + lc + n],
                                                                           start=(kc == 0), stop=(kc == 15)), r=[f"wg{j%2}"] + H2R, w=[f"pb{gb}"])
                    P.op("act", lambda e, n=n, gcol=gcol: e.activation(out=gbuf[:, 2 + gcol:2 + gcol + n], in_=pb[gb][:, 0:n], func=AF.Copy), r=[f"pb{gb}"], w=["gbuf"])
                if half == 0:
                    P.op("dve", lambda e: e.tensor_scalar(out=gbuf[:, 2:66], in0=gbuf[:, 2:66], scalar1=flag, scalar2=None, op0=ALU.mult), r=["gbuf", "prm"], w=["gbuf"])
                    g0 = 66
                else:
                    P.op("dve", lambda e, j=j: e.tensor_copy(out=gbuf[:, 0:2], in_=carry[:, 2 * j:2 * j + 2]), r=["carry"], w=["gbuf"])
                    g0 = 2
                if half == 0:
                    P.op("dve", lambda e, j=j, g0=g0: e.tensor_copy(out=carry[:, 2 * j:2 * j + 2], in_=gbuf[:, g0 + 510:g0 + 512]), r=["gbuf"], w=["carry"])
                else:
                    P.op("dve", lambda e, j=j, g0=g0: e.tensor_copy(out=cvcol[:, 2 * j:2 * j + 2], in_=gbuf[:, g0 + 510:g0 + 512]), r=["gbuf"], w=["cvcol"])

                def conv(src, s0, n, j=j):
                    P.op("dve", lambda e: e.tensor_scalar(out=cacc[:, 0:n], in0=src[:, s0 - 2:s0 - 2 + n], scalar1=pcol("dw0", j), scalar2=pcol("dwb", j), op0=ALU.mult, op1=ALU.add),
                         r=["gbuf", "gsb", "prm"], w=["cacc"])
                    P.op("dve", lambda e: e.scalar_tensor_tensor(out=cacc[:, 0:n], in0=src[:, s0 - 1:s0 - 1 + n], scalar=pcol("dw1", j), in1=cacc[:, 0:n], op0=ALU.mult, op1=ALU.add),
                         r=["gbuf", "gsb", "cacc", "prm"], w=["cacc"])
                    P.op("dve", lambda e: e.scalar_tensor_tensor(out=cacc[:, 0:n], in0=src[:, s0:s0 + n], scalar=pcol("dw2", j), in1=cacc[:, 0:n], op0=ALU.mult, op1=ALU.add),
                         r=["gbuf", "gsb", "cacc", "prm"], w=["cacc"])
                    P.op("act", lambda e: e.activation(out=gel[:, 0:n], in_=cacc[:, 0:n], func=AF.Gelu), r=["cacc"], w=["gel"])
                conv(gbuf, g0, 512)
                lcv = 64 if half == 0 else 0
                for kc in range(16):
                    P.op("pe", lambda e, wvt=wvt, kc=kc, lcv=lcv: e.matmul(pb[vb][:, 0:512], lhsT=wvt[:, kc * 128:(kc + 1) * 128], rhs=h2T[:, kc * NH + lcv:kc * NH + lcv + 512],
                                                                       start=(kc == 0), stop=(kc == 15)), r=[f"wu{j%2}"] + H2R, w=[f"pb{vb}"])
                P.op("dve", lambda e, j=j: e.tensor_tensor(out=uT[:, j * 528:j * 528 + 512], in0=pb[vb][:, 0:512], in1=gel[:, 0:512], op=ALU.mult), r=[f"pb{vb}", "gel"], w=[f"uT{j}"])
                if half == 0:
                    for kc in range(16):
                        P.op("pe", lambda e, wgt=wgt, kc=kc: e.matmul(pb[gb][:, 0:TS], lhsT=wgt[:, kc * 128:(kc + 1) * 128], rhs=h2T[:, kc * NH + 576:kc * NH + 592],
                                                                   start=(kc == 0), stop=(kc == 15)), r=[f"wg{j%2}"] + H2R, w=[f"pb{gb}"])
                    P.op("dve", lambda e, j=j: e.tensor_copy(out=gsb[:, 0:2], in_=prm[:, PC["cvs"] + 2 * j:PC["cvs"] + 2 * j + 2]), r=["prm"], w=["gsb"])
                    P.op("act", lambda e: e.activation(out=gsb[:, 2:2 + TS], in_=pb[gb][:, 0:TS], func=AF.Copy), r=[f"pb{gb}"], w=["gsb"])
                    P.op("dve", lambda e, j=j: e.tensor_copy(out=cvcol[:, 88 + 2 * j:88 + 2 * j + 2], in_=gsb[:, TS:TS + 2]), r=["gsb"], w=["cvcol"])
                    conv(gsb, 2, TS)
                    for kc in range(16):
                        P.op("pe", lambda e, wvt=wvt, kc=kc: e.matmul(pb[vb][:, 0:TS], lhsT=wvt[:, kc * 128:(kc + 1) * 128], rhs=h2T[:, kc * NH + 576:kc * NH + 592],
                                                                   start=(kc == 0), stop=(kc == 15)), r=[f"wu{j%2}"] + H2R, w=[f"pb{vb}"])
                    P.op("dve", lambda e, j=j: e.tensor_tensor(out=uT[:, j * 528 + 512:j * 528 + 528], in0=pb[vb][:, 0:TS], in1=gel[:, 0:TS], op=ALU.mult), r=[f"pb{vb}", "gel"], w=[f"uT{j}"])
            UR = [f"uT{j}" for j in range(NJ)]
            utiles = [(0, 512)] + ([(512, TS)] if half == 0 else [])
            for jo in range(16):
                for hf in range(2):
                    P.op("pool", lambda e, hf=hf, jo=jo: e.dma_start(out=wd[hf][:], in_=w_down_t[jo][:, hf * 2816:(hf + 1) * 2816]), w=[f"wd{hf}"], dma=True)
                for (uc, n) in utiles:
                    bk = 4 if n == 512 else 5
                    v = 0 if n == 512 else 1
                    for kc in range(NJ):
                        P.op("pe", lambda e, kc=kc, uc=uc, n=n, bk=bk: e.matmul(pb[bk][:, 0:n], lhsT=wd[kc // 22][:, (kc % 22) * 128:(kc % 22 + 1) * 128], rhs=uT[:, kc * 528 + uc:kc * 528 + uc + n],
                                                                                   start=(kc == 0), stop=(kc == NJ - 1)), r=[f"wd{kc//22}"] + UR, w=[f"pb{bk}"])
                    if n == 512:
                        xc = 64 if half == 0 else 576
                    else:
                        xc = 1088
                    P.op("dve", lambda e, jo=jo, n=n, bk=bk, xc=xc, uc=uc, v=v: e.scalar_tensor_tensor(
                        out=yTt[:, uc:uc + n], in0=pb[bk][:, 0:n], scalar=mod(5, jo, v), in1=xm(jo, xc, n), op0=ALU.mult, op1=ALU.add),
                        r=[f"pb{bk}", "modT5", f"xm{jo}"], w=["yTt"])
                    if n == 512:
                        for blk in range(4):
                            P.op("pe", lambda e, blk=blk: e.matmul(pb[6 + blk % 2][:, (blk // 2) * 128:(blk // 2) * 128 + 128], lhsT=yTt[:, blk * 128:(blk + 1) * 128], rhs=identf,
                                                                 start=True, stop=True), r=["yTt", "cst"], w=[f"pb{6+blk%2}"])
                        for blk in range(4):
                            P.op("act" if blk % 2 == 0 else "dve",
                                 (lambda e, blk=blk: e.activation(out=ytok[:, blk * 128:(blk + 1) * 128], in_=pb[6 + blk % 2][:, (blk // 2) * 128:(blk // 2) * 128 + 128], func=AF.Copy))
                                 if blk % 2 == 0 else
                                 (lambda e, blk=blk: e.tensor_copy(out=ytok[:, blk * 128:(blk + 1) * 128], in_=pb[6 + blk % 2][:, (blk // 2) * 128:(blk // 2) * 128 + 128])),
                                 r=[f"pb{6+blk%2}"], w=[f"ytok{blk}"])
                            r0 = half * 512 + blk * 128
                            P.op("sp", lambda e, blk=blk, r0=r0, jo=jo: e.dma_start(out=y_own[r0:r0 + 128, jo * 128:(jo + 1) * 128], in_=ytok[:, blk * 128:(blk + 1) * 128]),
                                 r=[f"ytok{blk}"], dma=True, out=True)
                    else:
                        P.op("pe", lambda e, uc=uc: e.matmul(pb[6][0:TS, 0:128], lhsT=yTt[:, uc:uc + TS], rhs=identf, start=True, stop=True), r=["yTt", "cst"], w=["pb6"])
                        P.op("dve", lambda e: e.tensor_copy(out=tmpn[0:TS, 0:128], in_=pb[6][0:TS, 0:128]), r=["pb6"], w=["tmpn"])
                        P.op("sp", lambda e, jo=jo: e.dma_start(out=y_s[:, jo * 128:(jo + 1) * 128], in_=tmpn[0:TS, 0:128]), r=["tmpn"], dma=True, out=True)
            P.barrier()
        P.op("sp", lambda e: e.dma_start(out=cv_out[:, :], in_=cvcol[:]), r=["cvcol"], dma=True, out=True)


_NC_CACHE = {}


def _tile_w(w, kc_n):
    K, N = w.shape
    return np.ascontiguousarray(w.reshape(K // 128, 128, N // 128, 128).transpose(2, 1, 0, 3)).reshape(N // 128, 128, (K // 128) * 128)


def _col(vec, n):
    return np.ascontiguousarray(vec.reshape(n, 128).T)


def _consts():
    c = np.zeros((128, 128 * 2 + 3 * 512), np.float32)
    c[:, 0:128] = np.eye(128, dtype=np.float32)
    bd = np.zeros((128, 128), np.float32)
    bd[0:64, 0:64] = 1
    bd[64:128, 64:128] = 1
    c[:, 128:256] = bd
    s = (np.arange(128) % 64)[:, None]
    t = np.arange(64)[None, :]
    su = (s < t).astype(np.float32)
    ui = (s <= t).astype(np.float32)
    sl = (s > t).astype(np.float32)
    c[:, 256:768] = np.tile(su, (1, 8))
    c[:, 768:1280] = np.tile(ui, (1, 8))
    c[:, 1280:1792] = np.tile(sl, (1, 8))
    return c


def prep_inputs(inp, cores):
    f = lambda k: np.asarray(inp[k], np.float32)
    shared = {}
    shared["w_ada_t"] = _tile_w(f("w_ada")[0], 16)
    shared["w_in_t"] = _tile_w(f("w_in")[0], 16)
    shared["w_out_t"] = _tile_w(f("w_out")[0], 16)
    shared["w_up_t"] = _tile_w(f("w_up")[0], 16)
    shared["w_down_t"] = _tile_w(f("w_down")[0], NJ)
    shared["badat"] = _col(f("b_ada")[0], 96)
    shared["w2a2"] = np.ascontiguousarray(np.concatenate([f("w2")[0], f("a2")[0]], axis=0))
    shared["g2"] = np.ascontiguousarray(f("g2")[0])
    shared["consts"] = _consts()
    tab = f("rel_bias")[0]
    k = np.arange(64)[:, None, None]
    jj = np.arange(9)[None, :, None]
    q = np.arange(64)[None, None, :]
    idx = np.clip(512 - 64 * jj + q - k, -128, 128) + 128
    bt = tab[:, idx].reshape(16, 64, 576)
    shared["biasT"] = np.ascontiguousarray(bt.reshape(8, 128, 576))
    ks = np.arange(640)[:, None]
    qs = np.arange(16)[None, :]
    idxs = np.clip(512 + qs - ks, -128, 128) + 128
    bs = tab[:, idxs]
    shared["biasS"] = np.ascontiguousarray(bs.reshape(16, 5, 128, 16).transpose(0, 2, 1, 3).reshape(16, 128, 80))

    pr = np.zeros((128, NP), np.float32)

    def put(name, arr):
        pr[:, PC[name]:PC[name] + arr.shape[1]] = arr
    put("nag", _col(f("norm_att_g")[0], 16))
    put("nfg", _col(f("norm_ffn_g")[0], 16))
    mu = f("mu_shift")[0]
    put("mu", _col(mu, 26))
    put("omu", _col(1.0 - mu, 26) * 0 + (1.0 - _col(mu, 26)))
    put("w0", _col(f("w0")[0], 8))
    put("a0", _col(f("a0")[0], 8))
    put("kk", _col(f("k_k")[0], 8))
    ka = _col(f("k_a")[0], 8)
    put("ka", ka)
    put("omka", 1.0 - ka)
    put("rk", _col(f("r_k")[0].reshape(-1), 8))
    put("lnw", _col(f("ln_x_w")[0], 8))
    put("lnb", _col(f("ln_x_b")[0], 8))
    put("qg", np.tile(f("q_norm_g")[0], 2)[:, None])
    put("kg", np.tile(f("k_norm_g")[0], 2)[:, None])
    dw = f("dw_conv")[0]
    put("dw0", _col(dw[0], NJ))
    put("dw1", _col(dw[1], NJ))
    put("dw2", _col(dw[2], NJ))
    put("dwb", _col(f("dw_bias")[0], NJ))

    xp, xs = f("x_prompt"), f("x_sample")
    cp, csm = f("c_prompt"), f("c_sample")
    maps = []
    for core in cores:
        b, half = core // 2, core % 2
        m = dict(shared)
        xwin = np.zeros((NT, D), np.float32)
        if half == 1:
            xwin[0:TW] = xp[b]
        else:
            xwin[0:OWN0] = xp[b, 0:1024]
            xwin[OWN0:TW] = xp[b, 0:1024]
        xwin[TW:] = xs[core]
        m["xw"] = xwin
        cT = np.zeros((128, 16, 2), np.float32)
        cT[:, :, 0] = _col(cp[b], 16)
        cT[:, :, 1] = _col(csm[core], 16)
        m["cT"] = cT.reshape(128, 32)
        p2 = pr.copy()
        p2[:, PC["flag"]] = float(half)
        p2[:, PC["negm"]] = (float(half) - 1.0) * 30000.0
        p2[:, PC["shs"]:PC["shs"] + 26] = _col(f("state_shift")[0, core], 26)
        cv = f("state_ffn_conv")[0, core]
        cvc = np.stack([_col(cv[0], NJ), _col(cv[1], NJ)], axis=2).reshape(128, 2 * NJ)
        p2[:, PC["cvs"]:PC["cvs"] + 88] = cvc
        m["params"] = p2
        m["cache_k"] = np.ascontiguousarray(f("cache_att_k")[0, core].reshape(512, 1024))
        m["cache_v"] = np.ascontiguousarray(f("cache_att_v")[0, core].reshape(512, 1024))
        m["state0"] = np.ascontiguousarray(f("state_rwkv")[0, core].transpose(1, 0, 2).reshape(64, 1024))
        maps.append(m)
    return maps


def assemble(res, cores=range(8)):
    y_p = np.zeros((4, 2048, D), np.float32)
    y_s = np.zeros((8, TS, D), np.float32)
    nk = np.zeros((1, 4, 512, 16, 64), np.float32)
    nv = np.zeros_like(nk)
    nS = np.zeros((1, 4, 16, 64, 64), np.float32)
    nsh = np.zeros((1, 4, 3328), np.float32)
    ncv = np.zeros((1, 4, 2, DFF), np.float32)
    sk = np.zeros((1, 8, TS, 16, 64), np.float32)
    sv = np.zeros_like(sk)
    sS = np.zeros((1, 8, 16, 64, 64), np.float32)
    ssh = np.zeros((1, 8, 3328), np.float32)
    scv = np.zeros((1, 8, 2, DFF), np.float32)
    for core, r in zip(cores, res):
        b, half = core // 2, core % 2
        y_p[b, half * 1024:(half + 1) * 1024] = r["y_own"]
        y_s[core] = r["y_s"]
        sk[0, core] = r["ks_out"].reshape(TS, 16, 64)
        sv[0, core] = r["vs_out"].reshape(TS, 16, 64)
        sS[0, core] = r["Ss_out"].reshape(64, 16, 64).transpose(1, 0, 2)
        sh = r["sh_out"]
        ssh[0, core] = sh[:, 26:52].T.reshape(-1)
        cv = r["cv_out"]
        scv[0, core] = cv[:, 88:176].reshape(128, NJ, 2).transpose(2, 1, 0).reshape(2, DFF)
        if half == 1:
            nk[0, b] = r["k_out"].reshape(512, 16, 64)
            nv[0, b] = r["v_out"].reshape(512, 16, 64)
            nS[0, b] = r["S_out"].reshape(64, 16, 64).transpose(1, 0, 2)
            nsh[0, b] = sh[:, 0:26].T.reshape(-1)
            ncv[0, b] = cv[:, 0:88].reshape(128, NJ, 2).transpose(2, 1, 0).reshape(2, DFF)
    return (y_p, y_s, nk, nv, nS, nsh, ncv, sk, sv, sS, ssh, scv)


def kernel(**inputs):
    if "nc" not in _NC_CACHE:
        _NC_CACHE["nc"] = build_program()
    nc = _NC_CACHE["nc"]
    cores = list(range(8))
    maps = prep_inputs(inputs, cores)
    res = run_bass_kernel_spmd(nc, maps, core_ids=cores)
    return assemble(res.results, cores)
```

```python
import types
import numpy as np
from contextlib import ExitStack
import concourse.bass as bass
import concourse.mybir as mybir
from concourse.bass_utils import run_bass_kernel_spmd

F32 = mybir.dt.float32
BF16 = mybir.dt.bfloat16
AF = mybir.ActivationFunctionType
ALU = mybir.AluOpType

D = 2048
TW = 2048
TS = 16
NT = TW + TS
DFF = 5632
NJ = DFF // 128
OWN0 = 1024
HALO0 = 960
KV0 = 448
RMS_EPS = 1e-6
GN_EPS = 64e-5
EPOCH = 16000
NDMA_SEM = 8

PC = {}
_o = 0
for _n, _w in [("nag", 16), ("nfg", 16), ("mu", 26), ("omu", 26), ("w0", 8), ("a0", 8), ("kk", 8), ("ka", 8), ("omka", 8),
               ("rk", 8), ("lnw", 8), ("lnb", 8), ("qg", 1), ("kg", 1), ("dw0", 44), ("dw1", 44), ("dw2", 44),
               ("dwb", 44), ("flag", 1), ("negm", 1), ("shs", 26), ("cvs", 88)]:
    PC[_n] = _o
    _o += _w
NP = _o


class Prog:
    ENG = ["pe", "act", "dve", "pool", "sp"]

    def __init__(self):
        self.ops = {e: [] for e in self.ENG}
        self.cnt = {e: 0 for e in self.ENG}
        self.ndma = {e: 0 for e in self.ENG}
        self.waited = {e: {} for e in self.ENG}
        self.lastw = {}
        self.readers = {}
        self.same_sync = {"pe": False, "act": True, "dve": True, "pool": True, "sp": True}
        self.out_events = []
        self.last_ps = {}

    def _wait(self, eng, ev, waits):
        sk, val = ev
        if sk[0] == "eng":
            e2, ep = sk[1], sk[2]
            if e2 == eng and not self.same_sync[eng]:
                return
            cur = self.waited[eng].get(("engmax", e2), (-1, 0))
            if (ep, val) <= cur:
                return
            self.waited[eng][("engmax", e2)] = (ep, val)
            waits.append((sk, val))
        else:
            if self.waited[eng].get(sk, 0) >= val:
                return
            self.waited[eng][sk] = val
            waits.append((sk, val))

    @staticmethod
    def _freeze(fn):
        if fn.__closure__ is None:
            return fn
        cells = []
        for c in fn.__closure__:
            try:
                cells.append(types.CellType(c.cell_contents))
            except ValueError:
                cells.append(c)
        g = types.FunctionType(fn.__code__, fn.__globals__, fn.__name__, fn.__defaults__, tuple(cells))
        g.__kwdefaults__ = fn.__kwdefaults__
        return g

    dead = False

    def op(self, eng, fn, r=(), w=(), dma=False, out=False):
        if self.dead:
            return None
        fn = self._freeze(fn)
        deps = []
        for b in r:
            if b in self.lastw:
                deps.append(self.lastw[b])
        for b in w:
            if b in self.lastw:
                deps.append(self.lastw[b])
            deps.extend(self.readers.get(b, ()))
        is_ps = eng in ("act", "dve") and any(k.startswith("pb") for k in list(r) + list(w))
        if is_ps:
            other = "dve" if eng == "act" else "act"
            if other in self.last_ps:
                deps.append(self.last_ps[other])
        best = {}
        for (sk, val) in deps:
            if sk[0] == "eng":
                key = ("eng", sk[1])
                cand = (sk[2], val)
                if key not in best or cand > best[key]:
                    best[key] = cand
            else:
                if sk not in best or val > best[sk]:
                    best[sk] = val
        waits = []
        for key, v in best.items():
            if key[0] == "eng":
                self._wait(eng, (("eng", key[1], v[0]), v[1]), waits)
            else:
                self._wait(eng, (key, v), waits)
        if dma:
            n = self.ndma[eng]
            self.ndma[eng] += 1
            i = n % NDMA_SEM
            sk = ("dma", eng, i)
            val = 16 * (n // NDMA_SEM + 1)
            if n >= NDMA_SEM:
                self._wait(eng, (sk, val - 16), waits)
            ev = (sk, val)
        else:
            c = self.cnt[eng]
            self.cnt[eng] += 1
            ev = (("eng", eng, c // EPOCH), c % EPOCH + 1)
        self.ops[eng].append((waits, fn, ev, dma))
        if is_ps:
            self.last_ps[eng] = ev
        for b in w:
            self.lastw[b] = ev
            self.readers[b] = []
        for b in r:
            if b not in w:
                self.readers.setdefault(b, []).append(ev)
        if out:
            self.out_events.append(ev)
        return ev

    def barrier(self):
        if self.dead:
            return
        evs = []
        for e in self.ENG:
            if self.cnt[e] > 0:
                c = self.cnt[e] - 1
                evs.append((("eng", e, c // EPOCH), c % EPOCH + 1))
            n = self.ndma[e]
            for i in range(NDMA_SEM):
                if n > i:
                    last = ((n - 1 - i) // NDMA_SEM) * NDMA_SEM + i
                    evs.append((("dma", e, i), 16 * (last // NDMA_SEM + 1)))
        for e in self.ENG:
            waits = []
            for ev in evs:
                self._wait(e, ev, waits)
            if waits:
                self.ops[e].append((waits, None, None, False))
        self.lastw = {}
        self.readers = {}

    def finish(self):
        waits = []
        for ev in self.out_events:
            self._wait("sp", ev, waits)
        self.ops["sp"].append((waits, None, None, False))

    def sem_keys(self):
        ks = set()
        for e in self.ENG:
            for waits, fn, ev, dma in self.ops[e]:
                if ev is not None:
                    ks.add(ev[0])
        return sorted(ks)

    def replay(self, nc, sems):
        engmap = {"pe": "tensor", "act": "scalar", "dve": "vector", "pool": "gpsimd", "sp": "sync"}
        with nc.Block() as block:
            for ename in self.ENG:
                ops = self.ops[ename]

                def body(e, ops=ops):
                    for waits, fn, ev, dma in ops:
                        for sk, val in waits:
                            e.wait_ge(sems[sk], val)
                        if fn is not None:
                            ins = fn(e)
                            ins.then_inc(sems[ev[0]], 16 if dma else 1)
                getattr(block, engmap[ename])(body)


class StopBuild(Exception):
    pass


def build_program(stop=None):
    nc = bass.Bass("TRN2", target_bir_lowering=False)
    P = Prog()
    dt = nc.dram_tensor
    P.stop = stop
    P.dbg_names = []

    def checkpoint(name, dumps):
        if stop != name:
            return
        P.barrier()
        for (label, ap, ncols, dtype) in dumps:
            d = dt("dbg_" + label, [ap.shape[0], ncols], dtype, kind="ExternalOutput").ap()
            P.dbg_names.append("dbg_" + label)
            step = 4096
            for c0 in range(0, ncols, step):
                c1 = min(ncols, c0 + step)
                P.op("sp", lambda e, d=d, ap=ap, c0=c0, c1=c1: e.dma_start(out=d[:, c0:c1], in_=ap[:, c0:c1]), dma=True, out=True)
        P.dead = True
    P.checkpoint = checkpoint

    def din(name, shape, dtype=F32):
        return dt(name, list(shape), dtype, kind="ExternalInput").ap()

    def dout(name, shape):
        return dt(name, list(shape), F32, kind="ExternalOutput").ap()

    xw = din("xw", [NT, D])
    cT = din("cT", [128, 32])
    params = din("params", [128, NP])
    badat = din("badat", [128, 96])
    w_ada_t = din("w_ada_t", [96, 128, 2048])
    w_in_t = din("w_in_t", [50, 128, 2048])
    w_out_t = din("w_out_t", [16, 128, 2048])
    w_up_t = din("w_up_t", [88, 128, 2048])
    w_down_t = din("w_down_t", [16, 128, NJ * 128])
    w2a2 = din("w2a2", [128, 1024])
    g2d = din("g2", [128, 1024])
    biasT = din("biasT", [8, 128, 576])
    biasS = din("biasS", [16, 128, 80])
    consts = din("consts", [128, 128 * 2 + 3 * 512])
    cache_k = din("cache_k", [512, 1024])
    cache_v = din("cache_v", [512, 1024])
    state0 = din("state0", [64, 1024])

    y_own = dout("y_own", [1024, D])
    y_s = dout("y_s", [TS, D])
    k_out = dout("k_out", [512, 1024])
    v_out = dout("v_out", [512, 1024])
    ks_out = dout("ks_out", [TS, 1024])
    vs_out = dout("vs_out", [TS, 1024])
    S_out = dout("S_out", [64, 1024])
    Ss_out = dout("Ss_out", [64, 1024])
    sh_out = dout("sh_out", [128, 52])
    cv_out = dout("cv_out", [128, 176])

    es = ExitStack()
    with es:
        def sb(name, shape, dtype=F32):
            return es.enter_context(nc.sbuf_tensor(name, list(shape), dtype))

        pb = [es.enter_context(nc.psum_tensor(f"pb{i}", [128, 512], F32)) for i in range(8)]

        try:
            prm = sb("prm", [128, NP])
            cst = sb("cst", [128, 256])
            mskb = sb("mskb", [128, 3 * 512], BF16)
            identf = cst[:, 0:128]
            bdones = cst[:, 128:256]
            m_su = mskb[:, 0:512]
            m_ui = mskb[:, 512:1024]
            m_sl = mskb[:, 1024:1536]
            identb = sb("identb", [128, 128], BF16)
            onesb = sb("onesb", [128, 128], BF16)
            modT = sb("modT", [128, 192])
            A1 = sb("A1", [128, 32])
            F1 = sb("F1", [128, 32])
            shcol = sb("shcol", [128, 52])
            cvcol = sb("cvcol", [128, 176])

            def pcol(name, i=0):
                c = PC[name] + i
                return prm[:, c:c + 1]

            def mod(g, kc, v):
                c = (g * 16 + kc) * 2 + v
                return modT[:, c:c + 1]

            P.op("sp", lambda e: e.dma_start(out=prm[:], in_=params[:, :]), w=["prm"], dma=True)
            P.op("sp", lambda e: e.dma_start(out=cst[:], in_=consts[:, 0:256]), w=["cst"], dma=True)
            P.op("pool", lambda e: e.dma_start(out=mskb[:], in_=consts[:, 256:1792]), w=["cst"], dma=True)
            P.op("dve", lambda e: e.tensor_copy(out=identb[:], in_=identf), r=["cst"], w=["identb"])
            P.op("dve", lambda e: e.memset(onesb[:], 1.0), w=["onesb"])

            with ExitStack() as ph:
                def sbp(name, shape, dtype=F32):
                    return ph.enter_context(nc.sbuf_tensor(name, list(shape), dtype))
                cTs = sbp("cTs", [128, 32])
                cs = sbp("cs", [128, 32], BF16)
                bad = sbp("bad", [128, 96])
                wab = [sbp(f"wab{i}", [128, 2048], BF16) for i in range(4)]
                P.op("sp", lambda e: e.dma_start(out=cTs[:], in_=cT[:, :]), w=["cTs"], dma=True)
                P.op("sp", lambda e: e.dma_start(out=bad[:], in_=badat[:, :]), w=["bad"], dma=True)
                P.op("act", lambda e: e.activation(out=cs[:], in_=cTs[:], func=AF.Silu), r=["cTs"], w=["cs"])
                import os
                for j in range(96 if not os.environ.get("KSKIPA") else 0):
                    wb = wab[j % 4]
                    P.op("pool", lambda e, wb=wb, j=j: e.dma_start(out=wb[:], in_=w_ada_t[j]), w=[f"wab{j%4}"], dma=True)
                    for kc in range(16):
                        P.op("pe", lambda e, wb=wb, j=j, kc=kc: e.matmul(pb[0][:, 2 * j:2 * j + 2], lhsT=wb[:, kc * 128:(kc + 1) * 128],
                                                                     rhs=cs[:, 2 * kc:2 * kc + 2], start=(kc == 0), stop=(kc == 15)),
                             r=[f"wab{j%4}", "cs"], w=["pb0"])
                    if j % 16 == 15:
                        g = j // 16
                        P.op("dve", lambda e, g=g: e.tensor_tensor(
                            out=modT[:, g * 32:(g + 1) * 32].rearrange("p (j v) -> p j v", v=2),
                            in0=pb[0][:, g * 32:(g + 1) * 32].rearrange("p (j v) -> p j v", v=2),
                            in1=bad[:, g * 16:(g + 1) * 16].unsqueeze(2).to_broadcast([128, 16, 2]), op=ALU.add),
                            r=["pb0", "bad"], w=[f"modT{g}"])
                        if g in (1, 4):
                            dst = A1 if g == 1 else F1
                            gn = "nag" if g == 1 else "nfg"
                            P.op("dve", lambda e, g=g, dst=dst: e.tensor_scalar(out=dst[:], in0=modT[:, g * 32:(g + 1) * 32], scalar1=1.0,
                                                                              scalar2=None, op0=ALU.add),
                                 r=[f"modT{g}"], w=["A1" if g == 1 else "F1"])
                            P.op("dve", lambda e, dst=dst, gn=gn: e.tensor_tensor(
                                out=dst[:].rearrange("p (j v) -> p j v", v=2), in0=dst[:].rearrange("p (j v) -> p j v", v=2),
                                in1=prm[:, PC[gn]:PC[gn] + 16].unsqueeze(2).to_broadcast([128, 16, 2]), op=ALU.mult),
                                r=["prm"], w=["A1" if g == 1 else "F1"])
                P.barrier()
                P.checkpoint("A", [("modT", modT[:], 192, F32), ("A1", A1[:], 32, F32), ("F1", F1[:], 32, F32)])

            with ExitStack() as ph:
                def sbp(name, shape, dtype=F32):
                    return ph.enter_context(nc.sbuf_tensor(name, list(shape), dtype))
                NMX = 1088 + TS
                Rr = sbp("Rr", [128, 16 * 1040])
                mixT = sbp("mixT", [128, 16 * NMX], BF16)

                def mixcol(tok):
                    return tok - HALO0

                with ExitStack() as ph2:
                    def sb2(name, shape, dtype=F32):
                        return ph2.enter_context(nc.sbuf_tensor(name, list(shape), dtype))
                    h1T = Rr[:].bitcast(BF16)

                    with ExitStack() as ph3:
                        def sb3(name, shape, dtype=F32):
                            return ph3.enter_context(nc.sbuf_tensor(name, list(shape), dtype))
                        xblk = [sb3(f"xblk{i}", [128, D]) for i in range(2)]
                        junk = sb3("junk", [128, D], BF16)
                        ssb = sb3("ssb", [128, 17])
                        rsb = sb3("rsb", [128, 17])
                        for tb in range(17):
                            n = 128 if tb < 16 else TS
                            v = 0 if tb < 16 else 1
                            xb = xblk[tb % 2]
                            xk = f"xblk{tb%2}"
                            P.op("sp", lambda e, xb=xb, tb=tb, n=n: e.dma_start(out=xb[:n, :], in_=xw[tb * 128:tb * 128 + n, :]), w=[xk], dma=True)
                            P.op("act", lambda e, xb=xb, tb=tb, n=n: e.activation(out=junk[:n, :], in_=xb[:n, :], func=AF.Square,
                                                                               accum_out=ssb[:n, tb:tb + 1]), r=[xk], w=["junk", f"ss{tb}"])
                            P.op("act", lambda e, tb=tb, n=n: e.activation(out=rsb[:n, tb:tb + 1], in_=ssb[:n, tb:tb + 1], func=AF.Sqrt,
                                                                        scale=1.0 / D, bias=RMS_EPS), r=[f"ss{tb}"], w=[f"rs{tb}"])
                            P.op("dve", lambda e, tb=tb, n=n: e.reciprocal(out=rsb[:n, tb:tb + 1], in_=rsb[:n, tb:tb + 1]), r=[f"rs{tb}"], w=[f"rs{tb}"])
                            P.op("dve", lambda e, xb=xb, tb=tb, n=n: e.tensor_scalar(out=xb[:n, :], in0=xb[:n, :], scalar1=rsb[:n, tb:tb + 1],
                                                                                  scalar2=None, op0=ALU.mult), r=[xk, f"rs{tb}"], w=[xk])
                            if tb == 0:
                                P.checkpoint("B0", [("xb", xblk[0][:], D, F32), ("rsb", rsb[:], 17, F32), ("ssb", ssb[:], 17, F32)])
                            if tb == 1:
                                P.checkpoint("B1", [("h1T", h1T[:, 0:16 * NT], 16 * NT, BF16)])
                            for q4 in range(4):
                                gq = tb * 4 + q4
                                for i in range(4):
                                    kc = q4 * 4 + i
                                    bi = (1 + gq % 2) if i % 2 == 0 else (3 + gq % 2)
                                    co = (i // 2) * 128
                                    P.op("pe", lambda e, xb=xb, n=n, kc=kc, co=co, bi=bi: e.matmul(
                                        pb[bi][:, co:co + n], lhsT=xb[:n, kc * 128:(kc + 1) * 128], rhs=identf[:n, :n], start=True, stop=True),
                                        r=[xk, "cst"], w=[f"pb{bi}"])
                                for i in range(4):
                                    kc = q4 * 4 + i
                                    bi = (1 + gq % 2) if i % 2 == 0 else (3 + gq % 2)
                                    co = (i // 2) * 128
                                    dstap = h1T[:, kc * NT + tb * 128:kc * NT + tb * 128 + n]
                                    if i % 2 == 0:
                                        P.op("dve", lambda e, dstap=dstap, bi=bi, co=co, n=n, kc=kc, v=v: e.tensor_scalar(
                                            out=dstap, in0=pb[bi][:, co:co + n], scalar1=A1[:, kc * 2 + v:kc * 2 + v + 1], scalar2=mod(0, kc, v),
                                            op0=ALU.mult, op1=ALU.add), r=[f"pb{bi}", "A1", "modT0"], w=[f"h1T{kc}"])
                                    else:
                                        P.op("act", lambda e, dstap=dstap, bi=bi, co=co, n=n, kc=kc, v=v: e.activation(
                                            out=dstap, in_=pb[bi][:, co:co + n], func=AF.Identity, scale=A1[:, kc * 2 + v:kc * 2 + v + 1],
                                            bias=mod(0, kc, v)), r=[f"pb{bi}", "A1", "modT0"], w=[f"h1T{kc}"])
                                if tb == 0 and q4 == 0:
                                    P.checkpoint("B0a", [("xb", xblk[0][:], D, F32)])
                        P.barrier()
                        P.checkpoint("B", [("h1T", h1T[:, 0:16 * NT], 16 * NT, BF16)])

                    H1R = [f"h1T{kc}" for kc in range(16)]

                    def dense(wt, wkey, tiles, evac, banks=(0, 1), rhsT=None, rkeys=None, RN=NT, nk=16):
                        rt = h1T if rhsT is None else rhsT
                        rk = H1R if rkeys is None else rkeys
                        for ti, (c0, n) in enumerate(tiles):
                            bi = banks[dense.ctr % len(banks)]
                            dense.ctr += 1
                            for kc in range(nk):
                                P.op("pe", lambda e, kc=kc, c0=c0, n=n, bi=bi: e.matmul(
                                    pb[bi][:, 0:n], lhsT=wt[:, kc * 128:(kc + 1) * 128], rhs=rt[:, kc * RN + c0:kc * RN + c0 + n],
                                    start=(kc == 0), stop=(kc == nk - 1)), r=[wkey] + rk, w=[f"pb{bi}"])
                            evac(pb[bi], f"pb{bi}", c0, n)
                    dense.ctr = 0

                    with ExitStack() as ph3:
                        def sb3(name, shape, dtype=F32):
                            return ph3.enter_context(nc.sbuf_tensor(name, list(shape), dtype))
                        NQ = 1088 + TS
                        NK = 1600 + TS
                        wq = [sb3(f"wq{i}", [128, 2048], BF16) for i in range(2)]
                        wk = [sb3(f"wk{i}", [128, 2048], BF16) for i in range(2)]
                        wv = [sb3(f"wv{i}", [128, 2048], BF16) for i in range(2)]
                        zq = sb3("zq", [128, NQ])
                        zk = sb3("zk", [128, NK])
                        zv = sb3("zv", [128, NK])
                        sq = sb3("sq", [128, 512])
                        rs = sb3("rs", [128, 512])
                        sq_b = sb3("sq_b", [128, 512])
                        rs_b = sb3("rs_b", [128, 512])
                        qT = sb3("qT", [128, NQ], BF16)
                        kT = sb3("kT", [128, NK], BF16)
                        kTs = sb3("kTs", [128, 512], BF16)
                        vtok = sb3("vtok", [128, 25 * 64], BF16)
                        vstok = sb3("vstok", [128, 128], BF16)
                        ckf = [sb3(f"ckf{i}", [128, 512]) for i in range(2)]
                        cvb = [sb3(f"cvb{i}", [128, 512], BF16) for i in range(2)]
                        bT = [sb3(f"bT{i}", [128, 576]) for i in range(2)]
                        bS = [sb3(f"bS{i}", [128, 160]) for i in range(2)]
                        sbuf_s = [sb3(f"sbs{i}", [128, 576]) for i in range(2)]
                        pT = [sb3(f"pT{i}", [128, 576], BF16) for i in range(2)]
                        rden = sb3("rden", [128, 128])
                        bT2 = sb3("bT2", [128, 1280])
                        sb2 = sb3("sb2", [128, 1280])
                        pT2 = [sb3(f"pT2{i}", [128, 1280], BF16) for i in range(2)]
                        otok = sb3("otok", [128, 512])
                        otok2 = sb3("otok2", [128, 512])
                        kvs = sb3("kvs", [16, 256])


                        qtiles = [(960, 512), (1472, 512), (1984, 64), (2048, TS)]
                        ktiles = [(448, 512), (960, 512), (1472, 512), (1984, 64), (2048, TS)]

                        def qcol(c0):
                            return c0 - 960 if c0 < 2048 else 1088

                        def kcol(c0):
                            return c0 - 448 if c0 < 2048 else 1600

                        for p in range(8):
                            pi = p % 2
                            for (wt, nm, ch) in ((wq[pi], "wq", p), (wk[pi], "wk", 8 + p), (wv[pi], "wv", 16 + p)):
                                P.op("pool", lambda e, wt=wt, ch=ch: e.dma_start(out=wt[:], in_=w_in_t[ch]), w=[f"{nm}{pi}"], dma=True)
                            for blk in range(4):
                                P.op("sp", lambda e, blk=blk, p=p, pi=pi: e.dma_start(out=ckf[pi][:, blk * 128:(blk + 1) * 128], in_=cache_k[blk * 128:(blk + 1) * 128, p * 128:(p + 1) * 128]),
                                     w=[f"ckf{pi}"], dma=True)
                                P.op("pool", lambda e, blk=blk, p=p, pi=pi: e.dma_start(out=cvb[pi][:, blk * 128:(blk + 1) * 128], in_=cache_v[blk * 128:(blk + 1) * 128, p * 128:(p + 1) * 128]),
                                     w=[f"cvb{pi}"], dma=True)
                            P.op("sp", lambda e, p=p, pi=pi: e.dma_start(out=bT[pi][:], in_=biasT[p]), w=[f"bT{pi}"], dma=True)
                            P.op("sp", lambda e, p=p, pi=pi: e.dma_start(out=bS[pi][:, 0:80], in_=biasS[2 * p]), w=[f"bS{pi}"], dma=True)
                            P.op("sp", lambda e, p=p, pi=pi: e.dma_start(out=bS[pi][:, 80:160], in_=biasS[2 * p + 1]), w=[f"bS{pi}"], dma=True)

                            def ev_q(bank, bk, c0, n):
                                P.op("act", lambda e: e.activation(out=zq[:, qcol(c0):qcol(c0) + n], in_=bank[:, 0:n], func=AF.Copy), r=[bk], w=["zq"])
                            def ev_k(bank, bk, c0, n):
                                P.op("act", lambda e: e.activation(out=zk[:, kcol(c0):kcol(c0) + n], in_=bank[:, 0:n], func=AF.Copy), r=[bk], w=["zk"])
                            def ev_v(bank, bk, c0, n):
                                P.op("dve", lambda e: e.tensor_copy(out=zv[:, kcol(c0):kcol(c0) + n], in_=bank[:, 0:n]), r=[bk], w=["zv"])
                            dense(wq[pi], f"wq{pi}", qtiles, ev_q)
                            dense(wk[pi], f"wk{pi}", ktiles, ev_k)
                            dense(wv[pi], f"wv{pi}", ktiles, ev_v)

                            qk_tiles = []
                            for (z, zkey, ncols, gname, outb, okey, outf) in ((zq, "zq", NQ, "qg", qT, "qT", False), (zk, "zk", NK, "kg", kT, "kT", True)):
                                c0 = 0
                                while c0 < ncols:
                                    n = min(512, ncols - c0)
                                    qk_tiles.append((z, zkey, c0, n, gname, outb, okey, outf))
                                    c0 += n
                            sqs = [sq, sq_b]
                            rss = [rs, rs_b]

                            def qk_stageA(i):
                                z, zkey, c0, n, gname, outb, okey, outf = qk_tiles[i]
                                b = i % 2
                                P.op("act", lambda e: e.activation(out=sqs[b][:, 0:n], in_=z[:, c0:c0 + n], func=AF.Square), r=[zkey], w=[f"sq{b}"])
                                P.op("pe", lambda e: e.matmul(pb[6 + b][:, 0:n], lhsT=bdones, rhs=sqs[b][:, 0:n], start=True, stop=True), r=[f"sq{b}", "cst"], w=[f"pb{6+b}"])

                            def qk_stageB(i):
                                z, zkey, c0, n, gname, outb, okey, outf = qk_tiles[i]
                                b = i % 2
                                P.op("act", lambda e: e.activation(out=rss[b][:, 0:n], in_=pb[6 + b][:, 0:n], func=AF.Ln, scale=1.0 / 64, bias=RMS_EPS), r=[f"pb{6+b}"], w=[f"rs{b}"])
                                P.op("act", lambda e: e.activation(out=rss[b][:, 0:n], in_=rss[b][:, 0:n], func=AF.Exp, scale=-0.5), r=[f"rs{b}"], w=[f"rs{b}"])
                                if outf:
                                    P.op("dve", lambda e: e.scalar_tensor_tensor(out=z[:, c0:c0 + n], in0=z[:, c0:c0 + n], scalar=pcol(gname), in1=rss[b][:, 0:n], op0=ALU.mult, op1=ALU.mult),
                                         r=[zkey, f"rs{b}", "prm"], w=[zkey])
                                    P.op("act", lambda e: e.activation(out=outb[:, c0:c0 + n], in_=z[:, c0:c0 + n], func=AF.Copy), r=[zkey], w=[okey])
                                else:
                                    P.op("dve", lambda e: e.scalar_tensor_tensor(out=outb[:, c0:c0 + n], in0=z[:, c0:c0 + n], scalar=pcol(gname), in1=rss[b][:, 0:n], op0=ALU.mult, op1=ALU.mult),
                                         r=[zkey, f"rs{b}", "prm"], w=[okey])
                            qk_stageA(0)
                            for i in range(len(qk_tiles)):
                                if i + 1 < len(qk_tiles):
                                    qk_stageA(i + 1)
                                qk_stageB(i)

                            for (z, zkey, od, ods, stg, skey, so) in ((zk, "zk", k_out, ks_out, otok, "otok", 0), (zv, "zv", v_out, vs_out, otok2, "otok2", 128)):
                                for blk in range(4):
                                    c0 = kcol(1536 + blk * 128)
                                    P.op("pe", lambda e, z=z, c0=c0, blk=blk: e.matmul(pb[7][:, blk * 128:(blk + 1) * 128], lhsT=z[:, c0:c0 + 128], rhs=identf,
                                                                                   start=True, stop=True), r=[zkey, "cst"], w=["pb7"])
                                P.op("dve", lambda e, stg=stg: e.tensor_copy(out=stg[:, 0:512], in_=pb[7][:, 0:512]), r=["pb7"], w=[skey])
                                for blk in range(4):
                                    P.op("sp", lambda e, od=od, stg=stg, blk=blk, p=p: e.dma_start(out=od[blk * 128:(blk + 1) * 128, p * 128:(p + 1) * 128],
                                                                                           in_=stg[:, blk * 128:(blk + 1) * 128]), r=[skey], dma=True, out=True)
                                P.op("pe", lambda e, z=z: e.matmul(pb[7][0:TS, 0:128], lhsT=z[:, 1600:1600 + TS], rhs=identf, start=True, stop=True),
                                     r=[zkey, "cst"], w=["pb7"])
                                P.op("dve", lambda e, so=so: e.tensor_copy(out=kvs[:, so:so + 128], in_=pb[7][0:TS, 0:128]), r=["pb7"], w=["kvs"])
                                P.op("sp", lambda e, ods=ods, so=so, p=p: e.dma_start(out=ods[:, p * 128:(p + 1) * 128], in_=kvs[:, so:so + 128]), r=["kvs"], dma=True, out=True)
                            P.op("act", lambda e: e.activation(out=vstok[0:TS, :], in_=kvs[:, 128:256], func=AF.Copy), r=["kvs"], w=["vstok"])

                            for hh in range(2):
                                pbs = hh * 64
                                bi = 6 + hh
                                for c0 in range(0, 25, 8):
                                    nch = min(8, 25 - c0)
                                    for ci in range(nch):
                                        col = (c0 + ci) * 64
                                        P.op("pe", lambda e, pbs=pbs, bi=bi, ci=ci, col=col: e.matmul(
                                            pb[bi][pbs:pbs + 64, ci * 64:(ci + 1) * 64], lhsT=zv[pbs:pbs + 64, col:col + 64],
                                            rhs=identf[pbs:pbs + 64, pbs:pbs + 64], start=True, stop=True), r=["zv", "cst"], w=[f"pb{bi}"])
                                    P.op("act" if hh == 0 else "dve",
                                         (lambda e, pbs=pbs, bi=bi, c0=c0, nch=nch: e.activation(out=vtok[pbs:pbs + 64, c0 * 64:(c0 + nch) * 64],
                                                                                              in_=pb[bi][pbs:pbs + 64, 0:nch * 64], func=AF.Copy)) if hh == 0 else
                                         (lambda e, pbs=pbs, bi=bi, c0=c0, nch=nch: e.tensor_copy(out=vtok[pbs:pbs + 64, c0 * 64:(c0 + nch) * 64],
                                                                                               in_=pb[bi][pbs:pbs + 64, 0:nch * 64])),
                                         r=[f"pb{bi}"], w=["vtok"])
                            for blk in range(4):
                                P.op("pe", lambda e, blk=blk, pi=pi: e.matmul(pb[6][:, blk * 128:(blk + 1) * 128],
                                                                          lhsT=ckf[pi][:, blk * 128:(blk + 1) * 128], rhs=identf, start=True, stop=True),
                                     r=[f"ckf{pi}", "cst"], w=["pb6"])
                            P.op("act", lambda e: e.activation(out=kTs[:], in_=pb[6][:, 0:512], func=AF.Copy), r=["pb6"], w=["kTs"])

                            def att_scores(c, par):
                                b0, b1 = (2, 3) if par == 0 else (4, 5)
                                qc = qcol(c * 64)
                                for jj in range(9):
                                    kc_ = kcol((c - 8 + jj) * 64)
                                    bi_ = b0 if jj < 8 else b1
                                    co = (jj % 8) * 64
                                    for hh in range(2):
                                        sl = slice(hh * 64, hh * 64 + 64)
                                        P.op("pe", lambda e: e.matmul(pb[bi_][sl, co:co + 64], lhsT=kT[sl, kc_:kc_ + 64], rhs=qT[sl, qc:qc + 64], start=True, stop=True),
                                             r=["kT", "qT"], w=[f"pb{bi_}"])
                                sbk = f"sbs{par}"
                                P.op("dve", lambda e: e.scalar_tensor_tensor(out=sbuf_s[par][:, 0:512], in0=pb[b0][:, 0:512], scalar=0.125, in1=bT[pi][:, 0:512], op0=ALU.mult, op1=ALU.add),
                                     r=[f"pb{b0}", f"bT{pi}"], w=[sbk])
                                P.op("dve", lambda e: e.scalar_tensor_tensor(out=sbuf_s[par][:, 512:576], in0=pb[b1][:, 0:64], scalar=0.125, in1=bT[pi][:, 512:576], op0=ALU.mult, op1=ALU.add),
                                     r=[f"pb{b1}", f"bT{pi}"], w=[sbk])
                                nm = min(8, max(0, 24 - c))
                                if nm > 0:
                                    P.op("act", lambda e: e.activation(out=pT[par][:, 0:nm * 64], in_=sbuf_s[par][:, 0:nm * 64], func=AF.Exp, bias=prm[:, PC["negm"]:PC["negm"] + 1]),
                                         r=[sbk, "prm"], w=[f"pT{par}"])
                                P.op("act", lambda e: e.activation(out=pT[par][:, nm * 64:576], in_=sbuf_s[par][:, nm * 64:576], func=AF.Exp), r=[sbk], w=[f"pT{par}"])

                            def att_pv(c, par):
                                bo = 6 + par
                                for jj in range(9):
                                    vc = (c - 8 + jj - 7) * 64
                                    for hh in range(2):
                                        sl = slice(hh * 64, hh * 64 + 64)
                                        P.op("pe", lambda e: e.matmul(pb[bo][sl, 0:64], lhsT=vtok[sl, vc:vc + 64], rhs=pT[par][sl, jj * 64:(jj + 1) * 64], start=(jj == 0), stop=(jj == 8)),
                                             r=["vtok", f"pT{par}"], w=[f"pb{bo}"])
                                for jj in range(9):
                                    for hh in range(2):
                                        sl = slice(hh * 64, hh * 64 + 64)
                                        P.op("pe", lambda e: e.matmul(pb[bo][sl, 64:128], lhsT=onesb[sl, 0:64], rhs=pT[par][sl, jj * 64:(jj + 1) * 64], start=(jj == 0), stop=(jj == 8)),
                                             r=["onesb", f"pT{par}"], w=[f"pb{bo}"])
                                P.op("dve", lambda e: e.reciprocal(out=rden[:, 0:64], in_=pb[bo][:, 64:128]), r=[f"pb{bo}"], w=["rden"])
                                mc = p * NMX + mixcol(c * 64)
                                P.op("dve", lambda e: e.tensor_tensor(out=mixT[:, mc:mc + 64], in0=pb[bo][:, 0:64], in1=rden[:, 0:64], op=ALU.mult),
                                     r=[f"pb{bo}", "rden"], w=[f"mixT{p}"])

                            def v3_(ap, w):
                                return ap.rearrange("p (u w) -> p u w", w=w)
                            P.op("pool", lambda e: e.memset(bT2[:, :], -30000.0), w=["bT2"])
                            P.op("pool", lambda e: e.tensor_copy(out=v3_(bT2[:, :], 128)[:, 0:9, 0:64], in_=v3_(bT[pi][:, 0:576], 64)), r=[f"bT{pi}"], w=["bT2"])
                            P.op("pool", lambda e: e.tensor_copy(out=v3_(bT2[:, :], 128)[:, 1:10, 64:128], in_=v3_(bT[pi][:, 0:576], 64)), r=[f"bT{pi}"], w=["bT2"])

                            def att2_scores(c, par):
                                banks = (2, 3, 4) if par == 0 else (5, 6, 7)
                                qc = qcol(c * 64)
                                for u in range(10):
                                    kc_ = kcol((c - 8 + u) * 64)
                                    bi_ = banks[u // 4]
                                    co = (u % 4) * 128
                                    for hh in range(2):
                                        sl = slice(hh * 64, hh * 64 + 64)
                                        P.op("pe", lambda e: e.matmul(pb[bi_][sl, co:co + 128], lhsT=kT[sl, kc_:kc_ + 64], rhs=qT[sl, qc:qc + 128], start=True, stop=True),
                                             r=["kT", "qT"], w=[f"pb{bi_}"])
                                for bi3, (c0_, n_) in enumerate(((0, 512), (512, 512), (1024, 256))):
                                    bb = banks[bi3]
                                    P.op("dve", lambda e: e.scalar_tensor_tensor(out=sb2[:, c0_:c0_ + n_], in0=pb[bb][:, 0:n_], scalar=0.125, in1=bT2[:, c0_:c0_ + n_], op0=ALU.mult, op1=ALU.add),
                                         r=[f"pb{bb}", "bT2"], w=["sb2"])
                                nmA = min(8, max(0, 24 - c))
                                nmB = min(8, max(0, 23 - c)) + 1 if c + 1 < 24 else 0
                                negm = prm[:, PC["negm"]:PC["negm"] + 1]
                                s3 = v3_(sb2[:, :], 128)
                                p3 = v3_(pT2[par][:, :], 128)
                                for (u0, u1, h0, masked) in ((0, nmA, 0, True), (nmA, 10, 0, False), (0, nmB, 64, True), (nmB, 10, 64, False)):
                                    if u1 <= u0:
                                        continue
                                    if masked:
                                        P.op("act", lambda e: e.activation(out=p3[:, u0:u1, h0:h0 + 64], in_=s3[:, u0:u1, h0:h0 + 64], func=AF.Exp, bias=negm), r=["sb2", "prm"], w=[f"pT2{par}"])
                                    else:
                                        P.op("act", lambda e: e.activation(out=p3[:, u0:u1, h0:h0 + 64], in_=s3[:, u0:u1, h0:h0 + 64], func=AF.Exp), r=["sb2"], w=[f"pT2{par}"])

                            def att2_pv(c, par):
                                bo = par
                                for u in range(10):
                                    vc = (c - 8 + u - 7) * 64
                                    for hh in range(2):
                                        sl = slice(hh * 64, hh * 64 + 64)
                                        P.op("pe", lambda e: e.matmul(pb[bo][sl, 0:128], lhsT=vtok[sl, vc:vc + 64], rhs=pT2[par][sl, u * 128:(u + 1) * 128], start=(u == 0), stop=(u == 9)),
                                             r=["vtok", f"pT2{par}"], w=[f"pb{bo}"])
                                for u in range(10):
                                    for hh in range(2):
                                        sl = slice(hh * 64, hh * 64 + 64)
                                        P.op("pe", lambda e: e.matmul(pb[bo][sl, 128:256], lhsT=onesb[sl, 0:64], rhs=pT2[par][sl, u * 128:(u + 1) * 128], start=(u == 0), stop=(u == 9)),
                                             r=["onesb", f"pT2{par}"], w=[f"pb{bo}"])
                                P.op("dve", lambda e: e.reciprocal(out=rden[:, 0:128], in_=pb[bo][:, 128:256]), r=[f"pb{bo}"], w=["rden"])
                                mc = p * NMX + mixcol(c * 64)
                                P.op("dve", lambda e: e.tensor_tensor(out=mixT[:, mc:mc + 128], in0=pb[bo][:, 0:128], in1=rden[:, 0:128], op=ALU.mult),
                                     r=[f"pb{bo}", "rden"], w=[f"mixT{p}"])

                            dlist = list(range(15, 31, 2))
                            for ci, c in enumerate(dlist):
                                att2_scores(c, ci % 2)
                                if ci > 0:
                                    att2_pv(dlist[ci - 1], (ci - 1) % 2)
                            att_scores(31, 0)
                            att2_pv(dlist[-1], (len(dlist) - 1) % 2)
                            att_pv(31, 0)

                            for hh in range(2):
                                pbs = hh * 64
                                sl = slice(pbs, pbs + 64)
                                b0, bo = 2 + 2 * hh, 6 + hh
                                qc = 1088
                                for blk in range(4):
                                    P.op("pe", lambda e, sl=sl, b0=b0, blk=blk: e.matmul(pb[b0][:, blk * 16:(blk + 1) * 16], lhsT=kTs[sl, blk * 128:(blk + 1) * 128],
                                                                                      rhs=qT[sl, qc:qc + TS], start=True, stop=True), r=["kTs", "qT"], w=[f"pb{b0}"])
                                P.op("pe", lambda e, sl=sl, b0=b0: e.matmul(pb[b0][0:TS, 64:80], lhsT=kT[sl, 1600:1600 + TS], rhs=qT[sl, qc:qc + TS], start=True, stop=True),
                                     r=["kT", "qT"], w=[f"pb{b0}"])
                                sbk = f"sbs{hh}"
                                P.op("dve", lambda e, b0=b0, hh=hh, pi=pi: e.scalar_tensor_tensor(
                                    out=sbuf_s[hh][:, 0:80], in0=pb[b0][:, 0:80], scalar=0.125, in1=bS[pi][:, hh * 80:(hh + 1) * 80], op0=ALU.mult, op1=ALU.add),
                                    r=[f"pb{b0}", f"bS{pi}"], w=[sbk])
                                P.op("dve", lambda e, hh=hh: e.memset(pT[hh][:, 0:80], 0.0), w=[f"pT{hh}"])
                                P.op("act", lambda e, hh=hh: e.activation(out=pT[hh][:, 0:64], in_=sbuf_s[hh][:, 0:64], func=AF.Exp), r=[sbk], w=[f"pT{hh}"])
                                P.op("act", lambda e, hh=hh: e.activation(out=pT[hh][0:TS, 64:80], in_=sbuf_s[hh][0:TS, 64:80], func=AF.Exp), r=[sbk], w=[f"pT{hh}"])
                                for blk in range(5):
                                    if blk < 4:
                                        lhs = cvb[pi][:, blk * 128:(blk + 1) * 128]
                                        rhs = pT[hh][:, blk * 16:(blk + 1) * 16]
                                        lo = onesb[:, :]
                                    else:
                                        lhs = vstok[0:TS, :]
                                        rhs = pT[hh][0:TS, 64:80]
                                        lo = onesb[0:TS, :]
                                    P.op("pe", lambda e, lhs=lhs, rhs=rhs, bo=bo, blk=blk: e.matmul(pb[bo][:, 0:TS], lhsT=lhs, rhs=rhs, start=(blk == 0), stop=(blk == 4)),
                                         r=[f"cvb{pi}", "vstok", f"pT{hh}"], w=[f"pb{bo}"])
                                for blk in range(5):
                                    if blk < 4:
                                        rhs = pT[hh][:, blk * 16:(blk + 1) * 16]
                                        lo = onesb[:, :]
                                    else:
                                        rhs = pT[hh][0:TS, 64:80]
                                        lo = onesb[0:TS, :]
                                    P.op("pe", lambda e, lo=lo, rhs=rhs, bo=bo, blk=blk: e.matmul(pb[bo][:, 64:64 + TS], lhsT=lo, rhs=rhs, start=(blk == 0), stop=(blk == 4)),
                                         r=["onesb", f"pT{hh}"], w=[f"pb{bo}"])
                                P.op("dve", lambda e, sl=sl, bo=bo: e.reciprocal(out=rden[sl, 0:TS], in_=pb[bo][sl, 64:64 + TS]), r=[f"pb{bo}"], w=["rden"])
                                mc = p * NMX + 1088
                                P.op("dve", lambda e, sl=sl, bo=bo, mc=mc: e.tensor_tensor(out=mixT[sl, mc:mc + TS], in0=pb[bo][sl, 0:TS], in1=rden[sl, 0:TS], op=ALU.mult),
                                     r=[f"pb{bo}", "rden"], w=[f"mixT{p}"])
                        P.barrier()
                        P.checkpoint("C", [("mixT", mixT[:], 16 * NMX, BF16)])

                    rwkv_phase(nc, P, es, pb, dense, h1T, H1R, mixT, NMX, prm, pcol, cst, identf, bdones, m_su, m_ui, m_sl, identb,
                               w_in_t, w2a2, g2d, state0, S_out, Ss_out, shcol, sh_out)
                    P.barrier()
                    P.checkpoint("D", [("mixT", mixT[:], 16 * NMX, BF16)])

                ffn_phase(nc, P, es, pb, Rr, mixT, NMX, prm, pcol, modT, mod, F1, identf, cst, xw, w_out_t, w_up_t, w_down_t,
                          y_own, y_s, cvcol, cv_out)

        except StopBuild:
            pass
        P.finish()
        sems = {}
        for sk in P.sem_keys():
            sems[sk] = es.enter_context(nc.semaphore("s_" + "_".join(str(x) for x in sk)))
        P.replay(nc, sems)
    return nc


def rwkv_phase(nc, P, es, pb, dense, h1T, H1R, mixT, NMX, prm, pcol, cst, identf, bdones, m_su, m_ui, m_sl, identb,
               w_in_t, w2a2, g2d, state0, S_out, Ss_out, shcol, sh_out):
    with ExitStack() as ph:
        def sb(name, shape, dtype=F32):
            return ph.enter_context(nc.sbuf_tensor(name, list(shape), dtype))
        flag = pcol("flag")
        twd_ad = sb("twd_ad", [128, NT], BF16)
        sgd = sb("sgd", [128, NT], BF16)
        w2a2b = sb("w2a2b", [128, 1024], BF16)
        g2b = sb("g2b", [128, 1024], BF16)
        P.op("pool", lambda e: e.dma_start(out=w2a2b[:], in_=w2a2[:, :]), w=["w2a2b"], dma=True)
        P.op("pool", lambda e: e.dma_start(out=g2b[:], in_=g2d[:, :]), w=["g2b"], dma=True)
        wr = [sb("wr0", [128, 2048], BF16)]
        wk_ = [sb("wkk0", [128, 2048], BF16)]
        wv_ = [sb("wvv0", [128, 2048], BF16)]
        wl = [wr[0], wk_[0]]
        wlk = ["wr0", "wkk0"]
        zb = sb("zb", [128, 1 + 512])
        zm = sb("zm", [128, 512])
        rmask = sb("rmask", [128, 512], BF16)
        P.op("pool", lambda e: e.memset(rmask[:], 1.0), w=["rmask"])
        P.op("pool", lambda e: e.memset(rmask[:].rearrange("p (c l) -> p c l", l=64)[:, :, 0:1], 0.0), w=["rmask"])

        tiles = [(0, 512), (512, 512), (1024, 512), (1536, 512), (2048, TS)]

        zcar = sb("zcar", [128, 4])

        def shifted(bank, bk, ch, c0, n, dst_fn, cs=3):
            first = c0 == 0
            samp = c0 == 2048
            car = zcar[:, cs:cs + 1]
            if first:
                P.op("dve", lambda e: e.memset(zb[:, 0:1], 0.0), w=["zb"])
            elif samp:
                P.op("dve", lambda e: e.tensor_copy(out=zb[:, 0:1], in_=prm[:, PC["shs"] + ch:PC["shs"] + ch + 1]), r=["prm"], w=["zb"])
            elif c0 == OWN0:
                P.op("dve", lambda e: e.tensor_scalar(out=zb[:, 0:1], in0=car, scalar1=flag, scalar2=None, op0=ALU.mult), r=["zcar", "prm"], w=["zb"])
            else:
                P.op("dve", lambda e: e.tensor_copy(out=zb[:, 0:1], in_=car), r=["zcar"], w=["zb"])
            P.op("act", lambda e: e.activation(out=zb[:, 1:1 + n], in_=bank[:, 0:n], func=AF.Copy), r=[bk], w=["zb"])
            P.op("dve", lambda e: e.tensor_copy(out=car, in_=zb[:, n:n + 1]), r=["zb"], w=["zcar"])
            if c0 == 1536 or samp:
                col = ch if not samp else 26 + ch
                P.op("dve", lambda e: e.tensor_copy(out=shcol[:, col:col + 1], in_=zb[:, n:n + 1]), r=["zb"], w=["shcol"])
            P.op("dve", lambda e: e.tensor_scalar(out=zm[:, 0:n], in0=zb[:, 0:n], scalar1=pcol("mu", ch), scalar2=None, op0=ALU.mult), r=["zb", "prm"], w=["zm"])
            P.op("dve", lambda e: e.scalar_tensor_tensor(out=zm[:, 0:n], in0=zb[:, 1:1 + n], scalar=pcol("omu", ch), in1=zm[:, 0:n], op0=ALU.mult, op1=ALU.add),
                 r=["zb", "zm", "prm"], w=["zm"])
            dst_fn(n)

        for li, ch in enumerate((24, 25)):
            P.op("pool", lambda e, li=li, ch=ch: e.dma_start(out=wl[li][:], in_=w_in_t[24 + ch]), w=[wlk[li]], dma=True)

            def ev(bank, bk, c0, n, ch=ch):
                def dst(n, c0=c0, ch=ch):
                    if ch == 24:
                        P.op("act", lambda e: e.activation(out=twd_ad[0:64, c0:c0 + n], in_=zm[0:64, 0:n], func=AF.Tanh), r=["zm"], w=["twd_ad"])
                        P.op("dve", lambda e: e.tensor_copy(out=twd_ad[64:128, c0:c0 + n], in_=zm[64:128, 0:n]), r=["zm"], w=["twd_ad"])
                    else:
                        P.op("act", lambda e: e.activation(out=sgd[:, c0:c0 + n], in_=zm[:, 0:n], func=AF.Sigmoid), r=["zm"], w=["sgd"])
                shifted(bank, bk, ch, c0, n, dst)
            dense(wl[li], wlk[li], tiles, ev)

        def t32(name, n=512):
            return sb(name, [128, n])

        def t16(name, n=512):
            return sb(name, [128, n], BF16)
        RKV = [(t32("r_f0"), t32("k_f0"), t32("v_f0"), t16("g_f0")), (t32("r_f1"), t32("k_f1"), t32("v_f1"), t16("g_f1"))]
        ptmp1, ptmp2, ptmp3 = t32("ptmp1"), t32("ptmp2"), t32("ptmp3")
        a_f, ld_f, cl_f, kkn, tmp1, tmp2, tmp3 = t32("a_f"), t32("ld_f"), t32("cl_f"), t32("kkn"), t32("tmp1"), t32("tmp2"), t32("tmp3")
        at_b, rp_b, bp_b, kp_b, bpp_b, kpp_b, v_b = (t16(n) for n in ("at_b", "rp_b", "bp_b", "kp_b", "bpp_b", "kpp_b", "v_b"))
        gL = sb("gL", [128, 8])
        BppT, KppT, Vt = (t16(n) for n in ("BppT", "KppT", "Vt"))
        NabT, NakT, ArbT, ArkT, Nab = (t16(n) for n in ("NabT", "NakT", "ArbT", "ArkT", "Nab"))
        Pq = [t16("Pq0"), t16("Pq1")]
        PqT = [t16("PqT0"), t16("PqT1")]
        Wb = sb("Wb", [128, 1024], BF16)
        QeffT = sb("QeffT", [128, 512])
        YlocT = sb("YlocT", [128, 512])
        McT = sb("McT", [128, 512])
        Dc = sb("Dc", [128, 512])
        diagG = sb("diagG", [128, 512])
        ST = [sb("ST0", [128, 64]), sb("ST1", [128, 64])]
        yT = t32("yT")
        sstg = sb("sstg", [64, 128])
        s0ins = [sb("s0in0", [64, 128]), sb("s0in1", [64, 128])]

        curh = [0]
        tile_list = []

        def run2(a, b):
            gens = [g_ for g_ in (a, b) if g_ is not None]
            while gens:
                for g_ in list(gens):
                    try:
                        next(g_)
                    except StopIteration:
                        gens.remove(g_)

        def make_tile(p, pi, ti, c0, n, sx):
            r_f, k_f, v_f, g_f = RKV[sx]
            s0in = s0ins[p % 2]
            if True:
                samp = c0 == 2048
                L = TS if samp else 64
                G = 1 if samp else 8
                want_y = samp or c0 >= 512

                def skey(g):
                    return "0" if (G == 1 or g < G // 2) else "1"

                def prep_gen():
                    if ti == 0:
                        for (wt, nm, ch) in ((wr[pi], "wr", 24 + p), (wk_[pi], "wkk", 32 + p), (wv_[pi], "wvv", 40 + p)):
                            P.op("pool", lambda e, wt=wt, ch=ch: e.dma_start(out=wt[:], in_=w_in_t[ch]), w=[f"{nm}{pi}"], dma=True)
                        P.op("sp", lambda e: e.dma_start(out=s0ins[p % 2][:], in_=state0[:, p * 128:(p + 1) * 128]), w=[f"s0in{p%2}"], dma=True)
                    for csl, (wt, nm, mch, dstt, dk) in enumerate(((wr[pi], "wr", p, r_f, f"r_f{sx}"), (wk_[pi], "wkk", 8 + p, k_f, f"k_f{sx}"), (wv_[pi], "wvv", 16 + p, v_f, f"v_f{sx}"))):
                        def ev(bank, bk, c0_, n_, mch=mch, dstt=dstt, dk=dk, csl=csl):
                            def dst(n__):
                                P.op("act", lambda e: e.activation(out=dstt[:, 0:n__], in_=zm[:, 0:n__], func=AF.Copy), r=["zm"], w=[dk])
                            shifted(bank, bk, mch, c0_, n_, dst, cs=csl)
                        dense(wt, f"{nm}{pi}", [(c0, n)], ev, banks=(6,))
                    P.op("pe", lambda e, c0=c0, n=n, p=p: e.matmul(pb[6][:, 0:n], lhsT=w2a2b[0:64, p * 128:(p + 1) * 128], rhs=twd_ad[0:64, c0:c0 + n], start=True, stop=True),
                         r=["w2a2b", "twd_ad"], w=["pb6"])
                    P.op("pe", lambda e, c0=c0, n=n, p=p: e.matmul(pb[7][:, 0:n], lhsT=w2a2b[64:128, p * 128:(p + 1) * 128], rhs=twd_ad[64:128, c0:c0 + n], start=True, stop=True),
                         r=["w2a2b", "twd_ad"], w=["pb7"])
                    yield
                    P.op("act", lambda e, n=n, p=p: e.activation(out=ld_f[:, 0:n], in_=pb[6][:, 0:n], func=AF.Sigmoid, bias=pcol("w0", p)), r=["pb6", "prm"], w=["ld_f"])
                    P.op("pool", lambda e, n=n: e.tensor_scalar(out=ld_f[:, 0:n], in0=ld_f[:, 0:n], scalar1=-float(np.exp(-0.5)), scalar2=None, op0=ALU.mult), r=["ld_f"], w=["ld_f"])
                    P.op("act", lambda e, n=n, p=p: e.activation(out=a_f[:, 0:n], in_=pb[7][:, 0:n], func=AF.Sigmoid, bias=pcol("a0", p)), r=["pb7", "prm"], w=["a_f"])
                    if want_y:
                        P.op("pe", lambda e, c0=c0, n=n, p=p: e.matmul(pb[6][:, 0:n], lhsT=g2b[:, p * 128:(p + 1) * 128], rhs=sgd[:, c0:c0 + n], start=True, stop=True),
                             r=["g2b", "sgd"], w=["pb6"])
                        P.op("act", lambda e, n=n: e.activation(out=g_f[:, 0:n], in_=pb[6][:, 0:n], func=AF.Copy), r=["pb6"], w=[f"g_f{sx}"])
                    yield
                    P.op("act", lambda e, n=n, p=p: e.activation(out=kkn[:, 0:n], in_=k_f[:, 0:n], func=AF.Identity, scale=pcol("kk", p)), r=[f"k_f{sx}", "prm"], w=["kkn"])
                    P.op("act", lambda e, n=n: e.activation(out=tmp1[:, 0:n], in_=kkn[:, 0:n], func=AF.Square), r=["kkn"], w=["tmp1"])
                    P.op("pe", lambda e, n=n: e.matmul(pb[7][:, 0:n], lhsT=bdones, rhs=tmp1[:, 0:n], start=True, stop=True), r=["tmp1", "cst"], w=["pb7"])
                    yield
                    P.op("dve", lambda e, n=n: e.tensor_scalar(out=tmp1[:, 0:n], in0=pb[7][:, 0:n], scalar1=1e-18, scalar2=None, op0=ALU.max), r=["pb7"], w=["tmp1"])
                    P.op("act", lambda e, n=n: e.activation(out=tmp1[:, 0:n], in_=tmp1[:, 0:n], func=AF.Ln), r=["tmp1"], w=["tmp1"])
                    P.op("act", lambda e, n=n: e.activation(out=tmp1[:, 0:n], in_=tmp1[:, 0:n], func=AF.Exp, scale=-0.5), r=["tmp1"], w=["tmp1"])
                    yield
                    P.op("pool", lambda e, n=n: e.tensor_tensor(out=kkn[:, 0:n], in0=kkn[:, 0:n], in1=tmp1[:, 0:n], op=ALU.mult), r=["kkn", "tmp1"], w=["kkn"])
                    P.op("act", lambda e, n=n, p=p: e.activation(out=tmp1[:, 0:n], in_=a_f[:, 0:n], func=AF.Identity, scale=pcol("ka", p), bias=pcol("omka", p)),
                         r=["a_f", "prm"], w=["tmp1"])
                    P.op("pool", lambda e, n=n: e.tensor_tensor(out=k_f[:, 0:n], in0=k_f[:, 0:n], in1=tmp1[:, 0:n], op=ALU.mult), r=[f"k_f{sx}", "tmp1"], w=[f"k_f{sx}"])
                    yield
                    P.op("pool", lambda e, n=n: e.tensor_tensor(out=tmp2[:, 0:n], in0=kkn[:, 0:n], in1=a_f[:, 0:n], op=ALU.mult), r=["kkn", "a_f"], w=["tmp2"])
                    if samp:
                        P.op("dve", lambda e, n=n: e.tensor_tensor_scan(out=cl_f[:, 0:n], data0=rmask[:, 0:n], data1=ld_f[:, 0:n], initial=0.0, op0=ALU.mult, op1=ALU.add),
                             r=["rmask", "ld_f"], w=["cl_f"])
                    else:
                        P.op("dve", lambda e, n=n: e.tensor_tensor_scan(out=cl_f[:, 0:n], data0=rmask[:, 0:n], data1=ld_f[:, 0:n], initial=0.0, op0=ALU.mult, op1=ALU.add),
                             r=["rmask", "ld_f"], w=["cl_f"])
                    clv = cl_f[:, 0:n].rearrange("p (g l) -> p g l", l=L)
                    clL = clv[:, :, L - 1:L].to_broadcast([128, G, L])
                    P.op("act", lambda e, n=n: e.activation(out=tmp1[:, 0:n], in_=cl_f[:, 0:n], func=AF.Exp), r=["cl_f"], w=["tmp1"])
                    P.op("pool", lambda e, n=n: e.tensor_tensor(out=tmp3[:, 0:n], in0=cl_f[:, 0:n], in1=ld_f[:, 0:n], op=ALU.subtract), r=["cl_f", "ld_f"], w=["tmp3"])
                    yield
                    P.op("act", lambda e, n=n: e.activation(out=tmp3[:, 0:n], in_=tmp3[:, 0:n], func=AF.Exp), r=["tmp3"], w=["tmp3"])
                    P.op("act", lambda e, n=n: e.activation(out=ld_f[:, 0:n], in_=cl_f[:, 0:n], func=AF.Exp, scale=-1.0), r=["cl_f", "tmp3"], w=["ld_f"])
                    P.op("pool", lambda e, n=n, clv=clv, clL=clL, L=L: e.tensor_tensor(out=a_f[:, 0:n].rearrange("p (g l) -> p g l", l=L), in0=clL, in1=clv, op=ALU.subtract),
                         r=["cl_f", "tmp2", f"k_f{sx}"], w=["a_f"])
                    yield
                    P.op("act", lambda e, n=n: e.activation(out=a_f[:, 0:n], in_=a_f[:, 0:n], func=AF.Exp), r=["a_f"], w=["a_f"])
                    yield

                def prep2_gen():
                    clv = cl_f[:, 0:n].rearrange("p (g l) -> p g l", l=L)
                    P.op("act", lambda e, clv=clv, G=G, L=L: e.activation(out=gL[:, 0:G], in_=clv[:, :, L - 1:L].rearrange("p g o -> p (g o)"), func=AF.Exp), r=["cl_f"], w=["gL"])
                    P.op("pool", lambda e, n=n: e.tensor_tensor(out=rp_b[:, 0:n], in0=r_f[:, 0:n], in1=tmp1[:, 0:n], op=ALU.mult), r=[f"r_f{sx}", "tmp1"], w=["rp_b"])
                    P.op("dve", lambda e, n=n: e.scalar_tensor_tensor(out=at_b[:, 0:n], in0=kkn[:, 0:n], scalar=-1.0, in1=tmp3[:, 0:n], op0=ALU.mult, op1=ALU.mult),
                         r=["kkn", "tmp3"], w=["at_b"])
                    P.op("pool", lambda e, n=n: e.tensor_tensor(out=bp_b[:, 0:n], in0=tmp2[:, 0:n], in1=ld_f[:, 0:n], op=ALU.mult), r=["tmp2", "ld_f"], w=["bp_b"])
                    P.op("pool", lambda e, n=n: e.tensor_tensor(out=kp_b[:, 0:n], in0=k_f[:, 0:n], in1=ld_f[:, 0:n], op=ALU.mult), r=[f"k_f{sx}", "ld_f"], w=["kp_b"])
                    P.op("dve", lambda e, n=n: e.tensor_tensor(out=bpp_b[:, 0:n], in0=tmp2[:, 0:n], in1=a_f[:, 0:n], op=ALU.mult), r=["tmp2", "a_f"], w=["bpp_b"])
                    P.op("pool", lambda e, n=n: e.tensor_tensor(out=kpp_b[:, 0:n], in0=k_f[:, 0:n], in1=a_f[:, 0:n], op=ALU.mult), r=[f"k_f{sx}", "a_f"], w=["kpp_b"])
                    P.op("act", lambda e, n=n: e.activation(out=v_b[:, 0:n], in_=v_f[:, 0:n], func=AF.Copy), r=[f"v_f{sx}"], w=["v_b"])
                    for hh in range(2):
                        sl = slice(hh * 64, hh * 64 + 64)
                        P.op("pool", lambda e, sl=sl, G=G, hh=hh: e.tensor_tensor(
                            out=diagG[sl, 0:G * 64].rearrange("p (g l) -> p g l", l=64),
                            in0=identf[sl, hh * 64:hh * 64 + 64].unsqueeze(1).to_broadcast([64, G, 64]),
                            in1=gL[sl, 0:G].unsqueeze(2).to_broadcast([64, G, 64]), op=ALU.mult), r=["cst", "gL"], w=["diagG"])
                    yield

                def alg_run():
                    nlev = int(np.log2(L))

                    def alg(s_, g0, g1):
                        Gs = g1 - g0
                        cols = slice(g0 * 64, g1 * 64)
                        Bk = [pb[3 * s_ + (i % 3)] for i in range(4)]
                        BK = [f"pb{3*s_+(i%3)}" for i in range(4)]
                        S = str(s_)

                        def v3(ap, w=64):
                            return ap.rearrange("p (g l) -> p g l", l=w)

                        def each(fn):
                            for g in range(g0, g1):
                                for hh in range(2):
                                    pbs = hh * 64
                                    fn(g, g - g0, slice(pbs, pbs + 64), slice(pbs, pbs + L), pbs)

                        for (src, skey, dstt, dkey, bi) in ((bpp_b, "bpp_b", BppT, "BppT", 1), (kpp_b, "kpp_b", KppT, "KppT", 2), (v_b, "v_b", Vt, "Vt", 3)):
                            each(lambda g, gl, sl, slL, pbs: P.op("pe", lambda e: e.matmul(Bk[bi][slL, gl * 64:(gl + 1) * 64], lhsT=src[sl, g * L:(g + 1) * L], rhs=identb[sl, pbs:pbs + 64],
                                                                                          start=True, stop=True), r=[skey, "identb"], w=[BK[bi]]))
                            P.op("dve", lambda e: e.tensor_copy(out=dstt[:, cols], in_=Bk[bi][:, 0:Gs * 64]), r=[BK[bi]], w=[dkey + S])
                        yield
                        for (lt, ltk, rt_, rtk, dstt, dkey, msk, bi) in (
                                (bp_b, "bp_b", at_b, "at_b", NabT, "NabT", m_su, 0), (kp_b, "kp_b", at_b, "at_b", NakT, "NakT", m_su, 1),
                                (bp_b, "bp_b", rp_b, "rp_b", ArbT, "ArbT", m_ui, 2), (kp_b, "kp_b", rp_b, "rp_b", ArkT, "ArkT", m_ui, 3),
                                (at_b, "at_b", bp_b, "bp_b", Nab, "Nab", m_sl, 0)):
                            each(lambda g, gl, sl, slL, pbs: P.op("pe", lambda e: e.matmul(Bk[bi][slL, gl * 64:gl * 64 + L], lhsT=lt[sl, g * L:(g + 1) * L], rhs=rt_[sl, g * L:(g + 1) * L],
                                                                                          start=True, stop=True), r=[ltk, rtk], w=[BK[bi]]))
                            P.op("dve", lambda e: e.tensor_tensor(out=v3(dstt[:, cols])[:, :, 0:L], in0=v3(Bk[bi][:, 0:Gs * 64])[:, :, 0:L], in1=v3(msk[:, 0:Gs * 64])[:, :, 0:L], op=ALU.mult),
                                 r=[BK[bi], "cst"], w=[dkey + S])
                            if bi == 3:
                                yield
                        yield
                        each(lambda g, gl, sl, slL, pbs: P.op("pe", lambda e: e.matmul(Bk[1][slL, gl * 64:(gl + 1) * 64], lhsT=NakT[slL, g * 64:g * 64 + L], rhs=Vt[slL, g * 64:(g + 1) * 64],
                                                                                      start=True, stop=True), r=["NakT" + S, "Vt" + S], w=[BK[1]]))
                        wcols = slice(g0 * 128, g1 * 128)
                        P.op("dve", lambda e: e.tensor_copy(out=v3(Wb[:, wcols], 128)[:, :, 64:128], in_=v3(Bk[1][:, 0:Gs * 64])), r=[BK[1]], w=["Wb" + S])
                        each(lambda g, gl, sl, slL, pbs: P.op("pe", lambda e: e.matmul(Bk[0][slL, gl * 64:(gl + 1) * 64], lhsT=at_b[sl, g * L:(g + 1) * L], rhs=identb[sl, pbs:pbs + 64],
                                                                                      start=True, stop=True), r=["at_b", "identb"], w=[BK[0]]))
                        P.op("dve", lambda e: e.tensor_copy(out=v3(Wb[:, wcols], 128)[:, :, 0:64], in_=v3(Bk[0][:, 0:Gs * 64])), r=[BK[0]], w=["Wb" + S])
                        yield
                        Pc, PcT, pk, ptk = Nab, NabT, "Nab" + S, "NabT" + S
                        for lev in range(nlev):
                            each(lambda g, gl, sl, slL, pbs: P.op("pe", lambda e: e.matmul(Bk[2][slL, gl * 128:(gl + 1) * 128], lhsT=PcT[slL, g * 64:g * 64 + L], rhs=Wb[slL, g * 128:(g + 1) * 128],
                                                                                          start=True, stop=True), r=[ptk, "Wb" + S], w=[BK[2]]))
                            if lev < nlev - 1:
                                each(lambda g, gl, sl, slL, pbs: P.op("pe", lambda e: e.matmul(Bk[0][slL, gl * 64:gl * 64 + L], lhsT=PcT[slL, g * 64:g * 64 + L], rhs=Pc[slL, g * 64:g * 64 + L],
                                                                                              start=True, stop=True), r=[pk, ptk], w=[BK[0]]))
                                each(lambda g, gl, sl, slL, pbs: P.op("pe", lambda e: e.matmul(Bk[1][slL, gl * 64:gl * 64 + L], lhsT=Pc[slL, g * 64:g * 64 + L], rhs=PcT[slL, g * 64:g * 64 + L],
                                                                                              start=True, stop=True), r=[pk, ptk], w=[BK[1]]))
                            P.op("dve", lambda e: e.tensor_tensor(out=Wb[:, wcols], in0=Bk[2][:, 0:Gs * 128], in1=Wb[:, wcols], op=ALU.add), r=[BK[2], "Wb" + S], w=["Wb" + S])
                            if lev < nlev - 1:
                                nP, nPT = Pq[lev % 2], PqT[lev % 2]
                                npk, nptk = f"Pq{lev%2}" + S, f"PqT{lev%2}" + S
                                P.op("dve", lambda e: e.tensor_copy(out=nP[:, cols], in_=Bk[0][:, 0:Gs * 64]), r=[BK[0]], w=[npk])
                                P.op("dve", lambda e: e.tensor_copy(out=nPT[:, cols], in_=Bk[1][:, 0:Gs * 64]), r=[BK[1]], w=[nptk])
                                Pc, PcT, pk, ptk = nP, nPT, npk, nptk
                            yield
                        yield "final"
                        each(lambda g, gl, sl, slL, pbs: P.op("pe", lambda e: e.matmul(Bk[0][sl, gl * 64:(gl + 1) * 64], lhsT=Wb[slL, g * 128:g * 128 + 64], rhs=BppT[slL, g * 64:(g + 1) * 64],
                                                                                      start=True, stop=True), r=["Wb" + S, "BppT" + S], w=[BK[0]]))
                        P.op("dve", lambda e: e.tensor_tensor(out=McT[:, cols], in0=Bk[0][:, 0:Gs * 64], in1=diagG[:, cols], op=ALU.add), r=[BK[0], "diagG"], w=["McT" + S])

                        def dc_mm(g, gl, sl, slL, pbs):
                            P.op("pe", lambda e: e.matmul(Bk[1][sl, gl * 64:(gl + 1) * 64], lhsT=BppT[slL, g * 64:(g + 1) * 64], rhs=Wb[slL, g * 128 + 64:(g + 1) * 128], start=True, stop=False),
                                 r=["Wb" + S, "BppT" + S], w=[BK[1]])
                            P.op("pe", lambda e: e.matmul(Bk[1][sl, gl * 64:(gl + 1) * 64], lhsT=KppT[slL, g * 64:(g + 1) * 64], rhs=Vt[slL, g * 64:(g + 1) * 64], start=False, stop=True),
                                 r=["KppT" + S, "Vt" + S], w=[BK[1]])
                        each(dc_mm)
                        P.op("dve", lambda e: e.tensor_copy(out=Dc[:, cols], in_=Bk[1][:, 0:Gs * 64]), r=[BK[1]], w=["Dc" + S])
                        if want_y and not (c0 == 512 and s_ == 0):
                            each(lambda g, gl, sl, slL, pbs: P.op("pe", lambda e: e.matmul(Bk[2][sl, gl * 64:gl * 64 + L], lhsT=Wb[slL, g * 128:g * 128 + 64], rhs=ArbT[slL, g * 64:g * 64 + L],
                                                                                          start=True, stop=True), r=["Wb" + S, "ArbT" + S], w=[BK[2]]))
                            P.op("dve", lambda e: e.tensor_tensor(out=v3(QeffT[:, cols])[:, :, 0:L], in0=v3(Bk[2][:, 0:Gs * 64])[:, :, 0:L],
                                                                  in1=rp_b[:, g0 * L:g1 * L].rearrange("p (g l) -> p g l", l=L), op=ALU.add), r=[BK[2], "rp_b"], w=["QeffT" + S])

                            def yl_mm(g, gl, sl, slL, pbs):
                                P.op("pe", lambda e: e.matmul(Bk[3][sl, gl * 64:gl * 64 + L], lhsT=Wb[slL, g * 128 + 64:(g + 1) * 128], rhs=ArbT[slL, g * 64:g * 64 + L], start=True, stop=False),
                                     r=["Wb" + S, "ArbT" + S], w=[BK[3]])
                                P.op("pe", lambda e: e.matmul(Bk[3][sl, gl * 64:gl * 64 + L], lhsT=Vt[slL, g * 64:(g + 1) * 64], rhs=ArkT[slL, g * 64:g * 64 + L], start=False, stop=True),
                                     r=["Vt" + S, "ArkT" + S], w=[BK[3]])
                            each(yl_mm)
                            P.op("dve", lambda e: e.tensor_copy(out=YlocT[:, cols], in_=Bk[3][:, 0:Gs * 64]), r=[BK[3]], w=["YlocT" + S])
                        yield

                    return [alg(0, 0, 1)] if G == 1 else [alg(0, 0, G // 2), alg(1, G // 2, G)]


                def tail_gen():
                    cur = 0 if ti == 0 else curh[0]
                    if samp:
                        for hh in range(2):
                            sl = slice(hh * 64, hh * 64 + 64)
                            P.op("pe", lambda e, sl=sl, hh=hh: e.matmul(pb[7][sl, 0:64], lhsT=s0in[0:64, hh * 64:(hh + 1) * 64], rhs=identf[0:64, 0:64], start=True, stop=True),
                                 r=[f"s0in{p%2}", "cst"], w=["pb7"])
                        P.op("dve", lambda e: e.tensor_copy(out=ST[cur][:, :], in_=pb[7][:, 0:64]), r=["pb7"], w=[f"ST{cur}"])
                    elif ti == 0:
                        P.op("dve", lambda e: e.memset(ST[cur][:], 0.0), w=[f"ST{cur}"])
                    for g in range(G):
                        yield
                        chunk = (c0 // 64 + g) if not samp else 99
                        y_here = samp or chunk >= 15
                        sk_ = skey(g)
                        for hh in range(2):
                            sl = slice(hh * 64, hh * 64 + 64)
                            if y_here:
                                P.op("pe", lambda e, sl=sl, g=g, cur=cur: e.matmul(pb[6][sl, 0:L], lhsT=ST[cur][sl, :], rhs=QeffT[sl, g * 64:g * 64 + L], start=True, stop=True),
                                     r=[f"ST{cur}", "QeffT" + sk_], w=["pb6"])
                            P.op("pe", lambda e, sl=sl, g=g, cur=cur: e.matmul(pb[7][sl, 0:64], lhsT=McT[sl, g * 64:(g + 1) * 64], rhs=ST[cur][sl, :], start=True, stop=True),
                                 r=[f"ST{cur}", "McT" + sk_], w=["pb7"])
                        P.op("dve", lambda e, g=g, cur=cur: e.tensor_tensor(out=ST[1 - cur][:, :], in0=pb[7][:, 0:64], in1=Dc[:, g * 64:(g + 1) * 64], op=ALU.add),
                             r=["pb7", "Dc" + sk_], w=[f"ST{1-cur}"])
                        if y_here:
                            P.op("dve", lambda e, g=g: e.tensor_tensor(out=yT[:, g * L:(g + 1) * L], in0=pb[6][:, 0:L], in1=YlocT[:, g * 64:g * 64 + L], op=ALU.add),
                                 r=["pb6", "YlocT" + sk_], w=["yT"])
                        cur = 1 - cur
                        if chunk == 15:
                            P.op("dve", lambda e, cur=cur: e.tensor_scalar(out=ST[cur][:], in0=ST[cur][:], scalar1=flag, scalar2=None, op0=ALU.mult), r=[f"ST{cur}", "prm"], w=[f"ST{cur}"])
                    if c0 == 1536 or samp:
                        od = S_out if not samp else Ss_out
                        for hh in range(2):
                            h = 2 * p + hh
                            sl = slice(hh * 64, hh * 64 + 64)
                            P.op("pe", lambda e, sl=sl, hh=hh, cur=cur: e.matmul(pb[6 + hh][0:64, 0:64], lhsT=ST[cur][sl, :], rhs=identf[sl, hh * 64:hh * 64 + 64], start=True, stop=True),
                                 r=[f"ST{cur}", "cst"], w=[f"pb{6+hh}"])
                            P.op("dve", lambda e, h=h, hh=hh: e.tensor_copy(out=sstg[0:64, hh * 64:(hh + 1) * 64], in_=pb[6 + hh][0:64, 0:64]), r=[f"pb{6+hh}"], w=["sstg"])
                        P.op("sp", lambda e, od=od, p=p: e.dma_start(out=od[:, p * 128:(p + 1) * 128], in_=sstg[0:64, 0:128]), r=["sstg"], dma=True, out=True)
                    if want_y:
                        P.op("pe", lambda e, n=n: e.matmul(pb[7][:, 0:n], lhsT=bdones, rhs=yT[:, 0:n], start=True, stop=True), r=["yT", "cst"], w=["pb7"])
                        P.op("act", lambda e, n=n: e.activation(out=ptmp1[:, 0:n], in_=yT[:, 0:n], func=AF.Square), r=["yT"], w=["ptmp1"])
                        P.op("pe", lambda e, n=n: e.matmul(pb[6][:, 0:n], lhsT=bdones, rhs=ptmp1[:, 0:n], start=True, stop=True), r=["ptmp1", "cst"], w=["pb6"])
                        P.op("act", lambda e, n=n: e.activation(out=ptmp2[:, 0:n], in_=pb[7][:, 0:n], func=AF.Copy, scale=1.0 / 64), r=["pb7"], w=["ptmp2"])
                        P.op("pool", lambda e, n=n: e.tensor_tensor(out=ptmp3[:, 0:n], in0=ptmp2[:, 0:n], in1=ptmp2[:, 0:n], op=ALU.mult), r=["ptmp2"], w=["ptmp3"])
                        P.op("dve", lambda e, n=n: e.scalar_tensor_tensor(out=ptmp3[:, 0:n], in0=pb[6][:, 0:n], scalar=1.0 / 64, in1=ptmp3[:, 0:n], op0=ALU.mult, op1=ALU.subtract),
                             r=["pb6", "ptmp3"], w=["ptmp3"])
                        P.op("act", lambda e, n=n: e.activation(out=ptmp3[:, 0:n], in_=ptmp3[:, 0:n], func=AF.Ln, bias=GN_EPS), r=["ptmp3"], w=["ptmp3"])
                        P.op("act", lambda e, n=n: e.activation(out=ptmp3[:, 0:n], in_=ptmp3[:, 0:n], func=AF.Exp, scale=-0.5), r=["ptmp3"], w=["ptmp3"])
                        P.op("pool", lambda e, n=n: e.tensor_tensor(out=yT[:, 0:n], in0=yT[:, 0:n], in1=ptmp2[:, 0:n], op=ALU.subtract), r=["yT", "ptmp2"], w=["yT"])
                        P.op("pool", lambda e, n=n: e.tensor_tensor(out=yT[:, 0:n], in0=yT[:, 0:n], in1=ptmp3[:, 0:n], op=ALU.mult), r=["yT", "ptmp3"], w=["yT"])
                        P.op("act", lambda e, n=n, p=p: e.activation(out=yT[:, 0:n], in_=yT[:, 0:n], func=AF.Identity, scale=pcol("lnw", p), bias=pcol("lnb", p)), r=["yT", "prm"], w=["yT"])
                        P.op("dve", lambda e, n=n, p=p: e.scalar_tensor_tensor(out=ptmp1[:, 0:n], in0=r_f[:, 0:n], scalar=pcol("rk", p), in1=k_f[:, 0:n], op0=ALU.mult, op1=ALU.mult),
                             r=[f"r_f{sx}", f"k_f{sx}", "prm"], w=["ptmp1"])
                        P.op("pe", lambda e, n=n: e.matmul(pb[7][:, 0:n], lhsT=bdones, rhs=ptmp1[:, 0:n], start=True, stop=True), r=["ptmp1", "cst"], w=["pb7"])
                        P.op("dve", lambda e, n=n: e.tensor_tensor(out=ptmp1[:, 0:n], in0=pb[7][:, 0:n], in1=v_f[:, 0:n], op=ALU.mult), r=["pb7", f"v_f{sx}"], w=["ptmp1"])
                        P.op("pool", lambda e, n=n: e.tensor_tensor(out=yT[:, 0:n], in0=yT[:, 0:n], in1=ptmp1[:, 0:n], op=ALU.add), r=["yT", "ptmp1"], w=["yT"])
                        if samp:
                            lo, mc = 0, (8 + p) * NMX + 1088
                        elif c0 == 512:
                            lo, mc = 448, (8 + p) * NMX + 0
                        else:
                            lo, mc = 0, (8 + p) * NMX + (c0 - HALO0)
                        nn = n - lo
                        P.op("pool", lambda e, lo=lo, mc=mc, nn=nn: e.tensor_tensor(out=mixT[:, mc:mc + nn], in0=yT[:, lo:lo + nn], in1=g_f[:, lo:lo + nn], op=ALU.mult),
                             r=["yT", f"g_f{sx}"], w=[f"mixT{8+p}"])

                    curh[0] = cur
                    yield
                return prep_gen, prep2_gen, alg_run, tail_gen

        for p in range(8):
            pi = 0
            for ti, (c0, n) in enumerate(tiles):
                tile_list.append((p, pi, ti, c0, n))
        def drain(g_):
            for _ in g_:
                pass
        made = [make_tile(p_, pi_, ti_, c0_, n_, k_ % 2) for k_, (p_, pi_, ti_, c0_, n_) in enumerate(tile_list)]
        drain(made[0][0]())
        drain(made[0][1]())
        for t_ in range(len(made)):
            gens = list(made[t_][2]())
            tail_g = made[t_ - 1][3]() if t_ > 0 else None
            prep_g = made[t_ + 1][0]() if t_ + 1 < len(made) else None
            if tail_g is not None:
                gens.append(tail_g)
            prep_added = False
            while gens or not prep_added:
                if not prep_added and (tail_g is None or tail_g not in gens):
                    if prep_g is not None:
                        gens.append(prep_g)
                    prep_added = True
                for g_ in list(gens):
                    if g_ not in gens:
                        continue
                    try:
                        if next(g_) == "final" and tail_g is not None and tail_g in gens:
                            drain(tail_g)
                            gens.remove(tail_g)
                    except StopIteration:
                        gens.remove(g_)
            if t_ + 1 < len(made):
                drain(made[t_ + 1][1]())
        drain(made[-1][3]())
        P.op("sp", lambda e: e.dma_start(out=sh_out[:, :], in_=shcol[:]), r=["shcol"], dma=True, out=True)


def ffn_phase(nc, P, es, pb, xmidT, mixT, NMX, prm, pcol, modT, mod, F1, identf, cst, xw, w_out_t, w_up_t, w_down_t, y_own, y_s, cvcol, cv_out):
    flag = pcol("flag")
    with ExitStack() as ph:
        def sb(name, shape, dtype=F32):
            return ph.enter_context(nc.sbuf_tensor(name, list(shape), dtype))
        NX = 1040
        xhalo = sb("xhalo", [128, 16 * 64])

        def xm(kc, c, n):
            if c < 64:
                assert c + n <= 64
                return xhalo[:, kc * 64 + c:kc * 64 + c + n]
            if c < 1088:
                assert c + n <= 1088
                return xmidT[:, kc * NX + c - 64:kc * NX + c - 64 + n]
            return xmidT[:, kc * NX + 1024 + c - 1088:kc * NX + 1024 + c - 1088 + n]
        MIXR = [f"mixT{i}" for i in range(16)]
        with ExitStack() as phE:
            xblk = [phE.enter_context(nc.sbuf_tensor(f"xb{i}", [128, D], F32)) for i in range(2)]
            wo = [phE.enter_context(nc.sbuf_tensor(f"wo{i}", [128, 2048], BF16)) for i in range(2)]
            blocks = [(HALO0, 64, 0)] + [(OWN0 + i * 128, 128, 64 + i * 128) for i in range(8)] + [(TW, TS, 1088)]
            for bi_, (t0, n, col) in enumerate(blocks):
                xb = xblk[bi_ % 2]
                xk = f"xb{bi_%2}"
                P.op("sp", lambda e, xb=xb, t0=t0, n=n: e.dma_start(out=xb[:n, :], in_=xw[t0:t0 + n, :]), w=[xk], dma=True)
                for q4 in range(4):
                    bk = 4 + (bi_ * 4 + q4) % 4
                    for i in range(4):
                        kc = q4 * 4 + i
                        P.op("pe", lambda e, xb=xb, n=n, kc=kc, i=i, bk=bk: e.matmul(pb[bk][:, i * 128:i * 128 + n], lhsT=xb[:n, kc * 128:(kc + 1) * 128], rhs=identf[:n, :n],
                                                                               start=True, stop=True), r=[xk, "cst"], w=[f"pb{bk}"])
                    for i in range(4):
                        kc = q4 * 4 + i
                        dst = xm(kc, col, n)
                        if i % 2 == 0:
                            P.op("act", lambda e, dst=dst, bk=bk, i=i, n=n: e.activation(out=dst, in_=pb[bk][:, i * 128:i * 128 + n], func=AF.Copy), r=[f"pb{bk}"], w=[f"xm{kc}"])
                        else:
                            P.op("dve", lambda e, dst=dst, bk=bk, i=i, n=n: e.tensor_copy(out=dst, in_=pb[bk][:, i * 128:i * 128 + n]), r=[f"pb{bk}"], w=[f"xm{kc}"])
            otiles = [(0, 64), (64, 512), (576, 512), (1088, TS)]
            for j in range(16):
                wt = wo[j % 2]
                P.op("pool", lambda e, wt=wt, j=j: e.dma_start(out=wt[:], in_=w_out_t[j]), w=[f"wo{j%2}"], dma=True)
                for ti, (c0, n) in enumerate(otiles):
                    bk = ti % 2
                    v = 1 if c0 == 1088 else 0
                    for kc in range(16):
                        P.op("pe", lambda e, wt=wt, kc=kc, c0=c0, n=n, bk=bk: e.matmul(pb[bk][:, 0:n], lhsT=wt[:, kc * 128:(kc + 1) * 128], rhs=mixT[:, kc * NMX + c0:kc * NMX + c0 + n],
                                                                                 start=(kc == 0), stop=(kc == 15)), r=[f"wo{j%2}"] + MIXR, w=[f"pb{bk}"])
                    dst = xm(j, c0, n)
                    P.op("dve", lambda e, dst=dst, bk=bk, n=n, j=j, v=v: e.scalar_tensor_tensor(out=dst, in0=pb[bk][:, 0:n], scalar=mod(2, j, v), in1=dst, op0=ALU.mult, op1=ALU.add),
                         r=[f"pb{bk}", "modT2", f"xm{j}"], w=[f"xm{j}"])
            P.barrier()
        P.checkpoint("E", [("xmidT", xmidT[:], 16 * 1040, F32), ("xhalo", xhalo[:], 16 * 64, F32)])

        onesf = sb("onesf", [128, 128])
        P.op("dve", lambda e: e.memset(onesf[:], 1.0), w=["onesf"])
        NH = 592
        h2T = mixT
        uT = sb("uT", [128, NJ * 528], BF16)
        sqb = sb("sqb", [128, 512])
        rstd = sb("rstd", [128, NH])
        tmpn = sb("tmpn", [128, 512])
        wg = [sb(f"wg{i}", [128, 2048], BF16) for i in range(2)]
        wvv = [sb(f"wu{i}", [128, 2048], BF16) for i in range(2)]
        wd = [sb(f"wd{i}", [128, 22 * 128], BF16) for i in range(2)]
        gbuf = sb("gbuf", [128, 2 + 576])
        gsb = sb("gsb", [128, 2 + TS])
        carry = sb("carry", [128, NJ * 2])
        cacc = sb("cacc", [128, 576])
        gel = sb("gel", [128, 576])
        yTt = sb("yTt", [128, 528])
        ytok = sb("ytok", [128, 512])
        XMR = [f"xm{i}" for i in range(16)]

        for half in range(2):
            if half == 0:
                segs = [(0, 64), (64, 512), (1088, TS)]
            else:
                segs = [(576, 512)]
            loc = []
            o = 0
            for (c0, n) in segs:
                loc.append((c0, n, o))
                o += n
            NHc = o
            for (c0, n, lo) in loc:
                v = 1 if c0 == 1088 else 0
                for kc in range(16):
                    P.op("act", lambda e, kc=kc, c0=c0, n=n: e.activation(out=sqb[:, 0:n], in_=xm(kc, c0, n), func=AF.Square), r=[f"xm{kc}"], w=["sqb"])
                    P.op("pe", lambda e, kc=kc, n=n: e.matmul(pb[2][:, 0:n], lhsT=onesf[:], rhs=sqb[:, 0:n], start=(kc == 0), stop=(kc == 15)), r=["sqb", "onesf"], w=["pb2"])
                P.op("act", lambda e, n=n, lo=lo: e.activation(out=rstd[:, lo:lo + n], in_=pb[2][:, 0:n], func=AF.Sqrt, scale=1.0 / D, bias=RMS_EPS), r=["pb2"], w=["rstd"])
                P.op("dve", lambda e, n=n, lo=lo: e.reciprocal(out=rstd[:, lo:lo + n], in_=rstd[:, lo:lo + n]), r=["rstd"], w=["rstd"])
                for kc in range(16):
                    P.op("dve", lambda e, kc=kc, c0=c0, n=n, lo=lo: e.tensor_tensor(out=tmpn[:, 0:n], in0=xm(kc, c0, n), in1=rstd[:, lo:lo + n], op=ALU.mult),
                         r=[f"xm{kc}", "rstd"], w=["tmpn"])
                    P.op("act", lambda e, kc=kc, n=n, lo=lo, v=v: e.activation(out=h2T[:, kc * NH + lo:kc * NH + lo + n], in_=tmpn[:, 0:n], func=AF.Identity,
                                                                            scale=F1[:, kc * 2 + v:kc * 2 + v + 1], bias=mod(3, kc, v)), r=["tmpn", "F1", "modT3"], w=[f"h2T{kc}"])
            H2R = [f"h2T{kc}" for kc in range(16)]
            for j in range(NJ):
                wgt, wvt = wg[j % 2], wvv[j % 2]
                gb, vb = 2 * (j % 2), 1 + 2 * (j % 2)
                P.op("pool", lambda e, wgt=wgt, j=j: e.dma_start(out=wgt[:], in_=w_up_t[j]), w=[f"wg{j%2}"], dma=True)
                P.op("pool", lambda e, wvt=wvt, j=j: e.dma_start(out=wvt[:], in_=w_up_t[NJ + j]), w=[f"wu{j%2}"], dma=True)
                if half == 0:
                    gt = [(0, 512, 0), (512, 64, 512)]
                else:
                    gt = [(0, 512, 0)]
                for (lc, n, gcol) in gt:
                    bk = 0
                    for kc in range(16):
                        P.op("pe", lambda e, wgt=wgt, kc=kc, lc=lc, n=n: e.matmul(pb[gb][:, 0:n], lhsT=wgt[:, kc * 128:(kc + 1) * 128], rhs=h2T[:, kc * NH + lc:kc * NH + lc + n],
                                                                           start=(kc == 0), stop=(kc == 15)), r=[f"wg{j%2}"] + H2R, w=[f"pb{gb}"])
                    P.op("act", lambda e, n=n, gcol=gcol: e.activation(out=gbuf[:, 2 + gcol:2 + gcol + n], in_=pb[gb][:, 0:n], func=AF.Copy), r=[f"pb{gb}"], w=["gbuf"])
                if half == 0:
                    P.op("dve", lambda e: e.tensor_scalar(out=gbuf[:, 2:66], in0=gbuf[:, 2:66], scalar1=flag, scalar2=None, op0=ALU.mult), r=["gbuf", "prm"], w=["gbuf"])
                    g0 = 66
                else:
                    P.op("dve", lambda e, j=j: e.tensor_copy(out=gbuf[:, 0:2], in_=carry[:, 2 * j:2 * j + 2]), r=["carry"], w=["gbuf"])
                    g0 = 2
                if half == 0:
                    P.op("dve", lambda e, j=j, g0=g0: e.tensor_copy(out=carry[:, 2 * j:2 * j + 2], in_=gbuf[:, g0 + 510:g0 + 512]), r=["gbuf"], w=["carry"])
                else:
                    P.op("dve", lambda e, j=j, g0=g0: e.tensor_copy(out=cvcol[:, 2 * j:2 * j + 2], in_=gbuf[:, g0 + 510:g0 + 512]), r=["gbuf"], w=["cvcol"])

                def conv(src, s0, n, j=j):
                    P.op("dve", lambda e: e.tensor_scalar(out=cacc[:, 0:n], in0=src[:, s0 - 2:s0 - 2 + n], scalar1=pcol("dw0", j), scalar2=pcol("dwb", j), op0=ALU.mult, op1=ALU.add),
                         r=["gbuf", "gsb", "prm"], w=["cacc"])
                    P.op("dve", lambda e: e.scalar_tensor_tensor(out=cacc[:, 0:n], in0=src[:, s0 - 1:s0 - 1 + n], scalar=pcol("dw1", j), in1=cacc[:, 0:n], op0=ALU.mult, op1=ALU.add),
                         r=["gbuf", "gsb", "cacc", "prm"], w=["cacc"])
                    P.op("dve", lambda e: e.scalar_tensor_tensor(out=cacc[:, 0:n], in0=src[:, s0:s0 + n], scalar=pcol("dw2", j), in1=cacc[:, 0:n], op0=ALU.mult, op1=ALU.add),
                         r=["gbuf", "gsb", "cacc", "prm"], w=["cacc"])
                    P.op("act", lambda e: e.activation(out=gel[:, 0:n], in_=cacc[:, 0:n], func=AF.Gelu), r=["cacc"], w=["gel"])
                conv(gbuf, g0, 512)
                lcv = 64 if half == 0 else 0
                for kc in range(16):
                    P.op("pe", lambda e, wvt=wvt, kc=kc, lcv=lcv: e.matmul(pb[vb][:, 0:512], lhsT=wvt[:, kc * 128:(kc + 1) * 128], rhs=h2T[:, kc * NH + lcv:kc * NH + lcv + 512],
                                                                       start=(kc == 0), stop=(kc == 15)), r=[f"wu{j%2}"] + H2R, w=[f"pb{vb}"])
                P.op("dve", lambda e, j=j: e.tensor_tensor(out=uT[:, j * 528:j * 528 + 512], in0=pb[vb][:, 0:512], in1=gel[:, 0:512], op=ALU.mult), r=[f"pb{vb}", "gel"], w=[f"uT{j}"])
                if half == 0:
                    for kc in range(16):
                        P.op("pe", lambda e, wgt=wgt, kc=kc: e.matmul(pb[gb][:, 0:TS], lhsT=wgt[:, kc * 128:(kc + 1) * 128], rhs=h2T[:, kc * NH + 576:kc * NH + 592],
                                                                   start=(kc == 0), stop=(kc == 15)), r=[f"wg{j%2}"] + H2R, w=[f"pb{gb}"])
                    P.op("dve", lambda e, j=j: e.tensor_copy(out=gsb[:, 0:2], in_=prm[:, PC["cvs"] + 2 * j:PC["cvs"] + 2 * j + 2]), r=["prm"], w=["gsb"])
                    P.op("act", lambda e: e.activation(out=gsb[:, 2:2 + TS], in_=pb[gb][:, 0:TS], func=AF.Copy), r=[f"pb{gb}"], w=["gsb"])
                    P.op("dve", lambda e, j=j: e.tensor_copy(out=cvcol[:, 88 + 2 * j:88 + 2 * j + 2], in_=gsb[:, TS:TS + 2]), r=["gsb"], w=["cvcol"])
                    conv(gsb, 2, TS)
                    for kc in range(16):
                        P.op("pe", lambda e, wvt=wvt, kc=kc: e.matmul(pb[vb][:, 0:TS], lhsT=wvt[:, kc * 128:(kc + 1) * 128], rhs=h2T[:, kc * NH + 576:kc * NH + 592],
                                                                   start=(kc == 0), stop=(kc == 15)), r=[f"wu{j%2}"] + H2R, w=[f"pb{vb}"])
                    P.op("dve", lambda e, j=j: e.tensor_tensor(out=uT[:, j * 528 + 512:j * 528 + 528], in0=pb[vb][:, 0:TS], in1=gel[:, 0:TS], op=ALU.mult), r=[f"pb{vb}", "gel"], w=[f"uT{j}"])
            UR = [f"uT{j}" for j in range(NJ)]
            utiles = [(0, 512)] + ([(512, TS)] if half == 0 else [])
            for jo in range(16):
                for hf in range(2):
                    P.op("pool", lambda e, hf=hf, jo=jo: e.dma_start(out=wd[hf][:], in_=w_down_t[jo][:, hf * 2816:(hf + 1) * 2816]), w=[f"wd{hf}"], dma=True)
                for (uc, n) in utiles:
                    bk = 4 if n == 512 else 5
                    v = 0 if n == 512 else 1
                    for kc in range(NJ):
                        P.op("pe", lambda e, kc=kc, uc=uc, n=n, bk=bk: e.matmul(pb[bk][:, 0:n], lhsT=wd[kc // 22][:, (kc % 22) * 128:(kc % 22 + 1) * 128], rhs=uT[:, kc * 528 + uc:kc * 528 + uc + n],
                                                                                   start=(kc == 0), stop=(kc == NJ - 1)), r=[f"wd{kc//22}"] + UR, w=[f"pb{bk}"])
                    if n == 512:
                        xc = 64 if half == 0 else 576
                    else:
                        xc = 1088
                    P.op("dve", lambda e, jo=jo, n=n, bk=bk, xc=xc, uc=uc, v=v: e.scalar_tensor_tensor(
                        out=yTt[:, uc:uc + n], in0=pb[bk][:, 0:n], scalar=mod(5, jo, v), in1=xm(jo, xc, n), op0=ALU.mult, op1=ALU.add),
                        r=[f"pb{bk}", "modT5", f"xm{jo}"], w=["yTt"])
                    if n == 512:
                        for blk in range(4):
                            P.op("pe", lambda e, blk=blk: e.matmul(pb[6 + blk % 2][:, (blk // 2) * 128:(blk // 2) * 128 + 128], lhsT=yTt[:, blk * 128:(blk + 1) * 128], rhs=identf,
                                                                 start=True, stop=True), r=["yTt", "cst"], w=[f"pb{6+blk%2}"])
                        for blk in range(4):
                            P.op("act" if blk % 2 == 0 else "dve",
                                 (lambda e, blk=blk: e.activation(out=ytok[:, blk * 128:(blk + 1) * 128], in_=pb[6 + blk % 2][:, (blk // 2) * 128:(blk // 2) * 128 + 128], func=AF.Copy))
                                 if blk % 2 == 0 else
                                 (lambda e, blk=blk: e.tensor_copy(out=ytok[:, blk * 128:(blk + 1) * 128], in_=pb[6 + blk % 2][:, (blk // 2) * 128:(blk // 2) * 128 + 128])),
                                 r=[f"pb{6+blk%2}"], w=[f"ytok{blk}"])
                            r0 = half * 512 + blk * 128
                            P.op("sp", lambda e, blk=blk, r0=r0, jo=jo: e.dma_start(out=y_own[r0:r0 + 128, jo * 128:(jo + 1) * 128], in_=ytok[:, blk * 128:(blk + 1) * 128]),
                                 r=[f"ytok{blk}"], dma=True, out=True)
                    else:
                        P.op("pe", lambda e, uc=uc: e.matmul(pb[6][0:TS, 0:128], lhsT=yTt[:, uc:uc + TS], rhs=identf, start=True, stop=True), r=["yTt", "cst"], w=["pb6"])
                        P.op("dve", lambda e: e.tensor_copy(out=tmpn[0:TS, 0:128], in_=pb[6][0:TS, 0:128]), r=["pb6"], w=["tmpn"])
                        P.op("sp", lambda e, jo=jo: e.dma_start(out=y_s[:, jo * 128:(jo + 1) * 128], in_=tmpn[0:TS, 0:128]), r=["tmpn"], dma=True, out=True)
            P.barrier()
        P.op("sp", lambda e: e.dma_start(out=cv_out[:, :], in_=cvcol[:]), r=["cvcol"], dma=True, out=True)


_NC_CACHE = {}


def _tile_w(w, kc_n):
    K, N = w.shape
    return np.ascontiguousarray(w.reshape(K // 128, 128, N // 128, 128).transpose(2, 1, 0, 3)).reshape(N // 128, 128, (K // 128) * 128)


def _col(vec, n):
    return np.ascontiguousarray(vec.reshape(n, 128).T)


def _consts():
    c = np.zeros((128, 128 * 2 + 3 * 512), np.float32)
    c[:, 0:128] = np.eye(128, dtype=np.float32)
    bd = np.zeros((128, 128), np.float32)
    bd[0:64, 0:64] = 1
    bd[64:128, 64:128] = 1
    c[:, 128:256] = bd
    s = (np.arange(128) % 64)[:, None]
    t = np.arange(64)[None, :]
    su = (s < t).astype(np.float32)
    ui = (s <= t).astype(np.float32)
    sl = (s > t).astype(np.float32)
    c[:, 256:768] = np.tile(su, (1, 8))
    c[:, 768:1280] = np.tile(ui, (1, 8))
    c[:, 1280:1792] = np.tile(sl, (1, 8))
    return c


def prep_inputs(inp, cores):
    f = lambda k: np.asarray(inp[k], np.float32)
    shared = {}
    shared["w_ada_t"] = _tile_w(f("w_ada")[0], 16)
    shared["w_in_t"] = _tile_w(f("w_in")[0], 16)
    shared["w_out_t"] = _tile_w(f("w_out")[0], 16)
    shared["w_up_t"] = _tile_w(f("w_up")[0], 16)
    shared["w_down_t"] = _tile_w(f("w_down")[0], NJ)
    shared["badat"] = _col(f("b_ada")[0], 96)
    shared["w2a2"] = np.ascontiguousarray(np.concatenate([f("w2")[0], f("a2")[0]], axis=0))
    shared["g2"] = np.ascontiguousarray(f("g2")[0])
    shared["consts"] = _consts()
    tab = f("rel_bias")[0]
    k = np.arange(64)[:, None, None]
    jj = np.arange(9)[None, :, None]
    q = np.arange(64)[None, None, :]
    idx = np.clip(512 - 64 * jj + q - k, -128, 128) + 128
    bt = tab[:, idx].reshape(16, 64, 576)
    shared["biasT"] = np.ascontiguousarray(bt.reshape(8, 128, 576))
    ks = np.arange(640)[:, None]
    qs = np.arange(16)[None, :]
    idxs = np.clip(512 + qs - ks, -128, 128) + 128
    bs = tab[:, idxs]
    shared["biasS"] = np.ascontiguousarray(bs.reshape(16, 5, 128, 16).transpose(0, 2, 1, 3).reshape(16, 128, 80))

    pr = np.zeros((128, NP), np.float32)

    def put(name, arr):
        pr[:, PC[name]:PC[name] + arr.shape[1]] = arr
    put("nag", _col(f("norm_att_g")[0], 16))
    put("nfg", _col(f("norm_ffn_g")[0], 16))
    mu = f("mu_shift")[0]
    put("mu", _col(mu, 26))
    put("omu", _col(1.0 - mu, 26) * 0 + (1.0 - _col(mu, 26)))
    put("w0", _col(f("w0")[0], 8))
    put("a0", _col(f("a0")[0], 8))
    put("kk", _col(f("k_k")[0], 8))
    ka = _col(f("k_a")[0], 8)
    put("ka", ka)
    put("omka", 1.0 - ka)
    put("rk", _col(f("r_k")[0].reshape(-1), 8))
    put("lnw", _col(f("ln_x_w")[0], 8))
    put("lnb", _col(f("ln_x_b")[0], 8))
    put("qg", np.tile(f("q_norm_g")[0], 2)[:, None])
    put("kg", np.tile(f("k_norm_g")[0], 2)[:, None])
    dw = f("dw_conv")[0]
    put("dw0", _col(dw[0], NJ))
    put("dw1", _col(dw[1], NJ))
    put("dw2", _col(dw[2], NJ))
    put("dwb", _col(f("dw_bias")[0], NJ))

    xp, xs = f("x_prompt"), f("x_sample")
    cp, csm = f("c_prompt"), f("c_sample")
    maps = []
    for core in cores:
        b, half = core // 2, core % 2
        m = dict(shared)
        xwin = np.zeros((NT, D), np.float32)
        if half == 1:
            xwin[0:TW] = xp[b]
        else:
            xwin[0:OWN0] = xp[b, 0:1024]
            xwin[OWN0:TW] = xp[b, 0:1024]
        xwin[TW:] = xs[core]
        m["xw"] = xwin
        cT = np.zeros((128, 16, 2), np.float32)
        cT[:, :, 0] = _col(cp[b], 16)
        cT[:, :, 1] = _col(csm[core], 16)
        m["cT"] = cT.reshape(128, 32)
        p2 = pr.copy()
        p2[:, PC["flag"]] = float(half)
        p2[:, PC["negm"]] = (float(half) - 1.0) * 30000.0
        p2[:, PC["shs"]:PC["shs"] + 26] = _col(f("state_shift")[0, core], 26)
        cv = f("state_ffn_conv")[0, core]
        cvc = np.stack([_col(cv[0], NJ), _col(cv[1], NJ)], axis=2).reshape(128, 2 * NJ)
        p2[:, PC["cvs"]:PC["cvs"] + 88] = cvc
        m["params"] = p2
        m["cache_k"] = np.ascontiguousarray(f("cache_att_k")[0, core].reshape(512, 1024))
        m["cache_v"] = np.ascontiguousarray(f("cache_att_v")[0, core].reshape(512, 1024))
        m["state0"] = np.ascontiguousarray(f("state_rwkv")[0, core].transpose(1, 0, 2).reshape(64, 1024))
        maps.append(m)
    return maps


def assemble(res, cores=range(8)):
    y_p = np.zeros((4, 2048, D), np.float32)
    y_s = np.zeros((8, TS, D), np.float32)
    nk = np.zeros((1, 4, 512, 16, 64), np.float32)
    nv = np.zeros_like(nk)
    nS = np.zeros((1, 4, 16, 64, 64), np.float32)
    nsh = np.zeros((1, 4, 3328), np.float32)
    ncv = np.zeros((1, 4, 2, DFF), np.float32)
    sk = np.zeros((1, 8, TS, 16, 64), np.float32)
    sv = np.zeros_like(sk)
    sS = np.zeros((1, 8, 16, 64, 64), np.float32)
    ssh = np.zeros((1, 8, 3328), np.float32)
    scv = np.zeros((1, 8, 2, DFF), np.float32)
    for core, r in zip(cores, res):
        b, half = core // 2, core % 2
        y_p[b, half * 1024:(half + 1) * 1024] = r["y_own"]
        y_s[core] = r["y_s"]
        sk[0, core] = r["ks_out"].reshape(TS, 16, 64)
        sv[0, core] = r["vs_out"].reshape(TS, 16, 64)
        sS[0, core] = r["Ss_out"].reshape(64, 16, 64).transpose(1, 0, 2)
        sh = r["sh_out"]
        ssh[0, core] = sh[:, 26:52].T.reshape(-1)
        cv = r["cv_out"]
        scv[0, core] = cv[:, 88:176].reshape(128, NJ, 2).transpose(2, 1, 0).reshape(2, DFF)
        if half == 1:
            nk[0, b] = r["k_out"].reshape(512, 16, 64)
            nv[0, b] = r["v_out"].reshape(512, 16, 64)
            nS[0, b] = r["S_out"].reshape(64, 16, 64).transpose(1, 0, 2)
            nsh[0, b] = sh[:, 0:26].T.reshape(-1)
            ncv[0, b] = cv[:, 0:88].reshape(128, NJ, 2).transpose(2, 1, 0).reshape(2, DFF)
    return (y_p, y_s, nk, nv, nS, nsh, ncv, sk, sv, sS, ssh, scv)


def kernel(**inputs):
    if "nc" not in _NC_CACHE:
        _NC_CACHE["nc"] = build_program()
    nc = _NC_CACHE["nc"]
    cores = list(range(8))
    maps = prep_inputs(inputs, cores)
    res = run_bass_kernel_spmd(nc, maps, core_ids=cores)
    return assemble(res.results, cores)
```
